# Optimizing a Trainium2 kernel written in Bass

```python
import math
import jax, jax.numpy as jnp
from jax import lax
import numpy as np

D_MODEL = 1024
BATCH = 2
SEQ = 16384
DEPTH = 1
DEC_BATCH = 8
DEC_SEQ = 4096
PAST_LEN = 128

GRID_W = 64
NA_HEADS = 8
NA_HEAD_DIM = 64
NA_WIDTH = NA_HEADS * NA_HEAD_DIM
WIN_ROWS = 8
WIN_COLS = 16
SGU_GROUPS = 8
SGU_GROUP_DIM = 64
SGU_WIDTH = SGU_GROUPS * SGU_GROUP_DIM
CHUNK = 128
D_FF = 4 * D_MODEL
IN_WIDTH = 3 * NA_WIDTH + 2 * SGU_WIDTH + 2 * D_MODEL
N_ADA = 6
ALPHA = (2.0 * DEPTH) ** 0.25
BETA = (8.0 * DEPTH) ** -0.25
LN_EPS = 1e-5

kernel_name = "hybrid_natten_gmlp_deepnorm_adaln_encoder"


def layer_norm(x, g, b):
    xf = x.astype(jnp.float32)
    mu = jnp.mean(xf, axis=-1, keepdims=True)
    var = jnp.mean(jnp.square(xf - mu), axis=-1, keepdims=True)
    y = (xf - mu) * lax.rsqrt(var + LN_EPS) * g.astype(jnp.float32) + b.astype(jnp.float32)
    return y.astype(x.dtype)


def neighborhood_attention(q, k, v, rpb):
    B, T, H, dh = q.shape
    rows = T // GRID_W
    wr = min(WIN_ROWS, rows)
    qg = q.reshape(B, rows, GRID_W, H, dh) * (dh ** -0.5)
    kg = k.reshape(B, rows, GRID_W, H, dh)
    vg = v.reshape(B, rows, GRID_W, H, dh)
    col = jnp.arange(GRID_W)
    col_start = jnp.clip(col - WIN_COLS // 2, 0, GRID_W - WIN_COLS)
    col_idx = col_start[:, None] + jnp.arange(WIN_COLS)[None, :]
    dcol = col_idx - col[:, None] + (WIN_COLS - 1)
    rpb_c = jnp.transpose(rpb[:, :, dcol], (0, 2, 1, 3)).astype(jnp.float32)

    def row_fn(r):
        rs = jnp.clip(r - WIN_ROWS // 2, 0, rows - wr)
        k_rows = lax.dynamic_slice_in_dim(kg, rs, wr, axis=1)
        v_rows = lax.dynamic_slice_in_dim(vg, rs, wr, axis=1)
        k_win = k_rows[:, :, col_idx]
        v_win = v_rows[:, :, col_idx]
        q_r = lax.dynamic_index_in_dim(qg, r, axis=1, keepdims=False)
        s = jnp.einsum('bqhd,biqjhd->bhqij', q_r, k_win).astype(jnp.float32)
        drow = rs + jnp.arange(wr) - r + (WIN_ROWS - 1)
        s = s + rpb_c[:, :, drow][None]
        p = jax.nn.softmax(s.reshape(B, H, GRID_W, wr * WIN_COLS), axis=-1)
        p = p.reshape(B, H, GRID_W, wr, WIN_COLS).astype(v.dtype)
        return jnp.einsum('bhqij,biqjhd->bqhd', p, v_win)

    out = lax.map(row_fn, jnp.arange(rows))
    return jnp.transpose(out, (1, 0, 2, 3, 4)).reshape(B, T, H * dh)


def spatial_gating(u, vs, ln_g, ln_b, w_s, b_s):
    B, T, _ = vs.shape
    vs = layer_norm(vs, ln_g, ln_b)
    vc = vs.reshape(B, T // CHUNK, CHUNK, SGU_GROUPS, SGU_GROUP_DIM)
    s = jnp.einsum('gpq,bnqgc->bnpgc', w_s, vc) + jnp.transpose(b_s)[None, None, :, :, None]
    return u * s.reshape(B, T, SGU_WIDTH)


def encoder_layer(x, c, w_ada, b_ada, w_in, rpb, sgu_ln_g, sgu_ln_b, w_s, b_s,
                  w_attn_up, w_sgu_up, w_o, ln1_g, ln1_b, w_ff1, b_ff1, w_ff2, b_ff2,
                  ln2_g, ln2_b):
    B, T, _ = x.shape
    ada = jax.nn.silu(c) @ w_ada + b_ada
    shift1, scale1, gate1, shift2, scale2, gate2 = [a[:, None, :] for a in jnp.split(ada, N_ADA, axis=-1)]

    h = x * (1.0 + scale1) + shift1
    z = h @ w_in
    splits = np.cumsum([NA_WIDTH, NA_WIDTH, NA_WIDTH, SGU_WIDTH, SGU_WIDTH, D_MODEL]).tolist()
    q, k, v, u, vs, ga, gb = jnp.split(z, splits, axis=-1)
    attn = neighborhood_attention(q.reshape(B, T, NA_HEADS, NA_HEAD_DIM),
                                  k.reshape(B, T, NA_HEADS, NA_HEAD_DIM),
                                  v.reshape(B, T, NA_HEADS, NA_HEAD_DIM), rpb)
    sgu = spatial_gating(jax.nn.gelu(u), jax.nn.gelu(vs), sgu_ln_g, sgu_ln_b, w_s, b_s)
    merged = jax.nn.sigmoid(ga) * (attn @ w_attn_up) + jax.nn.sigmoid(gb) * (sgu @ w_sgu_up)
    mix = merged @ w_o
    x = layer_norm(ALPHA * x + gate1 * mix, ln1_g, ln1_b)

    h = x * (1.0 + scale2) + shift2
    f = jnp.square(jax.nn.relu(h @ w_ff1 + b_ff1)) @ w_ff2 + b_ff2
    x = layer_norm(ALPHA * x + gate2 * f, ln2_g, ln2_b)
    return x


def run_trunk(x, c, w_ada, b_ada, w_in, rpb, sgu_ln_g, sgu_ln_b, w_s, b_s,
              w_attn_up, w_sgu_up, w_o, ln1_g, ln1_b, w_ff1, b_ff1, w_ff2, b_ff2,
              ln2_g, ln2_b):
    for l in range(DEPTH):
        x = encoder_layer(x, c, w_ada[l], b_ada[l], w_in[l], rpb[l], sgu_ln_g[l], sgu_ln_b[l],
                          w_s[l], b_s[l], w_attn_up[l], w_sgu_up[l], w_o[l], ln1_g[l], ln1_b[l],
                          w_ff1[l], b_ff1[l], w_ff2[l], b_ff2[l], ln2_g[l], ln2_b[l])
    return x


def setup_inputs(seed: int = 0) -> dict:
    key = jax.random.key(seed)
    ks = jax.random.split(key, 24)
    f32 = jnp.float32
    L, D = DEPTH, D_MODEL

    def nrm(k, shape, std):
        return jax.random.normal(k, shape, f32) * std

    return {
        "x_prompt": nrm(ks[0], (BATCH, SEQ, D), 1.0),
        "x_sample": nrm(ks[1], (DEC_BATCH, DEC_SEQ, D), 1.0),
        "c_prompt": nrm(ks[2], (BATCH, D), 1.0),
        "c_sample": nrm(ks[3], (DEC_BATCH, D), 1.0),
        "w_ada": nrm(ks[4], (L, D, N_ADA * D), 0.5 * D ** -0.5),
        "b_ada": nrm(ks[5], (L, N_ADA * D), 0.02),
        "w_in": nrm(ks[6], (L, D, IN_WIDTH), D ** -0.5),
        "rpb": nrm(ks[7], (L, NA_HEADS, 2 * WIN_ROWS - 1, 2 * WIN_COLS - 1), 0.1),
        "sgu_ln_g": 1.0 + nrm(ks[8], (L, SGU_WIDTH), 0.02),
        "sgu_ln_b": nrm(ks[9], (L, SGU_WIDTH), 0.02),
        "w_s": nrm(ks[10], (L, SGU_GROUPS, CHUNK, CHUNK), 0.5 * CHUNK ** -0.5),
        "b_s": 1.0 + nrm(ks[11], (L, SGU_GROUPS, CHUNK), 0.02),
        "w_attn_up": nrm(ks[12], (L, NA_WIDTH, D), NA_WIDTH ** -0.5),
        "w_sgu_up": nrm(ks[13], (L, SGU_WIDTH, D), SGU_WIDTH ** -0.5),
        "w_o": nrm(ks[14], (L, D, D), BETA * D ** -0.5),
        "ln1_g": 1.0 + nrm(ks[15], (L, D), 0.02),
        "ln1_b": nrm(ks[16], (L, D), 0.02),
        "w_ff1": nrm(ks[17], (L, D, D_FF), D ** -0.5),
        "b_ff1": nrm(ks[18], (L, D_FF), 0.02),
        "w_ff2": nrm(ks[19], (L, D_FF, D), BETA * D_FF ** -0.5),
        "b_ff2": nrm(ks[20], (L, D), 0.02),
        "ln2_g": 1.0 + nrm(ks[21], (L, D), 0.02),
        "ln2_b": nrm(ks[22], (L, D), 0.02),
    }


def reference(x_prompt, x_sample, c_prompt, c_sample, w_ada, b_ada, w_in, rpb, sgu_ln_g, sgu_ln_b,
              w_s, b_s, w_attn_up, w_sgu_up, w_o, ln1_g, ln1_b, w_ff1, b_ff1, w_ff2, b_ff2,
              ln2_g, ln2_b):
    y_prompt = run_trunk(x_prompt, c_prompt, w_ada, b_ada, w_in, rpb, sgu_ln_g, sgu_ln_b, w_s, b_s,
                         w_attn_up, w_sgu_up, w_o, ln1_g, ln1_b, w_ff1, b_ff1, w_ff2, b_ff2,
                         ln2_g, ln2_b)
    y_sample = run_trunk(x_sample, c_sample, w_ada, b_ada, w_in, rpb, sgu_ln_g, sgu_ln_b, w_s, b_s,
                         w_attn_up, w_sgu_up, w_o, ln1_g, ln1_b, w_ff1, b_ff1, w_ff2, b_ff2,
                         ln2_g, ln2_b)
    return (y_prompt, y_sample)
```

```python
import numpy as np
from contextlib import ExitStack
import concourse.bass as bass
import concourse.mybir as mybir
from concourse.bass_utils import run_bass_kernel_spmd

F32 = mybir.dt.float32
BF16 = mybir.dt.bfloat16
AF = mybir.ActivationFunctionType
ALU = mybir.AluOpType

D = 1024
ALPHA = 2.0 ** 0.25
LN_EPS = 1e-5
NEG = -30000.0
NCORES = 8
SEG_TOK = 4096
EXT_TOK = 4608
NSLOT = 6
NDMASEM = 8
COMPUTE = ("pe", "act", "dve", "pool")


class Op:
    __slots__ = ("eng", "fn", "deps", "is_dma", "signal", "sigidx", "dsem", "dval", "prev_on_sem", "tag")

    def __init__(self, eng, fn, is_dma):
        self.eng = eng
        self.fn = fn
        self.deps = []
        self.is_dma = is_dma
        self.signal = False
        self.sigidx = 0
        self.dsem = None
        self.dval = 0
        self.prev_on_sem = None


class Cell:
    __slots__ = ("w", "r", "excl")

    def __init__(self, excl=False):
        self.w = None
        self.r = []
        self.excl = excl


class Sched:
    def __init__(self):
        self.ops = {e: [] for e in ("pe", "act", "dve", "pool", "sp")}
        self.dma_count = {e: 0 for e in ("act", "pool", "sp")}
        self.dma_last = {}

    def add(self, eng, fn, reads=(), writes=(), dma=False):
        op = Op(eng, fn, dma)
        op.tag = getattr(self, "stage", "")
        deps = []
        if any(c.excl for c in reads):
            writes = list(writes) + [c for c in reads if c.excl]
            reads = [c for c in reads if not c.excl]
        for c in reads:
            if c.w is not None:
                deps.append(c.w)
        for c in writes:
            if c.w is not None:
                deps.append(c.w)
            deps.extend(c.r)
        seen = set()
        out = []
        for d in deps:
            if id(d) in seen:
                continue
            seen.add(id(d))
            if (not d.is_dma) and (not dma) and d.eng == "pe" and eng == "pe":
                continue
            out.append(d)
        op.deps = out
        for c in reads:
            if not dma:
                c.r = [o for o in c.r if o.is_dma or o.eng != eng]
            c.r.append(op)
        for c in writes:
            c.w = op
            c.r = []
        self.ops[eng].append(op)
        if dma:
            k = self.dma_count[eng]
            self.dma_count[eng] = k + 1
            key = (eng, k % NDMASEM)
            op.prev_on_sem = self.dma_last.get(key)
            op.dsem = key
            op.dval = 16 * (k // NDMASEM + 1)
            self.dma_last[key] = op
            op.signal = True
        for d in out:
            d.signal = True
        return op

    def emit(self, nc, final_waits=()):
        with ExitStack() as es:
            csem = {e: es.enter_context(nc.semaphore(f"c_{e}")) for e in COMPUTE}
            dsem = {}
            for e in ("act", "pool", "sp"):
                for j in range(NDMASEM):
                    if self.dma_count[e] > j:
                        dsem[(e, j)] = es.enter_context(nc.semaphore(f"d_{e}{j}"))
            for e in COMPUTE:
                n = 0
                for op in self.ops[e]:
                    if not op.is_dma and op.signal:
                        n += 1
                        op.sigidx = n
            block = es.enter_context(nc.Block())

            def run(ename):
                def body(eng):
                    known = {}

                    def wait_for(d):
                        if d.is_dma:
                            key, val, sem = d.dsem, d.dval, dsem[d.dsem]
                        else:
                            key, val, sem = d.eng, d.sigidx, csem[d.eng]
                        if known.get(key, 0) >= val:
                            return
                        eng.wait_ge(sem, val)
                        known[key] = val

                    for op in self.ops[ename]:
                        for d in op.deps:
                            wait_for(d)
                        if op.is_dma and op.prev_on_sem is not None:
                            wait_for(op.prev_on_sem)
                        ins = op.fn(eng)
                        if op.is_dma:
                            ins.then_inc(dsem[op.dsem], 16)
                        elif op.signal:
                            ins.then_inc(csem[ename], 1)
                    if ename == "sp":
                        for d in final_waits:
                            wait_for(d)

                return body

            block.tensor(run("pe"))
            block.scalar(run("act"))
            block.vector(run("dve"))
            block.gpsimd(run("pool"))
            block.sync(run("sp"))


def build_program(nseg=2, ntiles=8):
    nc = bass.Bass("TRN2", target_bir_lowering=False)
    dt_in = lambda name, shape: nc.dram_tensor(name, shape, F32, kind="ExternalInput").ap()
    xs_d = dt_in("xs", [2, EXT_TOK, D])
    cc_d = dt_in("cc", [2, 8, 128])
    bt_d = dt_in("btab", [2, 5, 128, 5120])
    w_ada_d = dt_in("w_ada", [D, 6 * D])
    b_ada_d = dt_in("b_ada", [1, 6 * D])
    w_in_d = dt_in("w_in", [D, 4608])
    sgu_g_d = dt_in("sgu_ln_g", [1, 512])
    sgu_b_d = dt_in("sgu_ln_b", [1, 512])
    w_s_d = dt_in("w_s", [8, 128, 128])
    b_s_d = dt_in("b_s", [8, 128])
    w_au_d = dt_in("w_attn_up", [512, D])
    w_su_d = dt_in("w_sgu_up", [512, D])
    w_o_d = dt_in("w_o", [D, D])
    ln1_g_d = dt_in("ln1_g", [1, D])
    ln1_b_d = dt_in("ln1_b", [1, D])
    w_ff1_d = dt_in("w_ff1", [D, 4096])
    b_ff1_d = dt_in("b_ff1", [32, 128])
    w_ff2_d = dt_in("w_ff2", [4096, D])
    b_ff2_d = dt_in("b_ff2", [1, D])
    ln2_g_d = dt_in("ln2_g", [1, D])
    ln2_b_d = dt_in("ln2_b", [1, D])
    y_d = nc.dram_tensor("y", [2, SEG_TOK, D], F32, kind="ExternalOutput").ap()

    kp = lambda ap: ap.rearrange("(k p) n -> p k n", p=128)
    w_ada_v, w_in_v, w_au_v, w_su_v = kp(w_ada_d), kp(w_in_d), kp(w_au_d), kp(w_su_d)
    w_o_v, w_ff1_v, w_ff2_v = kp(w_o_d), kp(w_ff1_d), kp(w_ff2_d)

    S = Sched()
    C = Cell
    out_ops = []
    with ExitStack() as es:
        sb = lambda name, shape, dt: es.enter_context(nc.sbuf_tensor(name, shape, dt))
        wring = [sb(f"wring{i}", [128, 4096], BF16) for i in range(NSLOT)]
        wring_c = [C() for _ in range(NSLOT)]
        kT = [sb(f"kT{i}", [128, 4, 512], BF16) for i in range(3)]
        kT_c = [C() for _ in range(3)]
        Vr = [sb(f"V{i}", [128, 4, 8, 65], BF16) for i in range(3)]
        V_c = [C() for _ in range(3)]
        xs = [sb(f"xs{i}", [128, D], F32) for i in range(4)]
        xs_c = [C() for _ in range(4)]
        hT = sb("hT", [128, 8, 512], BF16)
        hT_c = [C() for _ in range(8)]
        pool32 = sb("pool32", [128, 32, 512], BF16)
        p32_c = [C() for _ in range(32)]
        gv = sb("gv", [128, 512], F32); gv_c = C()
        branch = [sb(f"branch{i}", [128, D], F32) for i in range(2)]
        br_c = [[C(), C()] for _ in range(2)]
        sga = sb("sga", [128, 512], F32); sga_c = C()
        sgb = sb("sgb", [128, 512], F32); sgb_c = C()
        x1 = sb("x1", [128, 4, D], F32); x1_c = [C() for _ in range(4)]
        rr = sb("rr", [128, 512], F32); rr_c = C()
        bt = sb("bt", [128, 8, 5, 128], BF16); bt_c = C()
        g1bc = sb("g1bc", [128, D], F32); g1bc_c = C()
        b1bc = sb("b1bc", [128, D], F32); b1bc_c = C()
        g2bc = sb("g2bc", [128, D], F32); g2bc_c = C()
        ln1g_bc = sb("ln1g_bc", [128, D], F32); ln1g_bc_c = C()
        ln2g_bc = sb("ln2g_bc", [128, D], F32); ln2g_bc_c = C()
        ln2b_bc = sb("ln2b_bc", [128, D], F32); ln2b_bc_c = C()
        PT = [sb(f"PT{i}", [128, 1280], BF16) for i in range(2)]
        PT_c = [[C(), C()] for _ in range(2)]
        WsT = sb("WsT", [128, 8, 128], BF16); WsT_c = C()
        ident = sb("ident", [128, 128], F32); ident_c = C()
        sgug_bc = sb("sgug_bc", [128, 512], F32); sgug_c = C()
        sgub_bc = sb("sgub_bc", [128, 512], F32); sgub_c = C()
        bsT = sb("bsT", [128, 8], F32); bsT_c = C()
        b1col = sb("b1col", [128, 32], F32); b1col_c = C()
        cols = sb("cols", [128, 6, 8], F32); cols_c = C()
        lncol = sb("lncol", [128, 2, 8], F32); lncol_c = C()
        rows8 = sb("rows8", [32, 128], F32); rows8_c = C()
        siluc = sb("siluc", [128, 8], F32); siluc_c = C()
        silubc = PT[0][:, 0:1024].rearrange("p (k n) -> p k n", n=128)
        stt = sb("stt", [128, 4, 2, 6], F32); stt_c = [C() for _ in range(4)]
        mv = sb("mv", [128, 4, 2], F32); mv_c = [C() for _ in range(4)]
        rstd = sb("rstd", [128, 4], F32); rstd_c = [C() for _ in range(4)]
        nmr = sb("nmr", [128, 4], F32); nmr_c = [C() for _ in range(4)]
        rec = sb("rec", [128, 8, 1], F32); rec_c = C()
        PSB = es.enter_context(nc.psum_tensor("psb", [128, 4096], F32))
        PS = [PSB[:, i * 1024:(i + 1) * 1024] for i in range(4)]
        ps_c = [C(True) for _ in range(8)]
        state = {"st": 0, "spair": 0, "bank": 0, "pair": 0, "slot": 0, "xsb": 0, "brb": 0, "ptb": 0, "alt": 0}

        def next_bank():
            b = state["bank"]
            state["bank"] = (b + 1) % 8
            return PS[b // 2][:, (b % 2) * 512:(b % 2) * 512 + 512], ps_c[b]

        def next_pair():
            p = state["pair"]
            state["pair"] = (p + 1) % 4
            return PS[p], [ps_c[2 * p], ps_c[2 * p + 1]]

        def mm(out, lhsT, rhs, start, stop, reads, writes):
            S.add("pe", lambda q: q.matmul(out, lhsT=lhsT, rhs=rhs, start=start, stop=stop), reads, writes)

        def tr(out, in_, idn, reads, writes):
            S.add("pe", lambda q: q.transpose(out=out, in_=in_, identity=idn), reads, writes)

        def act(out, in_, func, reads, writes, scale=1.0, bias=0.0):
            S.add("act", lambda q: q.activation(out=out, in_=in_, func=func, bias=bias, scale=scale), reads, writes)

        def tt(eng, out, in0, in1, op, reads, writes):
            S.add(eng, lambda q: q.tensor_tensor(out=out, in0=in0, in1=in1, op=op), reads, writes)

        def ts(eng, out, in0, s1, s2, op0, op1, reads, writes):
            S.add(eng, lambda q: q.tensor_scalar(out=out, in0=in0, scalar1=s1, scalar2=s2, op0=op0, op1=op1), reads, writes)

        def stt_op(eng, out, in0, scalar, in1, op0, op1, reads, writes):
            S.add(eng, lambda q: q.scalar_tensor_tensor(out=out, in0=in0, scalar=scalar, in1=in1, op0=op0, op1=op1), reads, writes)

        def cp(eng, out, in_, reads, writes):
            S.add(eng, lambda q: q.tensor_copy(out=out, in_=in_), reads, writes)

        def dma(eng, out, in_, reads, writes):
            return S.add(eng, lambda q: q.dma_start(out=out, in_=in_), reads, writes, dma=True)

        def load_slab(src_ap, nk, ncol):
            i = state["slot"]
            state["slot"] = (i + 1) % NSLOT
            view = wring[i][:, 0:nk * ncol].rearrange("p (k n) -> p k n", k=nk)
            dma("pool", view, src_ap, [], [wring_c[i]])
            return view, wring_c[i]

        def load_pair(srcA, srcB, nk, ncol):
            i = state["slot"]
            state["slot"] = (i + 1) % NSLOT
            half = nk * ncol
            vA = wring[i][:, 0:half].rearrange("p (k n) -> p k n", k=nk)
            vB = wring[i][:, half:2 * half].rearrange("p (k n) -> p k n", k=nk)
            dma("pool", vA, srcA, [], [wring_c[i]])
            dma("pool", vB, srcB, [], [wring_c[i]])
            return vA, vB, wring_c[i]

        def alt_eng():
            import os
            f = os.environ.get("KEVAC", "")
            if f:
                return f
            state["alt"] ^= 1
            return "act" if state["alt"] else "dve"

        def evac_affine(out, in_, scol, bcol, reads, writes):
            import os
            if os.environ.get("KAFF", "dve") == "dve" and alt_eng() == "dve":
                ts("dve", out, in_, scol, bcol, ALU.mult, ALU.add, reads, writes)
            else:
                act(out, in_, AF.Identity, reads, writes, scale=scol, bias=bcol)

        def evac_copy(out, in_, reads, writes, scale=None):
            if alt_eng() == "act":
                if scale is None:
                    act(out, in_, AF.Copy, reads, writes)
                else:
                    act(out, in_, AF.Copy, reads, writes, scale=scale)
            else:
                if scale is None:
                    cp("dve", out, in_, reads, writes)
                else:
                    S.add("dve", lambda q: q.tensor_scalar_mul(out=out, in0=in_, scalar1=scale), reads, writes)

        def next_xs():
            i = state["xsb"]
            state["xsb"] = (i + 1) % 4
            return i

        def ln_phase_a(src, src_cells, eps, nhalf=2):
            i = state["st"]
            state["st"] = (i + 1) % 4
            for h in range(nhalf):
                S.add("dve", lambda q, h=h: q.bn_stats(out=stt[:, i, h, :], in_=src[:, h * 512:(h + 1) * 512]), src_cells, [stt_c[i]])
            S.add("dve", lambda q: q.bn_aggr(out=mv[:, i, :], in_=stt[:, i, 0:nhalf, :].rearrange("p a b -> p (a b)")), [stt_c[i]], [mv_c[i]])
            S.add("dve", lambda q: q.tensor_scalar_add(out=rstd[:, i:i + 1], in0=mv[:, i, 1:2], scalar1=eps), [mv_c[i]], [rstd_c[i]])
            act(rstd[:, i:i + 1], rstd[:, i:i + 1], AF.Sqrt, [rstd_c[i]], [rstd_c[i]])
            return i

        def ln_phase_b(i, dst, src, cells):
            S.add("dve", lambda q: q.reciprocal(out=rstd[:, i:i + 1], in_=rstd[:, i:i + 1]), [rstd_c[i]], [rstd_c[i]])
            stt_op("dve", nmr[:, i:i + 1], mv[:, i, 0:1], -1.0, rstd[:, i:i + 1], ALU.mult, ALU.mult, [mv_c[i], rstd_c[i]], [nmr_c[i]])
            act(dst, src, AF.Identity, cells + [rstd_c[i], nmr_c[i]], cells, scale=rstd[:, i:i + 1], bias=nmr[:, i:i + 1])

        def lnb_stats(c, src, src_cells, eps, nhalf=2):
            for h in range(nhalf):
                S.add("dve", lambda q, h=h: q.bn_stats(out=stt[:, c, h, :], in_=src[:, h * 512:(h + 1) * 512]), src_cells, [stt_c[c]])
            S.add("dve", lambda q: q.bn_aggr(out=mv[:, c, :], in_=stt[:, c, 0:nhalf, :].rearrange("p a b -> p (a b)")), [stt_c[c]], [mv_c[c]])
            S.add("dve", lambda q: q.tensor_scalar_add(out=rstd[:, c:c + 1], in0=mv[:, c, 1:2], scalar1=eps), [mv_c[c]], [rstd_c[c]])

        def lnb_rsqrt():
            act(rstd[:, 0:4], rstd[:, 0:4], AF.Sqrt, rstd_c, rstd_c)
            S.add("dve", lambda q: q.reciprocal(out=rstd[:, 0:4], in_=rstd[:, 0:4]), rstd_c, rstd_c)
            stt_op("dve", nmr[:, 0:4], mv[:, :, 0], -1.0, rstd[:, 0:4], ALU.mult, ALU.mult, mv_c + rstd_c, nmr_c)

        def lnb_norm(c, dst, src, cells):
            act(dst, src, AF.Identity, cells + [rstd_c[c], nmr_c[c]], cells, scale=rstd[:, c:c + 1], bias=nmr[:, c:c + 1])

        def layer_norm_stats(src, src_cells, eps, nhalf=2):
            i = state["st"]
            state["st"] = (i + 1) % 4
            for h in range(nhalf):
                S.add("dve", lambda q, h=h: q.bn_stats(out=stt[:, i, h, :], in_=src[:, h * 512:(h + 1) * 512]), src_cells, [stt_c[i]])
            S.add("dve", lambda q: q.bn_aggr(out=mv[:, i, :], in_=stt[:, i, 0:nhalf, :].rearrange("p a b -> p (a b)")), [stt_c[i]], [mv_c[i]])
            S.add("dve", lambda q: q.tensor_scalar_add(out=rstd[:, i:i + 1], in0=mv[:, i, 1:2], scalar1=eps), [mv_c[i]], [rstd_c[i]])
            act(rstd[:, i:i + 1], rstd[:, i:i + 1], AF.Sqrt, [rstd_c[i]], [rstd_c[i]])
            S.add("dve", lambda q: q.reciprocal(out=rstd[:, i:i + 1], in_=rstd[:, i:i + 1]), [rstd_c[i]], [rstd_c[i]])
            stt_op("dve", nmr[:, i:i + 1], mv[:, i, 0:1], -1.0, rstd[:, i:i + 1], ALU.mult, ALU.mult, [mv_c[i], rstd_c[i]], [nmr_c[i]])
            return i

        S.add("pool", lambda q: q.memset(ident[:], 0.0), [], [ident_c])
        S.add("pool", lambda q: q.affine_select(out=ident[:], in_=ident[:], compare_op=ALU.not_equal, fill=1.0, base=0,
                                                pattern=[[-1, 128]], channel_multiplier=1), [ident_c], [ident_c])
        for i in range(3):
            S.add("pool", lambda q, i=i: q.memset(Vr[i][:], 1.0), [], [V_c[i]])
        dma("sp", ln1g_bc[:], ln1_g_d.partition_broadcast(128), [], [ln1g_bc_c])
        dma("sp", ln2g_bc[:], ln2_g_d.partition_broadcast(128), [], [ln2g_bc_c])
        dma("sp", ln2b_bc[:], ln2_b_d.partition_broadcast(128), [], [ln2b_bc_c])
        dma("sp", sgug_bc[:], sgu_g_d.partition_broadcast(128), [], [sgug_c])
        dma("sp", sgub_bc[:], sgu_b_d.partition_broadcast(128), [], [sgub_c])
        dma("sp", rows8[:, :], b_ff1_d, [], [rows8_c])
        o, oc = next_bank()
        tr(o[:, 0:32], rows8[0:32, :], ident[0:32, 0:32], [rows8_c, ident_c], [oc])
        cp("dve", b1col[:], o[:, 0:32], [oc], [b1col_c])
        dma("sp", rows8[0:8, :], b_s_d, [b1col_c], [rows8_c])
        o, oc = next_bank()
        tr(o[:, 0:8], rows8[0:8, :], ident[0:8, 0:8], [rows8_c, ident_c], [oc])
        cp("dve", bsT[:], o[:, 0:8], [oc], [bsT_c])
        dma("sp", rows8[0:8, :], ln1_g_d.rearrange("o (k p) -> (o k) p", p=128), [bsT_c], [rows8_c])
        dma("sp", rows8[8:16, :], ln1_b_d.rearrange("o (k p) -> (o k) p", p=128), [bsT_c], [rows8_c])
        o, oc = next_bank()
        tr(o[:, 0:16], rows8[0:16, :], ident[0:16, 0:16], [rows8_c, ident_c], [oc])
        cp("dve", lncol[:].rearrange("p a b -> p (a b)"), o[:, 0:16], [oc], [lncol_c])
        for g in range(8):
            dma("sp", xs[0][:, g * 128:(g + 1) * 128], w_s_d[g], [], [xs_c[0]])
        pr, prc = next_pair()
        for g in range(8):
            tr(pr[:, g * 128:(g + 1) * 128], xs[0][:, g * 128:(g + 1) * 128], ident[:], [xs_c[0], ident_c], [prc[g // 4]])
        cp("dve", WsT[:].rearrange("p g n -> p (g n)"), pr[:, :], prc, [WsT_c])

        def segment_setup(s):
            dma("sp", rows8[0:8, :], cc_d[s], [lncol_c, cols_c], [rows8_c])
            o, oc = next_bank()
            tr(o[:, 0:8], rows8[0:8, :], ident[0:8, 0:8], [rows8_c, ident_c], [oc])
            act(siluc[:], o[:, 0:8], AF.Silu, [oc], [siluc_c])
            cp("dve", silubc, siluc[:].unsqueeze(2).to_broadcast([128, 8, 128]), [siluc_c], PT_c[0])
            colmap = {0: 0, 1: 1, 3: 2, 4: 3}
            for n in range(12):
                comp, half = n // 2, n % 2
                slab, sc = load_slab(w_ada_v[:, :, n * 512:(n + 1) * 512], 8, 512)
                o, oc = next_bank()
                for k in range(8):
                    mm(o, silubc[:, k, :], slab[:, k, :], k == 0, k == 7, PT_c[0] + [sc], [oc])
                dma("sp", gv[:], b_ada_d[:, n * 512:(n + 1) * 512].partition_broadcast(128), [], [gv_c])
                if comp == 2 or comp == 5:
                    dst, dc = (g1bc, g1bc_c) if comp == 2 else (g2bc, g2bc_c)
                    tt("dve", dst[:, half * 512:(half + 1) * 512], o, gv[:], ALU.add, [oc, gv_c], [dc])
                    S.add("dve", lambda q, dst=dst, half=half: q.tensor_scalar_mul(out=dst[:, half * 512:(half + 1) * 512],
                                                                                 in0=dst[:, half * 512:(half + 1) * 512], scalar1=1.0 / ALPHA), [dc], [dc])
                else:
                    tt("dve", sga[:], o, gv[:], ALU.add, [oc, gv_c], [sga_c])
                    o2, oc2 = next_bank()
                    for j in range(4):
                        tr(o2[:, j * 128:(j + 1) * 128], sga[:, j * 128:(j + 1) * 128], ident[:], [sga_c, ident_c], [oc2])
                    ci = colmap[comp]
                    src = o2.rearrange("p (j n) -> p j n", n=128)[:, :, 0:1]
                    dstc = cols[:, ci, half * 4:(half + 1) * 4].unsqueeze(2)
                    if comp in (1, 4):
                        S.add("dve", lambda q, dstc=dstc, src=src: q.tensor_scalar_add(out=dstc, in0=src, scalar1=1.0), [oc2], [cols_c])
                    else:
                        cp("dve", dstc, src, [oc2], [cols_c])
            tt("dve", cols[:, 4, :], lncol[:, 0, :], cols[:, 3, :], ALU.mult, [lncol_c, cols_c], [cols_c])
            tt("dve", cols[:, 5, :], lncol[:, 1, :], cols[:, 3, :], ALU.mult, [lncol_c, cols_c], [cols_c])
            tt("dve", cols[:, 5, :], cols[:, 5, :], cols[:, 2, :], ALU.add, [cols_c], [cols_c])
            dma("sp", xs[0][:], b_ff2_d.partition_broadcast(128), [], [xs_c[0]])
            dma("sp", b1bc[:], ln1_b_d.partition_broadcast(128), [], [b1bc_c])
            tt("dve", xs[0][:], xs[0][:], g2bc[:], ALU.mult, [xs_c[0], g2bc_c], [xs_c[0]])
            tt("dve", b1bc[:], b1bc[:], xs[0][:], ALU.add, [xs_c[0], b1bc_c], [b1bc_c])

        import os as _os
        _stop = _os.environ.get("KSTOP", "")

        class _Stop(Exception):
            pass

        def ck(name):
            S.stage = name
            if name == _stop:
                raise _Stop()

        def transpose_affine(srcs, src_cells, scol_i, bcol_i, act_only=False):
            for half in range(2):
                banks = [next_bank() for _ in range(4)]
                for kk in range(4):
                    k = half * 4 + kk
                    for c in range(4):
                        tr(banks[kk][0][:, c * 128:(c + 1) * 128], srcs[c][:, k * 128:(k + 1) * 128], ident[:], [src_cells[c], ident_c], [banks[kk][1]])
                for kk in range(4):
                    k = half * 4 + kk
                    if act_only:
                        act(hT[:, k, :], banks[kk][0], AF.Identity, [banks[kk][1], cols_c], [hT_c[k]],
                            scale=cols[:, scol_i, k:k + 1], bias=cols[:, bcol_i, k:k + 1])
                    else:
                        evac_affine(hT[:, k, :], banks[kk][0], cols[:, scol_i, k:k + 1], cols[:, bcol_i, k:k + 1], [banks[kk][1], cols_c], [hT_c[k]])

        def load_x4(s, tok0):
            bufs = [next_xs() for _ in range(4)]
            for c in range(4):
                dma("sp", xs[bufs[c]][:], xs_d[s, tok0 + c * 128: tok0 + (c + 1) * 128, :], [], [xs_c[bufs[c]]])
            return bufs

        def make_hT(s, tok0, nchunks, bufs=None, act_only=False):
            if bufs is None:
                bufs = load_x4(s, tok0)
            transpose_affine([xs[b] for b in bufs], [xs_c[b] for b in bufs], 1, 0, act_only)

        def kv_front(s, b, bufs=None):
            ck("kv_start")
            make_hT(s, b * 512, 4, bufs)

        def kv_back(s, b):
            slot = b % 3
            ck("kv_h")
            slab, sc = load_slab(w_in_v[:, :, 512:1024], 8, 512)
            for m in range(4):
                o, oc = next_bank()
                for k in range(8):
                    mm(o, slab[:, k, m * 128:(m + 1) * 128], hT[:, k, :], k == 0, k == 7, [sc, hT_c[k]], [oc])
                evac_copy(kT[slot][:, m, :], o, [oc], [kT_c[slot]])
            ck("kv_k")
            slab, sc = load_slab(w_in_v[:, :, 1024:1536], 8, 512)
            for c in range(4):
                o, oc = next_bank()
                for k in range(8):
                    mm(o, hT[:, k, c * 128:(c + 1) * 128], slab[:, k, :], k == 0, k == 7, [sc, hT_c[k]], [oc])
                evac_copy(Vr[slot][:, c, :, 0:64], o.rearrange("p (h d) -> p h d", d=64), [oc], [V_c[slot]])

        def kv_block(s, b):
            kv_front(s, b)
            kv_back(s, b)

        QT0, UG0, VN0, VN1, BRT0, MT0 = 16, 20, 24, 25, 0, 8

        def main_tile(s, t):
            ck("m_start")
            make_hT(s, 256 + t * 512, 4, act_only=True)
            ck("m_h")
            vslab, vsc = load_slab(w_in_v[:, :, 2048:2560], 8, 512)
            gvb = [(gv, gv_c), (sga, sga_c), (sgb, sgb_c), (rr, rr_c)]
            for c in range(4):
                g_, g_c = gvb[c]
                o, oc = next_bank()
                for k in range(8):
                    mm(o, hT[:, k, c * 128:(c + 1) * 128], vslab[:, k, :], k == 0, k == 7, [vsc, hT_c[k]], [oc])
                act(g_[:], o, AF.Gelu_apprx_tanh, [oc], [g_c])
                lnb_stats(c, g_, [g_c], LN_EPS, nhalf=1)
            lnb_rsqrt()
            for c in range(4):
                g_, g_c = gvb[c]
                lnb_norm(c, g_[:], g_[:], [g_c])
                tt("dve", g_[:], g_[:], sgug_bc[:], ALU.mult, [g_c, sgug_c], [g_c])
                tt("dve", pool32[:, VN0 + c, :], g_[:], sgub_bc[:], ALU.add, [g_c, sgub_c], [p32_c[VN0 + c]])

            slab, sc = load_slab(w_in_v[:, :, 0:512], 8, 512)
            for m in range(4):
                o, oc = next_bank()
                for k in range(8):
                    mm(o, slab[:, k, m * 128:(m + 1) * 128], hT[:, k, :], k == 0, k == 7, [sc, hT_c[k]], [oc])
                evac_copy(pool32[:, QT0 + m, :], o, [oc], [p32_c[QT0 + m]], scale=0.125)
            ck("m_q")
            slab, sc = load_slab(w_in_v[:, :, 1536:2048], 8, 512)
            for c in range(4):
                o, oc = next_bank()
                for k in range(8):
                    mm(o, hT[:, k, c * 128:(c + 1) * 128], slab[:, k, :], k == 0, k == 7, [sc, hT_c[k]], [oc])
                act(pool32[:, UG0 + c, :], o, AF.Gelu_apprx_tanh, [oc], [p32_c[UG0 + c]])
            ck("m_u")
            def sgu_back(c, brt, bb):
                g_, g_c = gvb[c]
                vn_i = VN0 + c
                o, oc = next_bank()
                for g in range(8):
                    mm(o[:, g * 64:(g + 1) * 64], WsT[:, g, :], pool32[:, vn_i, g * 64:(g + 1) * 64], True, True, [WsT_c, p32_c[vn_i]], [oc])
                tt("dve", g_[:].rearrange("p (g d) -> p g d", d=64), o.rearrange("p (g d) -> p g d", d=64),
                   bsT[:].unsqueeze(2).to_broadcast([128, 8, 64]), ALU.add, [oc, bsT_c], [g_c])
                tt("dve", brt[:, 512:1024], g_[:], pool32[:, UG0 + c, :], ALU.mult, [g_c, p32_c[UG0 + c]], [br_c[bb][1]])

            for c in range(4):
                mc = 4 * t + c
                bb = state["brb"]; state["brb"] ^= 1
                brt = branch[bb]
                ck("c_start")
                ck("m_sgu")
                ttype = {0: 0, 1: 1, 30: 3, 31: 4}.get(mc, 2)
                if mc in (0, 1, 2, 30, 31):
                    btf = bt[:].rearrange("p h j n -> p (h j n)")
                    dma("pool", btf, bt_d[s, ttype], [], [bt_c])
                    act(btf, btf, AF.Exp, [bt_c], [bt_c])
                Opair, Oc = PS[3], [ps_c[6], ps_c[7]]
                Ov = Opair[:, :].rearrange("p (h d) -> p h d", d=128)

                def S_stage(i):
                    u = state["spair"]; state["spair"] = u ^ 1
                    base = u * 1536
                    cells = [ps_c[3 * u], ps_c[3 * u + 1], ps_c[3 * u + 2]]
                    for j in range(5):
                        e = mc + j
                        blk, ci = e // 4, e % 4
                        slot = blk % 3
                        for hh in range(2):
                            hp = hh * 64
                            off = hh * 640 + j * 128
                            mm(PSB[:, base + off: base + off + 128], kT[slot][hp:hp + 64, i, ci * 128:(ci + 1) * 128],
                               pool32[hp:hp + 64, QT0 + i, c * 128:(c + 1) * 128], True, True,
                               [kT_c[slot], p32_c[QT0 + i]], [cells[off // 512]])
                    return PSB[:, base: base + 1280], cells

                def E_stage(i, Sp, Sc):
                    pb = state["ptb"]; state["ptb"] = pb ^ 1
                    for hh in range(2):
                        hsl = slice(hh * 640, (hh + 1) * 640)
                        act(PT[pb][:, hsl], Sp[:, hsl], AF.Exp, Sc, [PT_c[pb][hh]])
                        tt("dve", PT[pb][:, hsl], PT[pb][:, hsl], bt[:, 2 * i + hh, :, :].rearrange("p j n -> p (j n)"), ALU.mult,
                           [PT_c[pb][hh], bt_c], [PT_c[pb][hh]])
                    return pb

                def PV_stage(i, pb):
                    for hh in range(2):
                        h = 2 * i + hh
                        for j in range(5):
                            e = mc + j
                            blk, ci = e // 4, e % 4
                            slot = blk % 3
                            mm(Ov[:, h, 0:65], PT[pb][:, hh * 640 + j * 128: hh * 640 + (j + 1) * 128], Vr[slot][:, ci, h, :], j == 0, j == 4,
                               [PT_c[pb][hh], V_c[slot]], [Oc[h // 4]])

                Sq = [S_stage(0), S_stage(1)]
                for i in range(4):
                    pb = E_stage(i, *Sq.pop(0))
                    PV_stage(i, pb)
                    if i + 2 < 4:
                        Sq.append(S_stage(i + 2))
                S.add("dve", lambda q, Ov=Ov: q.reciprocal(out=rec[:], in_=Ov[:, :, 64:65]), Oc, [rec_c])
                tt("dve", brt[:, 0:512].rearrange("p (h d) -> p h d", d=64), Ov[:, :, 0:64], rec[:].to_broadcast([128, 8, 64]),
                   ALU.mult, Oc + [rec_c], [br_c[bb][0]])
                sgu_back(c, brt, bb)
                ck("m_att")
                pr, prc = next_pair()
                for k in range(8):
                    tr(pr[:, k * 128:(k + 1) * 128], brt[:, k * 128:(k + 1) * 128], ident[:], [br_c[bb][k // 4], ident_c], [prc[k // 4]])
                ck("m_trp")
                for k in range(8):
                    evac_copy(pool32[:, BRT0 + k, c * 128:(c + 1) * 128], pr[:, k * 128:(k + 1) * 128], [prc[k // 4]], [p32_c[BRT0 + k]])
                ck("m_c%d" % c)
            kvbufs = load_x4(s, (t + 2) * 512) if t + 2 <= 8 else None
            ck("m_tr")
            for half in range(2):
                au, su, auc = load_pair(w_au_v[:, :, half * 512:(half + 1) * 512], w_su_v[:, :, half * 512:(half + 1) * 512], 4, 512)
                suc = auc
                ga, gac = load_slab(w_in_v[:, :, 2560 + half * 512: 2560 + (half + 1) * 512], 8, 512)
                gb, gbc = load_slab(w_in_v[:, :, 3584 + half * 512: 3584 + (half + 1) * 512], 8, 512)
                for m in range(4):
                    oA, oAc = next_bank()
                    for k in range(4):
                        mm(oA, au[:, k, m * 128:(m + 1) * 128], pool32[:, BRT0 + k, :], k == 0, k == 3, [auc, p32_c[BRT0 + k]], [oAc])
                    oB, oBc = next_bank()
                    for k in range(4):
                        mm(oB, su[:, k, m * 128:(m + 1) * 128], pool32[:, BRT0 + 4 + k, :], k == 0, k == 3, [suc, p32_c[BRT0 + 4 + k]], [oBc])
                    oG, oGc = next_bank()
                    for k in range(8):
                        mm(oG, ga[:, k, m * 128:(m + 1) * 128], hT[:, k, :], k == 0, k == 7, [gac, hT_c[k]], [oGc])
                    oH, oHc = next_bank()
                    for k in range(8):
                        mm(oH, gb[:, k, m * 128:(m + 1) * 128], hT[:, k, :], k == 0, k == 7, [gbc, hT_c[k]], [oHc])
                    tail_step()
                    act(sga[:], oG, AF.Sigmoid, [oGc], [sga_c])
                    act(sgb[:], oH, AF.Sigmoid, [oHc], [sgb_c])
                    tt("dve", sga[:], oA, sga[:], ALU.mult, [oAc, sga_c], [sga_c])
                    tt("dve", sgb[:], oB, sgb[:], ALU.mult, [oBc, sgb_c], [sgb_c])
                    tt("dve", pool32[:, MT0 + half * 4 + m, :], sga[:], sgb[:], ALU.add, [sga_c, sgb_c], [p32_c[MT0 + half * 4 + m]])
                    tail_step()
            ck("m_merge")
            flush_all()
            if t + 2 <= 8:
                kv_front(s, t + 2, kvbufs)
            ck("m_merge")
            wo = [load_slab(w_o_v[:, :, half * 512:(half + 1) * 512], 8, 512) for half in range(2)]
            xr = [next_xs() for _ in range(4)]
            ln1_slots = []
            for c in range(4):
                tok = 256 + t * 512 + c * 128
                dma("sp", xs[xr[c]][:], xs_d[s, tok:tok + 128, :], [], [xs_c[xr[c]]])
            for c in range(4):
                xb = xr[c]
                for half in range(2):
                    o, oc = next_bank()
                    for k in range(8):
                        mm(o, pool32[:, MT0 + k, c * 128:(c + 1) * 128], wo[half][0][:, k, :], k == 0, k == 7, [p32_c[MT0 + k], wo[half][1]], [oc])
                    hs = slice(half * 512, (half + 1) * 512)
                    tt("dve", x1[:, c, hs], o, g1bc[:, hs], ALU.mult, [oc, g1bc_c], [x1_c[c]])
                tt("dve", x1[:, c, :], x1[:, c, :], xs[xb][:], ALU.add, [xs_c[xb], x1_c[c]], [x1_c[c]])
                lnb_stats(c, x1[:, c, :], [x1_c[c]], LN_EPS / (ALPHA * ALPHA))
            lnb_rsqrt()
            for c in range(4):
                lnb_norm(c, x1[:, c, :], x1[:, c, :], [x1_c[c]])
            if t + 2 <= 8:
                kv_back(s, t + 2)
            ck("m_h2")
            transpose_affine([x1[:, c, :] for c in range(4)], [x1_c[c] for c in range(4)], 4, 5)
            for c in range(4):
                tt("dve", x1[:, c, :], x1[:, c, :], ln1g_bc[:], ALU.mult, [x1_c[c], ln1g_bc_c], [x1_c[c]])
                tt("dve", x1[:, c, :], x1[:, c, :], b1bc[:], ALU.add, [x1_c[c], b1bc_c], [x1_c[c]])
            ck("m_wo")
            for sl in range(8):
                slab, sc = load_slab(w_ff1_v[:, :, sl * 512:(sl + 1) * 512], 8, 512)
                for m in range(4):
                    mi = sl * 4 + m
                    o, oc = next_bank()
                    for k in range(8):
                        mm(o, slab[:, k, m * 128:(m + 1) * 128], hT[:, k, :], k == 0, k == 7, [sc, hT_c[k]], [oc])
                    act(rr[:], o, AF.Relu, [oc, b1col_c], [rr_c], bias=b1col[:, mi:mi + 1])
                    tt("dve", pool32[:, mi, :], rr[:], rr[:], ALU.mult, [rr_c], [p32_c[mi]])
            ck("m_ff1")
            for sl in range(8):
                slab, sc = load_slab(w_ff2_v[:, 4 * sl:4 * sl + 4, :], 4, 1024)
                for c in range(4):
                    for half in range(2):
                        b = c * 2 + half
                        o = PS[b // 2][:, (b % 2) * 512:(b % 2) * 512 + 512]
                        for kk in range(4):
                            mm(o, pool32[:, 4 * sl + kk, c * 128:(c + 1) * 128], slab[:, kk, half * 512:(half + 1) * 512],
                               sl == 0 and kk == 0, sl == 7 and kk == 3, [p32_c[4 * sl + kk], sc], [ps_c[b]])
            state["bank"] = 0
            state["pair"] = 0
            ck("m_ff2")
            tmps = [(gv, gv_c), (sga, sga_c), (sgb, sgb_c), (rr, rr_c)]
            bank = lambda b: PS[b // 2][:, (b % 2) * 512:(b % 2) * 512 + 512]
            for b in range(4):
                hs = slice((b % 2) * 512, (b % 2 + 1) * 512)
                tt("dve", tmps[b][0][:], bank(b), g2bc[:, hs], ALU.mult, [ps_c[b], g2bc_c], [tmps[b][1]])
            for b in range(4, 8):
                i = b - 4
                act(branch[i // 2][:, (i % 2) * 512:(i % 2 + 1) * 512], bank(b), AF.Copy, [ps_c[b]], [br_c[i // 2][i % 2]])
            for b in range(4):
                c, hs = b // 2, slice((b % 2) * 512, (b % 2 + 1) * 512)
                tt("dve", x1[:, c, hs], x1[:, c, hs], tmps[b][0][:], ALU.add, [x1_c[c], tmps[b][1]], [x1_c[c]])
            for b in range(4, 8):
                i = b - 4
                c, hs = b // 2, slice((b % 2) * 512, (b % 2 + 1) * 512)
                src = branch[i // 2][:, (i % 2) * 512:(i % 2 + 1) * 512]
                tt("dve", src, src, g2bc[:, hs], ALU.mult, [br_c[i // 2][i % 2], g2bc_c], [br_c[i // 2][i % 2]])
                tt("dve", x1[:, c, hs], x1[:, c, hs], src, ALU.add, [x1_c[c], br_c[i // 2][i % 2]], [x1_c[c]])

            def ln2_tails(s=s, t=t):
                for c in range(4):
                    lnb_stats(c, x1[:, c, :], [x1_c[c]], LN_EPS / (ALPHA * ALPHA))
                    yield
                lnb_rsqrt()
                yield
                for c in range(4):
                    lnb_norm(c, x1[:, c, :], x1[:, c, :], [x1_c[c]])
                    yield
                    tt("dve", x1[:, c, :], x1[:, c, :], ln2g_bc[:], ALU.mult, [x1_c[c], ln2g_bc_c], [x1_c[c]])
                    tt("dve", x1[:, c, :], x1[:, c, :], ln2b_bc[:], ALU.add, [x1_c[c], ln2b_bc_c], [x1_c[c]])
                    tok = t * 512 + c * 128
                    out_ops.append(dma("sp", y_d[s, tok:tok + 128, :], x1[:, c, :], [x1_c[c]], []))
                    yield

            pending.append(ln2_tails())

        pending = []

        def tail_step():
            while pending:
                try:
                    next(pending[0])
                    return
                except StopIteration:
                    pending.pop(0)

        def flush_all():
            while pending:
                tail_step()

        try:
          ck("setup")
          for s in range(nseg):
            segment_setup(s)
            ck("segsetup")
            kv_block(s, 0)
            ck("kv0")
            kv_block(s, 1)
            for t in range(ntiles):
                main_tile(s, t)
          flush_all()
        except _Stop:
            pass
        if _os.environ.get("KTAGS"):
            import json as _json
            _json.dump({e: [o.tag for o in S.ops[e] if not o.is_dma] for e in S.ops}, open(_os.environ["KTAGS"], "w"))
        S.emit(nc, final_waits=out_ops)
    return nc


def _ext_rows(seq_rows, row0):
    er = np.full((36, 2), -1, np.int64)
    src_chunk = np.zeros(36, np.int64)
    c0 = row0 // 2
    nchunk = seq_rows // 2
    for mc in range(32):
        er[2 + mc] = (row0 + 2 * mc, row0 + 2 * mc + 1)
        src_chunk[2 + mc] = c0 + mc
    if row0 > 0:
        for e in range(2):
            g = c0 - 2 + e
            er[e] = (2 * g, 2 * g + 1)
            src_chunk[e] = g
    else:
        er[1] = (6, 7)
        src_chunk[1] = 3
        src_chunk[0] = 0
    if row0 + 64 < seq_rows:
        for e in range(2):
            g = c0 + 32 + e
            er[34 + e] = (2 * g, 2 * g + 1)
            src_chunk[34 + e] = g
    else:
        er[34] = (seq_rows - 8, seq_rows - 7)
        src_chunk[34] = nchunk - 4
        src_chunk[35] = nchunk - 1
    return er, src_chunk


def _build_btab(rpb, seq_rows, row0):
    er, _ = _ext_rows(seq_rows, row0)
    out = np.full((5, 128, 8, 5, 128), NEG, np.float32)
    q = np.arange(128)
    qro, qc = q // 64, q % 64
    cs = np.clip(qc - 8, 0, 48)
    kcol = np.arange(64)
    colvalid = (kcol[None, :] >= cs[:, None]) & (kcol[None, :] < cs[:, None] + 16)
    dc = np.clip(kcol[None, :] - qc[:, None] + 15, 0, 30)
    for ti, mc in enumerate((0, 1, 2, 30, 31)):
        r = row0 + 2 * mc + qro
        rs = np.clip(r - 4, 0, seq_rows - 8)
        covered = np.zeros((128, seq_rows), bool)
        for j in range(5):
            e = mc + j
            for half in range(2):
                krow = int(er[e, half])
                if krow < 0:
                    continue
                rowvalid = (krow >= rs) & (krow < rs + 8) & (~covered[:, krow])
                covered[:, krow] |= rowvalid
                valid = rowvalid[:, None] & colvalid
                dr = np.clip(krow - r + 7, 0, 14)
                vals = rpb[:, dr[:, None], dc]
                vals = np.transpose(vals, (2, 0, 1))
                out[ti, half * 64:(half + 1) * 64, :, j, :] = np.where(valid.T[:, None, :], vals, np.float32(NEG))
    return out.reshape(5, 128, 5120)


def _ext_tokens(xseq, seq_rows, row0):
    _, src = _ext_rows(seq_rows, row0)
    xc = xseq.reshape(seq_rows // 2, 128, D)
    return xc[src].reshape(EXT_TOK, D)


_NC_CACHE = {}


def kernel(x_prompt, x_sample, c_prompt, c_sample, w_ada, b_ada, w_in, rpb, sgu_ln_g, sgu_ln_b,
           w_s, b_s, w_attn_up, w_sgu_up, w_o, ln1_g, ln1_b, w_ff1, b_ff1, w_ff2, b_ff2, ln2_g, ln2_b):
    f = lambda a: np.ascontiguousarray(np.asarray(a, dtype=np.float32))
    x_prompt, x_sample, c_prompt, c_sample = f(x_prompt), f(x_sample), f(c_prompt), f(c_sample)
    rpb0 = f(rpb)[0]
    shared = {
        "w_ada": f(w_ada)[0], "b_ada": f(b_ada)[0].reshape(1, -1), "w_in": f(w_in)[0],
        "sgu_ln_g": f(sgu_ln_g)[0].reshape(1, -1), "sgu_ln_b": f(sgu_ln_b)[0].reshape(1, -1),
        "w_s": f(w_s)[0], "b_s": f(b_s)[0], "w_attn_up": f(w_attn_up)[0], "w_sgu_up": f(w_sgu_up)[0],
        "w_o": f(w_o)[0], "ln1_g": f(ln1_g)[0].reshape(1, -1), "ln1_b": f(ln1_b)[0].reshape(1, -1),
        "w_ff1": f(w_ff1)[0], "b_ff1": f(b_ff1)[0].reshape(32, 128), "w_ff2": f(w_ff2)[0],
        "b_ff2": f(b_ff2)[0].reshape(1, -1), "ln2_g": f(ln2_g)[0].reshape(1, -1), "ln2_b": f(ln2_b)[0].reshape(1, -1),
    }
    bt_sample = _build_btab(rpb0, 64, 0)
    bt_prompt = [_build_btab(rpb0, 256, 64 * qi) for qi in range(4)]
    in_maps = []
    for i in range(NCORES):
        pi, qi = i // 4, i % 4
        xs = np.stack([_ext_tokens(x_sample[i], 64, 0), _ext_tokens(x_prompt[pi], 256, 64 * qi)])
        cc = np.stack([c_sample[i], c_prompt[pi]]).reshape(2, 8, 128)
        m = dict(shared)
        m["xs"] = np.ascontiguousarray(xs)
        m["cc"] = np.ascontiguousarray(cc)
        m["btab"] = np.ascontiguousarray(np.stack([bt_sample, bt_prompt[qi]]))
        in_maps.append(m)
    if "nc" not in _NC_CACHE:
        _NC_CACHE["nc"] = build_program()
    res = run_bass_kernel_spmd(_NC_CACHE["nc"], in_maps, core_ids=list(range(NCORES)))
    y_prompt = np.empty((2, 16384, D), np.float32)
    y_sample = np.empty((8, 4096, D), np.float32)
    for i in range(NCORES):
        y = np.asarray(res.results[i]["y"], dtype=np.float32)
        y_sample[i] = y[0]
        y_prompt[i // 4, (i % 4) * 4096:(i % 4 + 1) * 4096] = y[1]
    return (y_prompt, y_sample)
```

```python
import numpy as np
from contextlib import ExitStack
import concourse.bass as bass
import concourse.mybir as mybir
from concourse.bass_utils import run_bass_kernel_spmd

F32 = mybir.dt.float32
BF16 = mybir.dt.bfloat16
AF = mybir.ActivationFunctionType
ALU = mybir.AluOpType

D = 1024
ALPHA = 2.0 ** 0.25
LN_EPS = 1e-5
NEG = -30000.0
NCORES = 8
SEG_TOK = 4096
EXT_TOK = 4608
NSLOT = 6
NDMASEM = 8
COMPUTE = ("pe", "act", "dve", "pool")


class Op:
    __slots__ = ("eng", "fn", "deps", "is_dma", "signal", "sigidx", "dsem", "dval", "prev_on_sem", "tag")

    def __init__(self, eng, fn, is_dma):
        self.eng = eng
        self.fn = fn
        self.deps = []
        self.is_dma = is_dma
        self.signal = False
        self.sigidx = 0
        self.dsem = None
        self.dval = 0
        self.prev_on_sem = None


class Cell:
    __slots__ = ("w", "r", "excl")

    def __init__(self, excl=False):
        self.w = None
        self.r = []
        self.excl = excl


class Sched:
    def __init__(self):
        self.ops = {e: [] for e in ("pe", "act", "dve", "pool", "sp")}
        self.dma_count = {e: 0 for e in ("act", "pool", "sp")}
        self.dma_last = {}

    def add(self, eng, fn, reads=(), writes=(), dma=False):
        op = Op(eng, fn, dma)
        op.tag = getattr(self, "stage", "")
        deps = []
        if any(c.excl for c in reads):
            writes = list(writes) + [c for c in reads if c.excl]
            reads = [c for c in reads if not c.excl]
        for c in reads:
            if c.w is not None:
                deps.append(c.w)
        for c in writes:
            if c.w is not None:
                deps.append(c.w)
            deps.extend(c.r)
        seen = set()
        out = []
        for d in deps:
            if id(d) in seen:
                continue
            seen.add(id(d))
            if (not d.is_dma) and (not dma) and d.eng == "pe" and eng == "pe":
                continue
            out.append(d)
        op.deps = out
        for c in reads:
            if not dma:
                c.r = [o for o in c.r if o.is_dma or o.eng != eng]
            c.r.append(op)
        for c in writes:
            c.w = op
            c.r = []
        self.ops[eng].append(op)
        if dma:
            k = self.dma_count[eng]
            self.dma_count[eng] = k + 1
            key = (eng, k % NDMASEM)
            op.prev_on_sem = self.dma_last.get(key)
            op.dsem = key
            op.dval = 16 * (k // NDMASEM + 1)
            self.dma_last[key] = op
            op.signal = True
        for d in out:
            d.signal = True
        return op

    def emit(self, nc, final_waits=()):
        with ExitStack() as es:
            csem = {e: es.enter_context(nc.semaphore(f"c_{e}")) for e in COMPUTE}
            dsem = {}
            for e in ("act", "pool", "sp"):
                for j in range(NDMASEM):
                    if self.dma_count[e] > j:
                        dsem[(e, j)] = es.enter_context(nc.semaphore(f"d_{e}{j}"))
            for e in COMPUTE:
                n = 0
                for op in self.ops[e]:
                    if not op.is_dma and op.signal:
                        n += 1
                        op.sigidx = n
            block = es.enter_context(nc.Block())

            def run(ename):
                def body(eng):
                    known = {}

                    def wait_for(d):
                        if d.is_dma:
                            key, val, sem = d.dsem, d.dval, dsem[d.dsem]
                        else:
                            key, val, sem = d.eng, d.sigidx, csem[d.eng]
                        if known.get(key, 0) >= val:
                            return
                        eng.wait_ge(sem, val)
                        known[key] = val

                    for op in self.ops[ename]:
                        for d in op.deps:
                            wait_for(d)
                        if op.is_dma and op.prev_on_sem is not None:
                            wait_for(op.prev_on_sem)
                        ins = op.fn(eng)
                        if op.is_dma:
                            ins.then_inc(dsem[op.dsem], 16)
                        elif op.signal:
                            ins.then_inc(csem[ename], 1)
                    if ename == "sp":
                        for d in final_waits:
                            wait_for(d)

                return body

            block.tensor(run("pe"))
            block.scalar(run("act"))
            block.vector(run("dve"))
            block.gpsimd(run("pool"))
            block.sync(run("sp"))


def build_program(nseg=2, ntiles=8):
    nc = bass.Bass("TRN2", target_bir_lowering=False)
    dt_in = lambda name, shape: nc.dram_tensor(name, shape, F32, kind="ExternalInput").ap()
    xs_d = dt_in("xs", [2, EXT_TOK, D])
    cc_d = dt_in("cc", [2, 8, 128])
    bt_d = dt_in("btab", [2, 5, 128, 5120])
    w_ada_d = dt_in("w_ada", [D, 6 * D])
    b_ada_d = dt_in("b_ada", [1, 6 * D])
    w_in_d = dt_in("w_in", [D, 4608])
    sgu_g_d = dt_in("sgu_ln_g", [1, 512])
    sgu_b_d = dt_in("sgu_ln_b", [1, 512])
    w_s_d = dt_in("w_s", [8, 128, 128])
    b_s_d = dt_in("b_s", [8, 128])
    w_au_d = dt_in("w_attn_up", [512, D])
    w_su_d = dt_in("w_sgu_up", [512, D])
    w_o_d = dt_in("w_o", [D, D])
    ln1_g_d = dt_in("ln1_g", [1, D])
    ln1_b_d = dt_in("ln1_b", [1, D])
    w_ff1_d = dt_in("w_ff1", [D, 4096])
    b_ff1_d = dt_in("b_ff1", [32, 128])
    w_ff2_d = dt_in("w_ff2", [4096, D])
    b_ff2_d = dt_in("b_ff2", [1, D])
    ln2_g_d = dt_in("ln2_g", [1, D])
    ln2_b_d = dt_in("ln2_b", [1, D])
    y_d = nc.dram_tensor("y", [2, SEG_TOK, D], F32, kind="ExternalOutput").ap()

    kp = lambda ap: ap.rearrange("(k p) n -> p k n", p=128)
    w_ada_v, w_in_v, w_au_v, w_su_v = kp(w_ada_d), kp(w_in_d), kp(w_au_d), kp(w_su_d)
    w_o_v, w_ff1_v, w_ff2_v = kp(w_o_d), kp(w_ff1_d), kp(w_ff2_d)

    S = Sched()
    C = Cell
    out_ops = []
    with ExitStack() as es:
        sb = lambda name, shape, dt: es.enter_context(nc.sbuf_tensor(name, shape, dt))
        wring = [sb(f"wring{i}", [128, 4096], BF16) for i in range(NSLOT)]
        wring_c = [C() for _ in range(NSLOT)]
        kT = [sb(f"kT{i}", [128, 4, 512], BF16) for i in range(3)]
        kT_c = [C() for _ in range(3)]
        Vr = [sb(f"V{i}", [128, 4, 8, 65], BF16) for i in range(3)]
        V_c = [C() for _ in range(3)]
        xs = [sb(f"xs{i}", [128, D], F32) for i in range(4)]
        xs_c = [C() for _ in range(4)]
        hT = sb("hT", [128, 8, 512], BF16)
        hT_c = [C() for _ in range(8)]
        pool32 = sb("pool32", [128, 32, 512], BF16)
        p32_c = [C() for _ in range(32)]
        gv = sb("gv", [128, 512], F32); gv_c = C()
        branch = [sb(f"branch{i}", [128, D], F32) for i in range(2)]
        br_c = [[C(), C()] for _ in range(2)]
        sga = sb("sga", [128, 512], F32); sga_c = C()
        sgb = sb("sgb", [128, 512], F32); sgb_c = C()
        x1 = sb("x1", [128, 4, D], F32); x1_c = [C() for _ in range(4)]
        rr = sb("rr", [128, 512], F32); rr_c = C()
        bt = sb("bt", [128, 8, 5, 128], BF16); bt_c = C()
        g1bc = sb("g1bc", [128, D], F32); g1bc_c = C()
        b1bc = sb("b1bc", [128, D], F32); b1bc_c = C()
        g2bc = sb("g2bc", [128, D], F32); g2bc_c = C()
        ln1g_bc = sb("ln1g_bc", [128, D], F32); ln1g_bc_c = C()
        ln2g_bc = sb("ln2g_bc", [128, D], F32); ln2g_bc_c = C()
        ln2b_bc = sb("ln2b_bc", [128, D], F32); ln2b_bc_c = C()
        PT = [sb(f"PT{i}", [128, 1280], BF16) for i in range(2)]
        PT_c = [[C(), C()] for _ in range(2)]
        WsT = sb("WsT", [128, 8, 128], BF16); WsT_c = C()
        ident = sb("ident", [128, 128], F32); ident_c = C()
        sgug_bc = sb("sgug_bc", [128, 512], F32); sgug_c = C()
        sgub_bc = sb("sgub_bc", [128, 512], F32); sgub_c = C()
        bsT = sb("bsT", [128, 8], F32); bsT_c = C()
        b1col = sb("b1col", [128, 32], F32); b1col_c = C()
        cols = sb("cols", [128, 6, 8], F32); cols_c = C()
        lncol = sb("lncol", [128, 2, 8], F32); lncol_c = C()
        rows8 = sb("rows8", [32, 128], F32); rows8_c = C()
        siluc = sb("siluc", [128, 8], F32); siluc_c = C()
        silubc = PT[0][:, 0:1024].rearrange("p (k n) -> p k n", n=128)
        stt = sb("stt", [128, 4, 2, 6], F32); stt_c = [C() for _ in range(4)]
        mv = sb("mv", [128, 4, 2], F32); mv_c = [C() for _ in range(4)]
        rstd = sb("rstd", [128, 4], F32); rstd_c = [C() for _ in range(4)]
        nmr = sb("nmr", [128, 4], F32); nmr_c = [C() for _ in range(4)]
        rec = sb("rec", [128, 8, 1], F32); rec_c = C()
        PSB = es.enter_context(nc.psum_tensor("psb", [128, 4096], F32))
        PS = [PSB[:, i * 1024:(i + 1) * 1024] for i in range(4)]
        ps_c = [C(True) for _ in range(8)]
        state = {"st": 0, "spair": 0, "bank": 0, "pair": 0, "slot": 0, "xsb": 0, "brb": 0, "ptb": 0, "alt": 0}

        def next_bank():
            b = state["bank"]
            state["bank"] = (b + 1) % 8
            return PS[b // 2][:, (b % 2) * 512:(b % 2) * 512 + 512], ps_c[b]

        def next_pair():
            p = state["pair"]
            state["pair"] = (p + 1) % 4
            return PS[p], [ps_c[2 * p], ps_c[2 * p + 1]]

        def mm(out, lhsT, rhs, start, stop, reads, writes):
            S.add("pe", lambda q: q.matmul(out, lhsT=lhsT, rhs=rhs, start=start, stop=stop), reads, writes)

        def tr(out, in_, idn, reads, writes):
            S.add("pe", lambda q: q.transpose(out=out, in_=in_, identity=idn), reads, writes)

        def act(out, in_, func, reads, writes, scale=1.0, bias=0.0):
            S.add("act", lambda q: q.activation(out=out, in_=in_, func=func, bias=bias, scale=scale), reads, writes)

        def tt(eng, out, in0, in1, op, reads, writes):
            S.add(eng, lambda q: q.tensor_tensor(out=out, in0=in0, in1=in1, op=op), reads, writes)

        def ts(eng, out, in0, s1, s2, op0, op1, reads, writes):
            S.add(eng, lambda q: q.tensor_scalar(out=out, in0=in0, scalar1=s1, scalar2=s2, op0=op0, op1=op1), reads, writes)

        def stt_op(eng, out, in0, scalar, in1, op0, op1, reads, writes):
            S.add(eng, lambda q: q.scalar_tensor_tensor(out=out, in0=in0, scalar=scalar, in1=in1, op0=op0, op1=op1), reads, writes)

        def cp(eng, out, in_, reads, writes):
            S.add(eng, lambda q: q.tensor_copy(out=out, in_=in_), reads, writes)

        def dma(eng, out, in_, reads, writes):
            return S.add(eng, lambda q: q.dma_start(out=out, in_=in_), reads, writes, dma=True)

        def load_slab(src_ap, nk, ncol):
            i = state["slot"]
            state["slot"] = (i + 1) % NSLOT
            view = wring[i][:, 0:nk * ncol].rearrange("p (k n) -> p k n", k=nk)
            dma("pool", view, src_ap, [], [wring_c[i]])
            return view, wring_c[i]

        def load_pair(srcA, srcB, nk, ncol):
            i = state["slot"]
            state["slot"] = (i + 1) % NSLOT
            half = nk * ncol
            vA = wring[i][:, 0:half].rearrange("p (k n) -> p k n", k=nk)
            vB = wring[i][:, half:2 * half].rearrange("p (k n) -> p k n", k=nk)
            dma("pool", vA, srcA, [], [wring_c[i]])
            dma("pool", vB, srcB, [], [wring_c[i]])
            return vA, vB, wring_c[i]

        def alt_eng():
            import os
            f = os.environ.get("KEVAC", "")
            if f:
                return f
            state["alt"] ^= 1
            return "act" if state["alt"] else "dve"

        def evac_affine(out, in_, scol, bcol, reads, writes):
            import os
            if os.environ.get("KAFF", "dve") == "dve" and alt_eng() == "dve":
                ts("dve", out, in_, scol, bcol, ALU.mult, ALU.add, reads, writes)
            else:
                act(out, in_, AF.Identity, reads, writes, scale=scol, bias=bcol)

        def evac_copy(out, in_, reads, writes, scale=None):
            if alt_eng() == "act":
                if scale is None:
                    act(out, in_, AF.Copy, reads, writes)
                else:
                    act(out, in_, AF.Copy, reads, writes, scale=scale)
            else:
                if scale is None:
                    cp("dve", out, in_, reads, writes)
                else:
                    S.add("dve", lambda q: q.tensor_scalar_mul(out=out, in0=in_, scalar1=scale), reads, writes)

        def next_xs():
            i = state["xsb"]
            state["xsb"] = (i + 1) % 4
            return i

        def ln_phase_a(src, src_cells, eps, nhalf=2):
            i = state["st"]
            state["st"] = (i + 1) % 4
            for h in range(nhalf):
                S.add("dve", lambda q, h=h: q.bn_stats(out=stt[:, i, h, :], in_=src[:, h * 512:(h + 1) * 512]), src_cells, [stt_c[i]])
            S.add("dve", lambda q: q.bn_aggr(out=mv[:, i, :], in_=stt[:, i, 0:nhalf, :].rearrange("p a b -> p (a b)")), [stt_c[i]], [mv_c[i]])
            S.add("dve", lambda q: q.tensor_scalar_add(out=rstd[:, i:i + 1], in0=mv[:, i, 1:2], scalar1=eps), [mv_c[i]], [rstd_c[i]])
            act(rstd[:, i:i + 1], rstd[:, i:i + 1], AF.Sqrt, [rstd_c[i]], [rstd_c[i]])
            return i

        def ln_phase_b(i, dst, src, cells):
            S.add("dve", lambda q: q.reciprocal(out=rstd[:, i:i + 1], in_=rstd[:, i:i + 1]), [rstd_c[i]], [rstd_c[i]])
            stt_op("dve", nmr[:, i:i + 1], mv[:, i, 0:1], -1.0, rstd[:, i:i + 1], ALU.mult, ALU.mult, [mv_c[i], rstd_c[i]], [nmr_c[i]])
            act(dst, src, AF.Identity, cells + [rstd_c[i], nmr_c[i]], cells, scale=rstd[:, i:i + 1], bias=nmr[:, i:i + 1])

        def lnb_stats(c, src, src_cells, eps, nhalf=2):
            for h in range(nhalf):
                S.add("dve", lambda q, h=h: q.bn_stats(out=stt[:, c, h, :], in_=src[:, h * 512:(h + 1) * 512]), src_cells, [stt_c[c]])
            S.add("dve", lambda q: q.bn_aggr(out=mv[:, c, :], in_=stt[:, c, 0:nhalf, :].rearrange("p a b -> p (a b)")), [stt_c[c]], [mv_c[c]])
            S.add("dve", lambda q: q.tensor_scalar_add(out=rstd[:, c:c + 1], in0=mv[:, c, 1:2], scalar1=eps), [mv_c[c]], [rstd_c[c]])

        def lnb_rsqrt():
            act(rstd[:, 0:4], rstd[:, 0:4], AF.Sqrt, rstd_c, rstd_c)
            S.add("dve", lambda q: q.reciprocal(out=rstd[:, 0:4], in_=rstd[:, 0:4]), rstd_c, rstd_c)
            stt_op("dve", nmr[:, 0:4], mv[:, :, 0], -1.0, rstd[:, 0:4], ALU.mult, ALU.mult, mv_c + rstd_c, nmr_c)

        def lnb_norm(c, dst, src, cells):
            act(dst, src, AF.Identity, cells + [rstd_c[c], nmr_c[c]], cells, scale=rstd[:, c:c + 1], bias=nmr[:, c:c + 1])

        def layer_norm_stats(src, src_cells, eps, nhalf=2):
            i = state["st"]
            state["st"] = (i + 1) % 4
            for h in range(nhalf):
                S.add("dve", lambda q, h=h: q.bn_stats(out=stt[:, i, h, :], in_=src[:, h * 512:(h + 1) * 512]), src_cells, [stt_c[i]])
            S.add("dve", lambda q: q.bn_aggr(out=mv[:, i, :], in_=stt[:, i, 0:nhalf, :].rearrange("p a b -> p (a b)")), [stt_c[i]], [mv_c[i]])
            S.add("dve", lambda q: q.tensor_scalar_add(out=rstd[:, i:i + 1], in0=mv[:, i, 1:2], scalar1=eps), [mv_c[i]], [rstd_c[i]])
            act(rstd[:, i:i + 1], rstd[:, i:i + 1], AF.Sqrt, [rstd_c[i]], [rstd_c[i]])
            S.add("dve", lambda q: q.reciprocal(out=rstd[:, i:i + 1], in_=rstd[:, i:i + 1]), [rstd_c[i]], [rstd_c[i]])
            stt_op("dve", nmr[:, i:i + 1], mv[:, i, 0:1], -1.0, rstd[:, i:i + 1], ALU.mult, ALU.mult, [mv_c[i], rstd_c[i]], [nmr_c[i]])
            return i

        S.add("pool", lambda q: q.memset(ident[:], 0.0), [], [ident_c])
        S.add("pool", lambda q: q.affine_select(out=ident[:], in_=ident[:], compare_op=ALU.not_equal, fill=1.0, base=0,
                                                pattern=[[-1, 128]], channel_multiplier=1), [ident_c], [ident_c])
        for i in range(3):
            S.add("pool", lambda q, i=i: q.memset(Vr[i][:], 1.0), [], [V_c[i]])
        dma("sp", ln1g_bc[:], ln1_g_d.partition_broadcast(128), [], [ln1g_bc_c])
        dma("sp", ln2g_bc[:], ln2_g_d.partition_broadcast(128), [], [ln2g_bc_c])
        dma("sp", ln2b_bc[:], ln2_b_d.partition_broadcast(128), [], [ln2b_bc_c])
        dma("sp", sgug_bc[:], sgu_g_d.partition_broadcast(128), [], [sgug_c])
        dma("sp", sgub_bc[:], sgu_b_d.partition_broadcast(128), [], [sgub_c])
        dma("sp", rows8[:, :], b_ff1_d, [], [rows8_c])
        o, oc = next_bank()
        tr(o[:, 0:32], rows8[0:32, :], ident[0:32, 0:32], [rows8_c, ident_c], [oc])
        cp("dve", b1col[:], o[:, 0:32], [oc], [b1col_c])
        dma("sp", rows8[0:8, :], b_s_d, [b1col_c], [rows8_c])
        o, oc = next_bank()
        tr(o[:, 0:8], rows8[0:8, :], ident[0:8, 0:8], [rows8_c, ident_c], [oc])
        cp("dve", bsT[:], o[:, 0:8], [oc], [bsT_c])
        dma("sp", rows8[0:8, :], ln1_g_d.rearrange("o (k p) -> (o k) p", p=128), [bsT_c], [rows8_c])
        dma("sp", rows8[8:16, :], ln1_b_d.rearrange("o (k p) -> (o k) p", p=128), [bsT_c], [rows8_c])
        o, oc = next_bank()
        tr(o[:, 0:16], rows8[0:16, :], ident[0:16, 0:16], [rows8_c, ident_c], [oc])
        cp("dve", lncol[:].rearrange("p a b -> p (a b)"), o[:, 0:16], [oc], [lncol_c])
        for g in range(8):
            dma("sp", xs[0][:, g * 128:(g + 1) * 128], w_s_d[g], [], [xs_c[0]])
        pr, prc = next_pair()
        for g in range(8):
            tr(pr[:, g * 128:(g + 1) * 128], xs[0][:, g * 128:(g + 1) * 128], ident[:], [xs_c[0], ident_c], [prc[g // 4]])
        cp("dve", WsT[:].rearrange("p g n -> p (g n)"), pr[:, :], prc, [WsT_c])

        def segment_setup(s):
            dma("sp", rows8[0:8, :], cc_d[s], [lncol_c, cols_c], [rows8_c])
            o, oc = next_bank()
            tr(o[:, 0:8], rows8[0:8, :], ident[0:8, 0:8], [rows8_c, ident_c], [oc])
            act(siluc[:], o[:, 0:8], AF.Silu, [oc], [siluc_c])
            cp("dve", silubc, siluc[:].unsqueeze(2).to_broadcast([128, 8, 128]), [siluc_c], PT_c[0])
            colmap = {0: 0, 1: 1, 3: 2, 4: 3}
            for n in range(12):
                comp, half = n // 2, n % 2
                slab, sc = load_slab(w_ada_v[:, :, n * 512:(n + 1) * 512], 8, 512)
                o, oc = next_bank()
                for k in range(8):
                    mm(o, silubc[:, k, :], slab[:, k, :], k == 0, k == 7, PT_c[0] + [sc], [oc])
                dma("sp", gv[:], b_ada_d[:, n * 512:(n + 1) * 512].partition_broadcast(128), [], [gv_c])
                if comp == 2 or comp == 5:
                    dst, dc = (g1bc, g1bc_c) if comp == 2 else (g2bc, g2bc_c)
                    tt("dve", dst[:, half * 512:(half + 1) * 512], o, gv[:], ALU.add, [oc, gv_c], [dc])
                    S.add("dve", lambda q, dst=dst, half=half: q.tensor_scalar_mul(out=dst[:, half * 512:(half + 1) * 512],
                                                                                 in0=dst[:, half * 512:(half + 1) * 512], scalar1=1.0 / ALPHA), [dc], [dc])
                else:
                    tt("dve", sga[:], o, gv[:], ALU.add, [oc, gv_c], [sga_c])
                    o2, oc2 = next_bank()
                    for j in range(4):
                        tr(o2[:, j * 128:(j + 1) * 128], sga[:, j * 128:(j + 1) * 128], ident[:], [sga_c, ident_c], [oc2])
                    ci = colmap[comp]
                    src = o2.rearrange("p (j n) -> p j n", n=128)[:, :, 0:1]
                    dstc = cols[:, ci, half * 4:(half + 1) * 4].unsqueeze(2)
                    if comp in (1, 4):
                        S.add("dve", lambda q, dstc=dstc, src=src: q.tensor_scalar_add(out=dstc, in0=src, scalar1=1.0), [oc2], [cols_c])
                    else:
                        cp("dve", dstc, src, [oc2], [cols_c])
            tt("dve", cols[:, 4, :], lncol[:, 0, :], cols[:, 3, :], ALU.mult, [lncol_c, cols_c], [cols_c])
            tt("dve", cols[:, 5, :], lncol[:, 1, :], cols[:, 3, :], ALU.mult, [lncol_c, cols_c], [cols_c])
            tt("dve", cols[:, 5, :], cols[:, 5, :], cols[:, 2, :], ALU.add, [cols_c], [cols_c])
            dma("sp", xs[0][:], b_ff2_d.partition_broadcast(128), [], [xs_c[0]])
            dma("sp", b1bc[:], ln1_b_d.partition_broadcast(128), [], [b1bc_c])
            tt("dve", xs[0][:], xs[0][:], g2bc[:], ALU.mult, [xs_c[0], g2bc_c], [xs_c[0]])
            tt("dve", b1bc[:], b1bc[:], xs[0][:], ALU.add, [xs_c[0], b1bc_c], [b1bc_c])

        import os as _os
        _stop = _os.environ.get("KSTOP", "")

        class _Stop(Exception):
            pass

        def ck(name):
            S.stage = name
            if name == _stop:
                raise _Stop()

        def transpose_affine(srcs, src_cells, scol_i, bcol_i, act_only=False):
            for half in range(2):
                banks = [next_bank() for _ in range(4)]
                for kk in range(4):
                    k = half * 4 + kk
                    for c in range(4):
                        tr(banks[kk][0][:, c * 128:(c + 1) * 128], srcs[c][:, k * 128:(k + 1) * 128], ident[:], [src_cells[c], ident_c], [banks[kk][1]])
                for kk in range(4):
                    k = half * 4 + kk
                    if act_only:
                        act(hT[:, k, :], banks[kk][0], AF.Identity, [banks[kk][1], cols_c], [hT_c[k]],
                            scale=cols[:, scol_i, k:k + 1], bias=cols[:, bcol_i, k:k + 1])
                    else:
                        evac_affine(hT[:, k, :], banks[kk][0], cols[:, scol_i, k:k + 1], cols[:, bcol_i, k:k + 1], [banks[kk][1], cols_c], [hT_c[k]])

        def load_x4(s, tok0):
            bufs = [next_xs() for _ in range(4)]
            for c in range(4):
                dma("sp", xs[bufs[c]][:], xs_d[s, tok0 + c * 128: tok0 + (c + 1) * 128, :], [], [xs_c[bufs[c]]])
            return bufs

        def make_hT(s, tok0, nchunks, bufs=None, act_only=False):
            if bufs is None:
                bufs = load_x4(s, tok0)
            transpose_affine([xs[b] for b in bufs], [xs_c[b] for b in bufs], 1, 0, act_only)

        def kv_front(s, b, bufs=None):
            ck("kv_start")
            make_hT(s, b * 512, 4, bufs, act_only=(bufs is not None))

        def kv_back(s, b):
            slot = b % 3
            ck("kv_h")
            slab, sc = load_slab(w_in_v[:, :, 512:1024], 8, 512)
            for m in range(4):
                o, oc = next_bank()
                for k in range(8):
                    mm(o, slab[:, k, m * 128:(m + 1) * 128], hT[:, k, :], k == 0, k == 7, [sc, hT_c[k]], [oc])
                evac_copy(kT[slot][:, m, :], o, [oc], [kT_c[slot]])
            ck("kv_k")
            slab, sc = load_slab(w_in_v[:, :, 1024:1536], 8, 512)
            for c in range(4):
                o, oc = next_bank()
                for k in range(8):
                    mm(o, hT[:, k, c * 128:(c + 1) * 128], slab[:, k, :], k == 0, k == 7, [sc, hT_c[k]], [oc])
                evac_copy(Vr[slot][:, c, :, 0:64], o.rearrange("p (h d) -> p h d", d=64), [oc], [V_c[slot]])

        def kv_block(s, b):
            kv_front(s, b)
            kv_back(s, b)

        QT0, UG0, VN0, VN1, BRT0, MT0 = 16, 20, 24, 25, 0, 8

        def main_tile(s, t):
            ck("m_start")
            make_hT(s, 256 + t * 512, 4, act_only=True)
            ck("m_h")
            vslab, vsc = load_slab(w_in_v[:, :, 2048:2560], 8, 512)
            gvb = [(gv, gv_c), (sga, sga_c), (sgb, sgb_c), (rr, rr_c)]
            for c in range(4):
                g_, g_c = gvb[c]
                o, oc = next_bank()
                for k in range(8):
                    mm(o, hT[:, k, c * 128:(c + 1) * 128], vslab[:, k, :], k == 0, k == 7, [vsc, hT_c[k]], [oc])
                act(g_[:], o, AF.Gelu_apprx_tanh, [oc], [g_c])
                lnb_stats(c, g_, [g_c], LN_EPS, nhalf=1)
            lnb_rsqrt()
            for c in range(4):
                g_, g_c = gvb[c]
                lnb_norm(c, g_[:], g_[:], [g_c])
                tt("dve", g_[:], g_[:], sgug_bc[:], ALU.mult, [g_c, sgug_c], [g_c])
                tt("dve", pool32[:, VN0 + c, :], g_[:], sgub_bc[:], ALU.add, [g_c, sgub_c], [p32_c[VN0 + c]])

            slab, sc = load_slab(w_in_v[:, :, 0:512], 8, 512)
            for m in range(4):
                o, oc = next_bank()
                for k in range(8):
                    mm(o, slab[:, k, m * 128:(m + 1) * 128], hT[:, k, :], k == 0, k == 7, [sc, hT_c[k]], [oc])
                evac_copy(pool32[:, QT0 + m, :], o, [oc], [p32_c[QT0 + m]], scale=0.125)
            ck("m_q")
            slab, sc = load_slab(w_in_v[:, :, 1536:2048], 8, 512)
            for c in range(4):
                o, oc = next_bank()
                for k in range(8):
                    mm(o, hT[:, k, c * 128:(c + 1) * 128], slab[:, k, :], k == 0, k == 7, [sc, hT_c[k]], [oc])
                act(pool32[:, UG0 + c, :], o, AF.Gelu_apprx_tanh, [oc], [p32_c[UG0 + c]])
            ck("m_u")
            def sgu_back(c, brt, bb):
                g_, g_c = gvb[c]
                vn_i = VN0 + c
                o, oc = next_bank()
                for g in range(8):
                    mm(o[:, g * 64:(g + 1) * 64], WsT[:, g, :], pool32[:, vn_i, g * 64:(g + 1) * 64], True, True, [WsT_c, p32_c[vn_i]], [oc])
                tt("dve", g_[:].rearrange("p (g d) -> p g d", d=64), o.rearrange("p (g d) -> p g d", d=64),
                   bsT[:].unsqueeze(2).to_broadcast([128, 8, 64]), ALU.add, [oc, bsT_c], [g_c])
                tt("dve", brt[:, 512:1024], g_[:], pool32[:, UG0 + c, :], ALU.mult, [g_c, p32_c[UG0 + c]], [br_c[bb][1]])

            for c in range(4):
                mc = 4 * t + c
                bb = state["brb"]; state["brb"] ^= 1
                brt = branch[bb]
                ck("c_start")
                ck("m_sgu")
                ttype = {0: 0, 1: 1, 30: 3, 31: 4}.get(mc, 2)
                if mc in (0, 1, 2, 30, 31):
                    btf = bt[:].rearrange("p h j n -> p (h j n)")
                    dma("pool", btf, bt_d[s, ttype], [], [bt_c])
                    act(btf, btf, AF.Exp, [bt_c], [bt_c])
                Opair, Oc = PS[3], [ps_c[6], ps_c[7]]
                Ov = Opair[:, :].rearrange("p (h d) -> p h d", d=128)

                def S_stage(i):
                    u = state["spair"]; state["spair"] = u ^ 1
                    base = u * 1536
                    cells = [ps_c[3 * u], ps_c[3 * u + 1], ps_c[3 * u + 2]]
                    for j in range(5):
                        e = mc + j
                        blk, ci = e // 4, e % 4
                        slot = blk % 3
                        for hh in range(2):
                            hp = hh * 64
                            off = hh * 640 + j * 128
                            mm(PSB[:, base + off: base + off + 128], kT[slot][hp:hp + 64, i, ci * 128:(ci + 1) * 128],
                               pool32[hp:hp + 64, QT0 + i, c * 128:(c + 1) * 128], True, True,
                               [kT_c[slot], p32_c[QT0 + i]], [cells[off // 512]])
                    return PSB[:, base: base + 1280], cells

                def E_stage(i, Sp, Sc):
                    pb = state["ptb"]; state["ptb"] = pb ^ 1
                    for hh in range(2):
                        hsl = slice(hh * 640, (hh + 1) * 640)
                        act(PT[pb][:, hsl], Sp[:, hsl], AF.Exp, Sc, [PT_c[pb][hh]])
                        tt("dve", PT[pb][:, hsl], PT[pb][:, hsl], bt[:, 2 * i + hh, :, :].rearrange("p j n -> p (j n)"), ALU.mult,
                           [PT_c[pb][hh], bt_c], [PT_c[pb][hh]])
                    return pb

                def PV_stage(i, pb):
                    for hh in range(2):
                        h = 2 * i + hh
                        for j in range(5):
                            e = mc + j
                            blk, ci = e // 4, e % 4
                            slot = blk % 3
                            mm(Ov[:, h, 0:65], PT[pb][:, hh * 640 + j * 128: hh * 640 + (j + 1) * 128], Vr[slot][:, ci, h, :], j == 0, j == 4,
                               [PT_c[pb][hh], V_c[slot]], [Oc[h // 4]])

                Sq = [S_stage(0), S_stage(1)]
                for i in range(4):
                    pb = E_stage(i, *Sq.pop(0))
                    PV_stage(i, pb)
                    if i + 2 < 4:
                        Sq.append(S_stage(i + 2))
                S.add("dve", lambda q, Ov=Ov: q.reciprocal(out=rec[:], in_=Ov[:, :, 64:65]), Oc, [rec_c])
                tt("dve", brt[:, 0:512].rearrange("p (h d) -> p h d", d=64), Ov[:, :, 0:64], rec[:].to_broadcast([128, 8, 64]),
                   ALU.mult, Oc + [rec_c], [br_c[bb][0]])
                sgu_back(c, brt, bb)
                ck("m_att")
                pr, prc = next_pair()
                for k in range(8):
                    tr(pr[:, k * 128:(k + 1) * 128], brt[:, k * 128:(k + 1) * 128], ident[:], [br_c[bb][k // 4], ident_c], [prc[k // 4]])
                ck("m_trp")
                for k in range(8):
                    evac_copy(pool32[:, BRT0 + k, c * 128:(c + 1) * 128], pr[:, k * 128:(k + 1) * 128], [prc[k // 4]], [p32_c[BRT0 + k]])
                ck("m_c%d" % c)
            kvbufs = load_x4(s, (t + 2) * 512) if t + 2 <= 8 else None
            ck("m_tr")
            for half in range(2):
                au, su, auc = load_pair(w_au_v[:, :, half * 512:(half + 1) * 512], w_su_v[:, :, half * 512:(half + 1) * 512], 4, 512)
                suc = auc
                ga, gac = load_slab(w_in_v[:, :, 2560 + half * 512: 2560 + (half + 1) * 512], 8, 512)
                gb, gbc = load_slab(w_in_v[:, :, 3584 + half * 512: 3584 + (half + 1) * 512], 8, 512)
                for m in range(4):
                    oA, oAc = next_bank()
                    for k in range(4):
                        mm(oA, au[:, k, m * 128:(m + 1) * 128], pool32[:, BRT0 + k, :], k == 0, k == 3, [auc, p32_c[BRT0 + k]], [oAc])
                    oB, oBc = next_bank()
                    for k in range(4):
                        mm(oB, su[:, k, m * 128:(m + 1) * 128], pool32[:, BRT0 + 4 + k, :], k == 0, k == 3, [suc, p32_c[BRT0 + 4 + k]], [oBc])
                    oG, oGc = next_bank()
                    for k in range(8):
                        mm(oG, ga[:, k, m * 128:(m + 1) * 128], hT[:, k, :], k == 0, k == 7, [gac, hT_c[k]], [oGc])
                    oH, oHc = next_bank()
                    for k in range(8):
                        mm(oH, gb[:, k, m * 128:(m + 1) * 128], hT[:, k, :], k == 0, k == 7, [gbc, hT_c[k]], [oHc])
                    tail_step()
                    act(sga[:], oG, AF.Sigmoid, [oGc], [sga_c])
                    act(sgb[:], oH, AF.Sigmoid, [oHc], [sgb_c])
                    tt("dve", sga[:], oA, sga[:], ALU.mult, [oAc, sga_c], [sga_c])
                    tt("dve", sgb[:], oB, sgb[:], ALU.mult, [oBc, sgb_c], [sgb_c])
                    tt("dve", pool32[:, MT0 + half * 4 + m, :], sga[:], sgb[:], ALU.add, [sga_c, sgb_c], [p32_c[MT0 + half * 4 + m]])
                    tail_step()
            ck("m_merge")
            flush_all()
            if t + 2 <= 8:
                kv_front(s, t + 2, kvbufs)
            ck("m_merge")
            wo = [load_slab(w_o_v[:, :, half * 512:(half + 1) * 512], 8, 512) for half in range(2)]
            xr = [next_xs() for _ in range(4)]
            ln1_slots = []
            for c in range(4):
                tok = 256 + t * 512 + c * 128
                dma("sp", xs[xr[c]][:], xs_d[s, tok:tok + 128, :], [], [xs_c[xr[c]]])
            for c in range(4):
                xb = xr[c]
                for half in range(2):
                    o, oc = next_bank()
                    for k in range(8):
                        mm(o, pool32[:, MT0 + k, c * 128:(c + 1) * 128], wo[half][0][:, k, :], k == 0, k == 7, [p32_c[MT0 + k], wo[half][1]], [oc])
                    hs = slice(half * 512, (half + 1) * 512)
                    tt("dve", x1[:, c, hs], o, g1bc[:, hs], ALU.mult, [oc, g1bc_c], [x1_c[c]])
                tt("dve", x1[:, c, :], x1[:, c, :], xs[xb][:], ALU.add, [xs_c[xb], x1_c[c]], [x1_c[c]])
                lnb_stats(c, x1[:, c, :], [x1_c[c]], LN_EPS / (ALPHA * ALPHA))
            lnb_rsqrt()
            for c in range(4):
                lnb_norm(c, x1[:, c, :], x1[:, c, :], [x1_c[c]])
            if t + 2 <= 8:
                kv_back(s, t + 2)
            ck("m_h2")
            transpose_affine([x1[:, c, :] for c in range(4)], [x1_c[c] for c in range(4)], 4, 5)
            for c in range(4):
                tt("dve", x1[:, c, :], x1[:, c, :], ln1g_bc[:], ALU.mult, [x1_c[c], ln1g_bc_c], [x1_c[c]])
                tt("dve", x1[:, c, :], x1[:, c, :], b1bc[:], ALU.add, [x1_c[c], b1bc_c], [x1_c[c]])
            ck("m_wo")
            for sl in range(8):
                slab, sc = load_slab(w_ff1_v[:, :, sl * 512:(sl + 1) * 512], 8, 512)
                for m in range(4):
                    mi = sl * 4 + m
                    o, oc = next_bank()
                    for k in range(8):
                        mm(o, slab[:, k, m * 128:(m + 1) * 128], hT[:, k, :], k == 0, k == 7, [sc, hT_c[k]], [oc])
                    act(rr[:], o, AF.Relu, [oc, b1col_c], [rr_c], bias=b1col[:, mi:mi + 1])
                    tt("dve", pool32[:, mi, :], rr[:], rr[:], ALU.mult, [rr_c], [p32_c[mi]])
            ck("m_ff1")
            for sl in range(8):
                slab, sc = load_slab(w_ff2_v[:, 4 * sl:4 * sl + 4, :], 4, 1024)
                for c in range(4):
                    for half in range(2):
                        b = c * 2 + half
                        o = PS[b // 2][:, (b % 2) * 512:(b % 2) * 512 + 512]
                        for kk in range(4):
                            mm(o, pool32[:, 4 * sl + kk, c * 128:(c + 1) * 128], slab[:, kk, half * 512:(half + 1) * 512],
                               sl == 0 and kk == 0, sl == 7 and kk == 3, [p32_c[4 * sl + kk], sc], [ps_c[b]])
            state["bank"] = 0
            state["pair"] = 0
            ck("m_ff2")
            tmps = [(gv, gv_c), (sga, sga_c), (sgb, sgb_c), (rr, rr_c)]
            bank = lambda b: PS[b // 2][:, (b % 2) * 512:(b % 2) * 512 + 512]
            for b in range(4):
                hs = slice((b % 2) * 512, (b % 2 + 1) * 512)
                tt("dve", tmps[b][0][:], bank(b), g2bc[:, hs], ALU.mult, [ps_c[b], g2bc_c], [tmps[b][1]])
            for b in range(4, 8):
                i = b - 4
                act(branch[i // 2][:, (i % 2) * 512:(i % 2 + 1) * 512], bank(b), AF.Copy, [ps_c[b]], [br_c[i // 2][i % 2]])
            for b in range(4):
                c, hs = b // 2, slice((b % 2) * 512, (b % 2 + 1) * 512)
                tt("dve", x1[:, c, hs], x1[:, c, hs], tmps[b][0][:], ALU.add, [x1_c[c], tmps[b][1]], [x1_c[c]])
            for b in range(4, 8):
                i = b - 4
                c, hs = b // 2, slice((b % 2) * 512, (b % 2 + 1) * 512)
                src = branch[i // 2][:, (i % 2) * 512:(i % 2 + 1) * 512]
                tt("dve", src, src, g2bc[:, hs], ALU.mult, [br_c[i // 2][i % 2], g2bc_c], [br_c[i // 2][i % 2]])
                tt("dve", x1[:, c, hs], x1[:, c, hs], src, ALU.add, [x1_c[c], br_c[i // 2][i % 2]], [x1_c[c]])

            def ln2_tails(s=s, t=t):
                for c in range(4):
                    lnb_stats(c, x1[:, c, :], [x1_c[c]], LN_EPS / (ALPHA * ALPHA))
                    yield
                lnb_rsqrt()
                yield
                for c in range(4):
                    lnb_norm(c, x1[:, c, :], x1[:, c, :], [x1_c[c]])
                    yield
                    tt("dve", x1[:, c, :], x1[:, c, :], ln2g_bc[:], ALU.mult, [x1_c[c], ln2g_bc_c], [x1_c[c]])
                    tt("dve", x1[:, c, :], x1[:, c, :], ln2b_bc[:], ALU.add, [x1_c[c], ln2b_bc_c], [x1_c[c]])
                    tok = t * 512 + c * 128
                    out_ops.append(dma("sp", y_d[s, tok:tok + 128, :], x1[:, c, :], [x1_c[c]], []))
                    yield

            pending.append(ln2_tails())

        pending = []

        def tail_step():
            while pending:
                try:
                    next(pending[0])
                    return
                except StopIteration:
                    pending.pop(0)

        def flush_all():
            while pending:
                tail_step()

        try:
          ck("setup")
          for s in range(nseg):
            segment_setup(s)
            ck("segsetup")
            kv_block(s, 0)
            ck("kv0")
            kv_block(s, 1)
            for t in range(ntiles):
                main_tile(s, t)
          flush_all()
        except _Stop:
            pass
        if _os.environ.get("KTAGS"):
            import json as _json
            _json.dump({e: [o.tag for o in S.ops[e] if not o.is_dma] for e in S.ops}, open(_os.environ["KTAGS"], "w"))
        S.emit(nc, final_waits=out_ops)
    return nc


def _ext_rows(seq_rows, row0):
    er = np.full((36, 2), -1, np.int64)
    src_chunk = np.zeros(36, np.int64)
    c0 = row0 // 2
    nchunk = seq_rows // 2
    for mc in range(32):
        er[2 + mc] = (row0 + 2 * mc, row0 + 2 * mc + 1)
        src_chunk[2 + mc] = c0 + mc
    if row0 > 0:
        for e in range(2):
            g = c0 - 2 + e
            er[e] = (2 * g, 2 * g + 1)
            src_chunk[e] = g
    else:
        er[1] = (6, 7)
        src_chunk[1] = 3
        src_chunk[0] = 0
    if row0 + 64 < seq_rows:
        for e in range(2):
            g = c0 + 32 + e
            er[34 + e] = (2 * g, 2 * g + 1)
            src_chunk[34 + e] = g
    else:
        er[34] = (seq_rows - 8, seq_rows - 7)
        src_chunk[34] = nchunk - 4
        src_chunk[35] = nchunk - 1
    return er, src_chunk


def _build_btab(rpb, seq_rows, row0):
    er, _ = _ext_rows(seq_rows, row0)
    out = np.full((5, 128, 8, 5, 128), NEG, np.float32)
    q = np.arange(128)
    qro, qc = q // 64, q % 64
    cs = np.clip(qc - 8, 0, 48)
    kcol = np.arange(64)
    colvalid = (kcol[None, :] >= cs[:, None]) & (kcol[None, :] < cs[:, None] + 16)
    dc = np.clip(kcol[None, :] - qc[:, None] + 15, 0, 30)
    for ti, mc in enumerate((0, 1, 2, 30, 31)):
        r = row0 + 2 * mc + qro
        rs = np.clip(r - 4, 0, seq_rows - 8)
        covered = np.zeros((128, seq_rows), bool)
        for j in range(5):
            e = mc + j
            for half in range(2):
                krow = int(er[e, half])
                if krow < 0:
                    continue
                rowvalid = (krow >= rs) & (krow < rs + 8) & (~covered[:, krow])
                covered[:, krow] |= rowvalid
                valid = rowvalid[:, None] & colvalid
                dr = np.clip(krow - r + 7, 0, 14)
                vals = rpb[:, dr[:, None], dc]
                vals = np.transpose(vals, (2, 0, 1))
                out[ti, half * 64:(half + 1) * 64, :, j, :] = np.where(valid.T[:, None, :], vals, np.float32(NEG))
    return out.reshape(5, 128, 5120)


def _ext_tokens(xseq, seq_rows, row0):
    _, src = _ext_rows(seq_rows, row0)
    xc = xseq.reshape(seq_rows // 2, 128, D)
    return xc[src].reshape(EXT_TOK, D)


_NC_CACHE = {}


def kernel(x_prompt, x_sample, c_prompt, c_sample, w_ada, b_ada, w_in, rpb, sgu_ln_g, sgu_ln_b,
           w_s, b_s, w_attn_up, w_sgu_up, w_o, ln1_g, ln1_b, w_ff1, b_ff1, w_ff2, b_ff2, ln2_g, ln2_b):
    f = lambda a: np.ascontiguousarray(np.asarray(a, dtype=np.float32))
    x_prompt, x_sample, c_prompt, c_sample = f(x_prompt), f(x_sample), f(c_prompt), f(c_sample)
    rpb0 = f(rpb)[0]
    shared = {
        "w_ada": f(w_ada)[0], "b_ada": f(b_ada)[0].reshape(1, -1), "w_in": f(w_in)[0],
        "sgu_ln_g": f(sgu_ln_g)[0].reshape(1, -1), "sgu_ln_b": f(sgu_ln_b)[0].reshape(1, -1),
        "w_s": f(w_s)[0], "b_s": f(b_s)[0], "w_attn_up": f(w_attn_up)[0], "w_sgu_up": f(w_sgu_up)[0],
        "w_o": f(w_o)[0], "ln1_g": f(ln1_g)[0].reshape(1, -1), "ln1_b": f(ln1_b)[0].reshape(1, -1),
        "w_ff1": f(w_ff1)[0], "b_ff1": f(b_ff1)[0].reshape(32, 128), "w_ff2": f(w_ff2)[0],
        "b_ff2": f(b_ff2)[0].reshape(1, -1), "ln2_g": f(ln2_g)[0].reshape(1, -1), "ln2_b": f(ln2_b)[0].reshape(1, -1),
    }
    bt_sample = _build_btab(rpb0, 64, 0)
    bt_prompt = [_build_btab(rpb0, 256, 64 * qi) for qi in range(4)]
    in_maps = []
    for i in range(NCORES):
        pi, qi = i // 4, i % 4
        xs = np.stack([_ext_tokens(x_sample[i], 64, 0), _ext_tokens(x_prompt[pi], 256, 64 * qi)])
        cc = np.stack([c_sample[i], c_prompt[pi]]).reshape(2, 8, 128)
        m = dict(shared)
        m["xs"] = np.ascontiguousarray(xs)
        m["cc"] = np.ascontiguousarray(cc)
        m["btab"] = np.ascontiguousarray(np.stack([bt_sample, bt_prompt[qi]]))
        in_maps.append(m)
    if "nc" not in _NC_CACHE:
        _NC_CACHE["nc"] = build_program()
    res = run_bass_kernel_spmd(_NC_CACHE["nc"], in_maps, core_ids=list(range(NCORES)))
    y_prompt = np.empty((2, 16384, D), np.float32)
    y_sample = np.empty((8, 4096, D), np.float32)
    for i in range(NCORES):
        y = np.asarray(res.results[i]["y"], dtype=np.float32)
        y_sample[i] = y[0]
        y_prompt[i // 4, (i % 4) * 4096:(i % 4 + 1) * 4096] = y[1]
    return (y_prompt, y_sample)
```

```python
import numpy as np
from contextlib import ExitStack
import concourse.bass as bass
import concourse.mybir as mybir
from concourse.bass_utils import run_bass_kernel_spmd

F32 = mybir.dt.float32
BF16 = mybir.dt.bfloat16
AF = mybir.ActivationFunctionType
ALU = mybir.AluOpType

D = 1024
ALPHA = 2.0 ** 0.25
LN_EPS = 1e-5
NEG = -30000.0
NCORES = 8
SEG_TOK = 4096
EXT_TOK = 4608
NSLOT = 6
NDMASEM = 8
COMPUTE = ("pe", "act", "dve", "pool")


class Op:
    __slots__ = ("eng", "fn", "deps", "is_dma", "signal", "sigidx", "dsem", "dval", "prev_on_sem", "tag")

    def __init__(self, eng, fn, is_dma):
        self.eng = eng
        self.fn = fn
        self.deps = []
        self.is_dma = is_dma
        self.signal = False
        self.sigidx = 0
        self.dsem = None
        self.dval = 0
        self.prev_on_sem = None


class Cell:
    __slots__ = ("w", "r", "excl")

    def __init__(self, excl=False):
        self.w = None
        self.r = []
        self.excl = excl


class Sched:
    def __init__(self):
        self.ops = {e: [] for e in ("pe", "act", "dve", "pool", "sp")}
        self.dma_count = {e: 0 for e in ("act", "pool", "sp")}
        self.dma_last = {}

    def add(self, eng, fn, reads=(), writes=(), dma=False):
        op = Op(eng, fn, dma)
        op.tag = getattr(self, "stage", "")
        deps = []
        if any(c.excl for c in reads):
            writes = list(writes) + [c for c in reads if c.excl]
            reads = [c for c in reads if not c.excl]
        for c in reads:
            if c.w is not None:
                deps.append(c.w)
        for c in writes:
            if c.w is not None:
                deps.append(c.w)
            deps.extend(c.r)
        seen = set()
        out = []
        for d in deps:
            if id(d) in seen:
                continue
            seen.add(id(d))
            if (not d.is_dma) and (not dma) and d.eng == "pe" and eng == "pe":
                continue
            out.append(d)
        op.deps = out
        for c in reads:
            if not dma:
                c.r = [o for o in c.r if o.is_dma or o.eng != eng]
            c.r.append(op)
        for c in writes:
            c.w = op
            c.r = []
        self.ops[eng].append(op)
        if dma:
            k = self.dma_count[eng]
            self.dma_count[eng] = k + 1
            key = (eng, k % NDMASEM)
            op.prev_on_sem = self.dma_last.get(key)
            op.dsem = key
            op.dval = 16 * (k // NDMASEM + 1)
            self.dma_last[key] = op
            op.signal = True
        for d in out:
            d.signal = True
        return op

    def emit(self, nc, final_waits=()):
        with ExitStack() as es:
            csem = {e: es.enter_context(nc.semaphore(f"c_{e}")) for e in COMPUTE}
            dsem = {}
            for e in ("act", "pool", "sp"):
                for j in range(NDMASEM):
                    if self.dma_count[e] > j:
                        dsem[(e, j)] = es.enter_context(nc.semaphore(f"d_{e}{j}"))
            for e in COMPUTE:
                n = 0
                for op in self.ops[e]:
                    if not op.is_dma and op.signal:
                        n += 1
                        op.sigidx = n
            block = es.enter_context(nc.Block())

            def run(ename):
                def body(eng):
                    known = {}

                    def wait_for(d):
                        if d.is_dma:
                            key, val, sem = d.dsem, d.dval, dsem[d.dsem]
                        else:
                            key, val, sem = d.eng, d.sigidx, csem[d.eng]
                        if known.get(key, 0) >= val:
                            return
                        eng.wait_ge(sem, val)
                        known[key] = val

                    for op in self.ops[ename]:
                        for d in op.deps:
                            wait_for(d)
                        if op.is_dma and op.prev_on_sem is not None:
                            wait_for(op.prev_on_sem)
                        ins = op.fn(eng)
                        if op.is_dma:
                            ins.then_inc(dsem[op.dsem], 16)
                        elif op.signal:
                            ins.then_inc(csem[ename], 1)
                    if ename == "sp":
                        for d in final_waits:
                            wait_for(d)

                return body

            block.tensor(run("pe"))
            block.scalar(run("act"))
            block.vector(run("dve"))
            block.gpsimd(run("pool"))
            block.sync(run("sp"))


def build_program(nseg=2, ntiles=8):
    nc = bass.Bass("TRN2", target_bir_lowering=False)
    dt_in = lambda name, shape: nc.dram_tensor(name, shape, F32, kind="ExternalInput").ap()
    xs_d = dt_in("xs", [2, EXT_TOK, D])
    cc_d = dt_in("cc", [2, 8, 128])
    bt_d = dt_in("btab", [2, 5, 128, 5120])
    w_ada_d = dt_in("w_ada", [D, 6 * D])
    b_ada_d = dt_in("b_ada", [1, 6 * D])
    w_in_d = dt_in("w_in", [D, 4608])
    sgu_g_d = dt_in("sgu_ln_g", [1, 512])
    sgu_b_d = dt_in("sgu_ln_b", [1, 512])
    w_s_d = dt_in("w_s", [8, 128, 128])
    b_s_d = dt_in("b_s", [8, 128])
    w_au_d = dt_in("w_attn_up", [512, D])
    w_su_d = dt_in("w_sgu_up", [512, D])
    w_o_d = dt_in("w_o", [D, D])
    ln1_g_d = dt_in("ln1_g", [1, D])
    ln1_b_d = dt_in("ln1_b", [1, D])
    w_ff1_d = dt_in("w_ff1", [D, 4096])
    b_ff1_d = dt_in("b_ff1", [32, 128])
    w_ff2_d = dt_in("w_ff2", [4096, D])
    b_ff2_d = dt_in("b_ff2", [1, D])
    ln2_g_d = dt_in("ln2_g", [1, D])
    ln2_b_d = dt_in("ln2_b", [1, D])
    y_d = nc.dram_tensor("y", [2, SEG_TOK, D], F32, kind="ExternalOutput").ap()

    kp = lambda ap: ap.rearrange("(k p) n -> p k n", p=128)
    w_ada_v, w_in_v, w_au_v, w_su_v = kp(w_ada_d), kp(w_in_d), kp(w_au_d), kp(w_su_d)
    w_o_v, w_ff1_v, w_ff2_v = kp(w_o_d), kp(w_ff1_d), kp(w_ff2_d)

    S = Sched()
    C = Cell
    out_ops = []
    with ExitStack() as es:
        sb = lambda name, shape, dt: es.enter_context(nc.sbuf_tensor(name, shape, dt))
        wring = [sb(f"wring{i}", [128, 4096], BF16) for i in range(NSLOT)]
        wring_c = [C() for _ in range(NSLOT)]
        kT = [sb(f"kT{i}", [128, 4, 512], BF16) for i in range(3)]
        kT_c = [C() for _ in range(3)]
        Vr = [sb(f"V{i}", [128, 4, 8, 65], BF16) for i in range(3)]
        V_c = [C() for _ in range(3)]
        xs = [sb(f"xs{i}", [128, D], F32) for i in range(4)]
        xs_c = [C() for _ in range(4)]
        hT = sb("hT", [128, 8, 512], BF16)
        hT_c = [C() for _ in range(8)]
        pool32 = sb("pool32", [128, 32, 512], BF16)
        p32_c = [C() for _ in range(32)]
        gv = sb("gv", [128, 512], F32); gv_c = C()
        branch = [sb(f"branch{i}", [128, D], F32) for i in range(2)]
        br_c = [[C(), C()] for _ in range(2)]
        sga = sb("sga", [128, 512], F32); sga_c = C()
        sgb = sb("sgb", [128, 512], F32); sgb_c = C()
        x1 = sb("x1", [128, 4, D], F32); x1_c = [C() for _ in range(4)]
        rr = sb("rr", [128, 512], F32); rr_c = C()
        bt = sb("bt", [128, 8, 5, 128], BF16); bt_c = C()
        g1bc = sb("g1bc", [128, D], F32); g1bc_c = C()
        b1bc = sb("b1bc", [128, D], F32); b1bc_c = C()
        g2bc = sb("g2bc", [128, D], F32); g2bc_c = C()
        ln1g_bc = sb("ln1g_bc", [128, D], F32); ln1g_bc_c = C()
        ln2g_bc = sb("ln2g_bc", [128, D], F32); ln2g_bc_c = C()
        ln2b_bc = sb("ln2b_bc", [128, D], F32); ln2b_bc_c = C()
        PT = [sb(f"PT{i}", [128, 1280], BF16) for i in range(2)]
        PT_c = [[C(), C()] for _ in range(2)]
        WsT = sb("WsT", [128, 8, 128], BF16); WsT_c = C()
        ident = sb("ident", [128, 128], F32); ident_c = C()
        sgug_bc = sb("sgug_bc", [128, 512], F32); sgug_c = C()
        sgub_bc = sb("sgub_bc", [128, 512], F32); sgub_c = C()
        bsT = sb("bsT", [128, 8], F32); bsT_c = C()
        b1col = sb("b1col", [128, 32], F32); b1col_c = C()
        cols = sb("cols", [128, 6, 8], F32); cols_c = C()
        lncol = sb("lncol", [128, 2, 8], F32); lncol_c = C()
        rows8 = sb("rows8", [32, 128], F32); rows8_c = C()
        siluc = sb("siluc", [128, 8], F32); siluc_c = C()
        silubc = PT[0][:, 0:1024].rearrange("p (k n) -> p k n", n=128)
        stt = sb("stt", [128, 4, 2, 6], F32); stt_c = [C() for _ in range(4)]
        mv = sb("mv", [128, 4, 2], F32); mv_c = [C() for _ in range(4)]
        rstd = sb("rstd", [128, 4], F32); rstd_c = [C() for _ in range(4)]
        nmr = sb("nmr", [128, 4], F32); nmr_c = [C() for _ in range(4)]
        rec = sb("rec", [128, 8, 1], F32); rec_c = C()
        PSB = es.enter_context(nc.psum_tensor("psb", [128, 4096], F32))
        PS = [PSB[:, i * 1024:(i + 1) * 1024] for i in range(4)]
        ps_c = [C(True) for _ in range(8)]
        state = {"st": 0, "spair": 0, "bank": 0, "pair": 0, "slot": 0, "xsb": 0, "brb": 0, "ptb": 0, "alt": 0}

        def next_bank():
            b = state["bank"]
            state["bank"] = (b + 1) % 8
            return PS[b // 2][:, (b % 2) * 512:(b % 2) * 512 + 512], ps_c[b]

        def next_pair():
            p = state["pair"]
            state["pair"] = (p + 1) % 4
            return PS[p], [ps_c[2 * p], ps_c[2 * p + 1]]

        def mm(out, lhsT, rhs, start, stop, reads, writes):
            S.add("pe", lambda q: q.matmul(out, lhsT=lhsT, rhs=rhs, start=start, stop=stop), reads, writes)

        def tr(out, in_, idn, reads, writes):
            S.add("pe", lambda q: q.transpose(out=out, in_=in_, identity=idn), reads, writes)

        def act(out, in_, func, reads, writes, scale=1.0, bias=0.0):
            S.add("act", lambda q: q.activation(out=out, in_=in_, func=func, bias=bias, scale=scale), reads, writes)

        def tt(eng, out, in0, in1, op, reads, writes):
            S.add(eng, lambda q: q.tensor_tensor(out=out, in0=in0, in1=in1, op=op), reads, writes)

        def ts(eng, out, in0, s1, s2, op0, op1, reads, writes):
            S.add(eng, lambda q: q.tensor_scalar(out=out, in0=in0, scalar1=s1, scalar2=s2, op0=op0, op1=op1), reads, writes)

        def stt_op(eng, out, in0, scalar, in1, op0, op1, reads, writes):
            S.add(eng, lambda q: q.scalar_tensor_tensor(out=out, in0=in0, scalar=scalar, in1=in1, op0=op0, op1=op1), reads, writes)

        def cp(eng, out, in_, reads, writes):
            S.add(eng, lambda q: q.tensor_copy(out=out, in_=in_), reads, writes)

        def dma(eng, out, in_, reads, writes):
            return S.add(eng, lambda q: q.dma_start(out=out, in_=in_), reads, writes, dma=True)

        def load_slab(src_ap, nk, ncol):
            i = state["slot"]
            state["slot"] = (i + 1) % NSLOT
            view = wring[i][:, 0:nk * ncol].rearrange("p (k n) -> p k n", k=nk)
            dma("pool", view, src_ap, [], [wring_c[i]])
            return view, wring_c[i]

        def load_pair(srcA, srcB, nk, ncol):
            i = state["slot"]
            state["slot"] = (i + 1) % NSLOT
            half = nk * ncol
            vA = wring[i][:, 0:half].rearrange("p (k n) -> p k n", k=nk)
            vB = wring[i][:, half:2 * half].rearrange("p (k n) -> p k n", k=nk)
            dma("pool", vA, srcA, [], [wring_c[i]])
            dma("pool", vB, srcB, [], [wring_c[i]])
            return vA, vB, wring_c[i]

        def alt_eng():
            import os
            f = os.environ.get("KEVAC", "")
            if f:
                return f
            state["alt"] ^= 1
            return "act" if state["alt"] else "dve"

        def evac_affine(out, in_, scol, bcol, reads, writes):
            import os
            if os.environ.get("KAFF", "dve") == "dve" and alt_eng() == "dve":
                ts("dve", out, in_, scol, bcol, ALU.mult, ALU.add, reads, writes)
            else:
                act(out, in_, AF.Identity, reads, writes, scale=scol, bias=bcol)

        def evac_copy(out, in_, reads, writes, scale=None):
            if alt_eng() == "act":
                if scale is None:
                    act(out, in_, AF.Copy, reads, writes)
                else:
                    act(out, in_, AF.Copy, reads, writes, scale=scale)
            else:
                if scale is None:
                    cp("dve", out, in_, reads, writes)
                else:
                    S.add("dve", lambda q: q.tensor_scalar_mul(out=out, in0=in_, scalar1=scale), reads, writes)

        def next_xs():
            i = state["xsb"]
            state["xsb"] = (i + 1) % 4
            return i

        def ln_phase_a(src, src_cells, eps, nhalf=2):
            i = state["st"]
            state["st"] = (i + 1) % 4
            for h in range(nhalf):
                S.add("dve", lambda q, h=h: q.bn_stats(out=stt[:, i, h, :], in_=src[:, h * 512:(h + 1) * 512]), src_cells, [stt_c[i]])
            S.add("dve", lambda q: q.bn_aggr(out=mv[:, i, :], in_=stt[:, i, 0:nhalf, :].rearrange("p a b -> p (a b)")), [stt_c[i]], [mv_c[i]])
            S.add("dve", lambda q: q.tensor_scalar_add(out=rstd[:, i:i + 1], in0=mv[:, i, 1:2], scalar1=eps), [mv_c[i]], [rstd_c[i]])
            act(rstd[:, i:i + 1], rstd[:, i:i + 1], AF.Sqrt, [rstd_c[i]], [rstd_c[i]])
            return i

        def ln_phase_b(i, dst, src, cells):
            S.add("dve", lambda q: q.reciprocal(out=rstd[:, i:i + 1], in_=rstd[:, i:i + 1]), [rstd_c[i]], [rstd_c[i]])
            stt_op("dve", nmr[:, i:i + 1], mv[:, i, 0:1], -1.0, rstd[:, i:i + 1], ALU.mult, ALU.mult, [mv_c[i], rstd_c[i]], [nmr_c[i]])
            act(dst, src, AF.Identity, cells + [rstd_c[i], nmr_c[i]], cells, scale=rstd[:, i:i + 1], bias=nmr[:, i:i + 1])

        def lnb_stats(c, src, src_cells, eps, nhalf=2):
            for h in range(nhalf):
                S.add("dve", lambda q, h=h: q.bn_stats(out=stt[:, c, h, :], in_=src[:, h * 512:(h + 1) * 512]), src_cells, [stt_c[c]])
            S.add("dve", lambda q: q.bn_aggr(out=mv[:, c, :], in_=stt[:, c, 0:nhalf, :].rearrange("p a b -> p (a b)")), [stt_c[c]], [mv_c[c]])
            S.add("dve", lambda q: q.tensor_scalar_add(out=rstd[:, c:c + 1], in0=mv[:, c, 1:2], scalar1=eps), [mv_c[c]], [rstd_c[c]])

        def lnb_rsqrt():
            act(rstd[:, 0:4], rstd[:, 0:4], AF.Sqrt, rstd_c, rstd_c)
            S.add("dve", lambda q: q.reciprocal(out=rstd[:, 0:4], in_=rstd[:, 0:4]), rstd_c, rstd_c)
            stt_op("dve", nmr[:, 0:4], mv[:, :, 0], -1.0, rstd[:, 0:4], ALU.mult, ALU.mult, mv_c + rstd_c, nmr_c)

        def lnb_norm(c, dst, src, cells):
            act(dst, src, AF.Identity, cells + [rstd_c[c], nmr_c[c]], cells, scale=rstd[:, c:c + 1], bias=nmr[:, c:c + 1])

        def layer_norm_stats(src, src_cells, eps, nhalf=2):
            i = state["st"]
            state["st"] = (i + 1) % 4
            for h in range(nhalf):
                S.add("dve", lambda q, h=h: q.bn_stats(out=stt[:, i, h, :], in_=src[:, h * 512:(h + 1) * 512]), src_cells, [stt_c[i]])
            S.add("dve", lambda q: q.bn_aggr(out=mv[:, i, :], in_=stt[:, i, 0:nhalf, :].rearrange("p a b -> p (a b)")), [stt_c[i]], [mv_c[i]])
            S.add("dve", lambda q: q.tensor_scalar_add(out=rstd[:, i:i + 1], in0=mv[:, i, 1:2], scalar1=eps), [mv_c[i]], [rstd_c[i]])
            act(rstd[:, i:i + 1], rstd[:, i:i + 1], AF.Sqrt, [rstd_c[i]], [rstd_c[i]])
            S.add("dve", lambda q: q.reciprocal(out=rstd[:, i:i + 1], in_=rstd[:, i:i + 1]), [rstd_c[i]], [rstd_c[i]])
            stt_op("dve", nmr[:, i:i + 1], mv[:, i, 0:1], -1.0, rstd[:, i:i + 1], ALU.mult, ALU.mult, [mv_c[i], rstd_c[i]], [nmr_c[i]])
            return i

        S.add("pool", lambda q: q.memset(ident[:], 0.0), [], [ident_c])
        S.add("pool", lambda q: q.affine_select(out=ident[:], in_=ident[:], compare_op=ALU.not_equal, fill=1.0, base=0,
                                                pattern=[[-1, 128]], channel_multiplier=1), [ident_c], [ident_c])
        for i in range(3):
            S.add("pool", lambda q, i=i: q.memset(Vr[i][:], 1.0), [], [V_c[i]])
        dma("sp", ln1g_bc[:], ln1_g_d.partition_broadcast(128), [], [ln1g_bc_c])
        dma("sp", ln2g_bc[:], ln2_g_d.partition_broadcast(128), [], [ln2g_bc_c])
        dma("sp", ln2b_bc[:], ln2_b_d.partition_broadcast(128), [], [ln2b_bc_c])
        dma("sp", sgug_bc[:], sgu_g_d.partition_broadcast(128), [], [sgug_c])
        dma("sp", sgub_bc[:], sgu_b_d.partition_broadcast(128), [], [sgub_c])
        dma("sp", rows8[:, :], b_ff1_d, [], [rows8_c])
        o, oc = next_bank()
        tr(o[:, 0:32], rows8[0:32, :], ident[0:32, 0:32], [rows8_c, ident_c], [oc])
        cp("dve", b1col[:], o[:, 0:32], [oc], [b1col_c])
        dma("sp", rows8[0:8, :], b_s_d, [b1col_c], [rows8_c])
        o, oc = next_bank()
        tr(o[:, 0:8], rows8[0:8, :], ident[0:8, 0:8], [rows8_c, ident_c], [oc])
        cp("dve", bsT[:], o[:, 0:8], [oc], [bsT_c])
        dma("sp", rows8[0:8, :], ln1_g_d.rearrange("o (k p) -> (o k) p", p=128), [bsT_c], [rows8_c])
        dma("sp", rows8[8:16, :], ln1_b_d.rearrange("o (k p) -> (o k) p", p=128), [bsT_c], [rows8_c])
        o, oc = next_bank()
        tr(o[:, 0:16], rows8[0:16, :], ident[0:16, 0:16], [rows8_c, ident_c], [oc])
        cp("dve", lncol[:].rearrange("p a b -> p (a b)"), o[:, 0:16], [oc], [lncol_c])
        for g in range(8):
            dma("sp", xs[0][:, g * 128:(g + 1) * 128], w_s_d[g], [], [xs_c[0]])
        pr, prc = next_pair()
        for g in range(8):
            tr(pr[:, g * 128:(g + 1) * 128], xs[0][:, g * 128:(g + 1) * 128], ident[:], [xs_c[0], ident_c], [prc[g // 4]])
        cp("dve", WsT[:].rearrange("p g n -> p (g n)"), pr[:, :], prc, [WsT_c])

        def segment_setup(s):
            dma("sp", rows8[0:8, :], cc_d[s], [lncol_c, cols_c], [rows8_c])
            o, oc = next_bank()
            tr(o[:, 0:8], rows8[0:8, :], ident[0:8, 0:8], [rows8_c, ident_c], [oc])
            act(siluc[:], o[:, 0:8], AF.Silu, [oc], [siluc_c])
            cp("dve", silubc, siluc[:].unsqueeze(2).to_broadcast([128, 8, 128]), [siluc_c], PT_c[0])
            colmap = {0: 0, 1: 1, 3: 2, 4: 3}
            for n in range(12):
                comp, half = n // 2, n % 2
                slab, sc = load_slab(w_ada_v[:, :, n * 512:(n + 1) * 512], 8, 512)
                o, oc = next_bank()
                for k in range(8):
                    mm(o, silubc[:, k, :], slab[:, k, :], k == 0, k == 7, PT_c[0] + [sc], [oc])
                dma("sp", gv[:], b_ada_d[:, n * 512:(n + 1) * 512].partition_broadcast(128), [], [gv_c])
                if comp == 2 or comp == 5:
                    dst, dc = (g1bc, g1bc_c) if comp == 2 else (g2bc, g2bc_c)
                    tt("dve", dst[:, half * 512:(half + 1) * 512], o, gv[:], ALU.add, [oc, gv_c], [dc])
                    S.add("dve", lambda q, dst=dst, half=half: q.tensor_scalar_mul(out=dst[:, half * 512:(half + 1) * 512],
                                                                                 in0=dst[:, half * 512:(half + 1) * 512], scalar1=1.0 / ALPHA), [dc], [dc])
                else:
                    tt("dve", sga[:], o, gv[:], ALU.add, [oc, gv_c], [sga_c])
                    o2, oc2 = next_bank()
                    for j in range(4):
                        tr(o2[:, j * 128:(j + 1) * 128], sga[:, j * 128:(j + 1) * 128], ident[:], [sga_c, ident_c], [oc2])
                    ci = colmap[comp]
                    src = o2.rearrange("p (j n) -> p j n", n=128)[:, :, 0:1]
                    dstc = cols[:, ci, half * 4:(half + 1) * 4].unsqueeze(2)
                    if comp in (1, 4):
                        S.add("dve", lambda q, dstc=dstc, src=src: q.tensor_scalar_add(out=dstc, in0=src, scalar1=1.0), [oc2], [cols_c])
                    else:
                        cp("dve", dstc, src, [oc2], [cols_c])
            tt("dve", cols[:, 4, :], lncol[:, 0, :], cols[:, 3, :], ALU.mult, [lncol_c, cols_c], [cols_c])
            tt("dve", cols[:, 5, :], lncol[:, 1, :], cols[:, 3, :], ALU.mult, [lncol_c, cols_c], [cols_c])
            tt("dve", cols[:, 5, :], cols[:, 5, :], cols[:, 2, :], ALU.add, [cols_c], [cols_c])
            dma("sp", xs[0][:], b_ff2_d.partition_broadcast(128), [], [xs_c[0]])
            dma("sp", b1bc[:], ln1_b_d.partition_broadcast(128), [], [b1bc_c])
            tt("dve", xs[0][:], xs[0][:], g2bc[:], ALU.mult, [xs_c[0], g2bc_c], [xs_c[0]])
            tt("dve", b1bc[:], b1bc[:], xs[0][:], ALU.add, [xs_c[0], b1bc_c], [b1bc_c])

        import os as _os
        _stop = _os.environ.get("KSTOP", "")

        class _Stop(Exception):
            pass

        def ck(name):
            S.stage = name
            if name == _stop:
                raise _Stop()

        def transpose_affine(srcs, src_cells, scol_i, bcol_i, act_only=False):
            for half in range(2):
                banks = [next_bank() for _ in range(4)]
                for kk in range(4):
                    k = half * 4 + kk
                    for c in range(4):
                        tr(banks[kk][0][:, c * 128:(c + 1) * 128], srcs[c][:, k * 128:(k + 1) * 128], ident[:], [src_cells[c], ident_c], [banks[kk][1]])
                for kk in range(4):
                    k = half * 4 + kk
                    if act_only:
                        act(hT[:, k, :], banks[kk][0], AF.Identity, [banks[kk][1], cols_c], [hT_c[k]],
                            scale=cols[:, scol_i, k:k + 1], bias=cols[:, bcol_i, k:k + 1])
                    else:
                        evac_affine(hT[:, k, :], banks[kk][0], cols[:, scol_i, k:k + 1], cols[:, bcol_i, k:k + 1], [banks[kk][1], cols_c], [hT_c[k]])

        def load_x4(s, tok0):
            bufs = [next_xs() for _ in range(4)]
            for c in range(4):
                dma("sp", xs[bufs[c]][:], xs_d[s, tok0 + c * 128: tok0 + (c + 1) * 128, :], [], [xs_c[bufs[c]]])
            return bufs

        def make_hT(s, tok0, nchunks, bufs=None, act_only=False):
            if bufs is None:
                bufs = load_x4(s, tok0)
            transpose_affine([xs[b] for b in bufs], [xs_c[b] for b in bufs], 1, 0, act_only)

        def kv_front(s, b, bufs=None):
            ck("kv_start")
            make_hT(s, b * 512, 4, bufs, act_only=(bufs is not None))

        def kv_back(s, b):
            slot = b % 3
            ck("kv_h")
            slab, sc = load_slab(w_in_v[:, :, 512:1024], 8, 512)
            for m in range(4):
                o, oc = next_bank()
                for k in range(8):
                    mm(o, slab[:, k, m * 128:(m + 1) * 128], hT[:, k, :], k == 0, k == 7, [sc, hT_c[k]], [oc])
                act(kT[slot][:, m, :], o, AF.Copy, [oc], [kT_c[slot]])
            ck("kv_k")
            slab, sc = load_slab(w_in_v[:, :, 1024:1536], 8, 512)
            for c in range(4):
                o, oc = next_bank()
                for k in range(8):
                    mm(o, hT[:, k, c * 128:(c + 1) * 128], slab[:, k, :], k == 0, k == 7, [sc, hT_c[k]], [oc])
                act(Vr[slot][:, c, :, 0:64], o.rearrange("p (h d) -> p h d", d=64), AF.Copy, [oc], [V_c[slot]])

        def kv_block(s, b):
            kv_front(s, b)
            kv_back(s, b)

        QT0, UG0, VN0, VN1, BRT0, MT0 = 16, 20, 24, 25, 0, 8

        def main_tile(s, t):
            ck("m_start")
            make_hT(s, 256 + t * 512, 4, act_only=True)
            ck("m_h")
            vslab, vsc = load_slab(w_in_v[:, :, 2048:2560], 8, 512)
            gvb = [(gv, gv_c), (sga, sga_c), (sgb, sgb_c), (rr, rr_c)]
            for c in range(4):
                g_, g_c = gvb[c]
                o, oc = next_bank()
                for k in range(8):
                    mm(o, hT[:, k, c * 128:(c + 1) * 128], vslab[:, k, :], k == 0, k == 7, [vsc, hT_c[k]], [oc])
                act(g_[:], o, AF.Gelu_apprx_tanh, [oc], [g_c])
                lnb_stats(c, g_, [g_c], LN_EPS, nhalf=1)
            lnb_rsqrt()
            for c in range(4):
                g_, g_c = gvb[c]
                lnb_norm(c, g_[:], g_[:], [g_c])
                tt("dve", g_[:], g_[:], sgug_bc[:], ALU.mult, [g_c, sgug_c], [g_c])
                tt("dve", pool32[:, VN0 + c, :], g_[:], sgub_bc[:], ALU.add, [g_c, sgub_c], [p32_c[VN0 + c]])

            slab, sc = load_slab(w_in_v[:, :, 0:512], 8, 512)
            for m in range(4):
                o, oc = next_bank()
                for k in range(8):
                    mm(o, slab[:, k, m * 128:(m + 1) * 128], hT[:, k, :], k == 0, k == 7, [sc, hT_c[k]], [oc])
                evac_copy(pool32[:, QT0 + m, :], o, [oc], [p32_c[QT0 + m]], scale=0.125)
            ck("m_q")
            slab, sc = load_slab(w_in_v[:, :, 1536:2048], 8, 512)
            for c in range(4):
                o, oc = next_bank()
                for k in range(8):
                    mm(o, hT[:, k, c * 128:(c + 1) * 128], slab[:, k, :], k == 0, k == 7, [sc, hT_c[k]], [oc])
                act(pool32[:, UG0 + c, :], o, AF.Gelu_apprx_tanh, [oc], [p32_c[UG0 + c]])
            ck("m_u")
            def sgu_back(c, brt, bb):
                g_, g_c = gvb[c]
                vn_i = VN0 + c
                o, oc = next_bank()
                for g in range(8):
                    mm(o[:, g * 64:(g + 1) * 64], WsT[:, g, :], pool32[:, vn_i, g * 64:(g + 1) * 64], True, True, [WsT_c, p32_c[vn_i]], [oc])
                tt("dve", g_[:].rearrange("p (g d) -> p g d", d=64), o.rearrange("p (g d) -> p g d", d=64),
                   bsT[:].unsqueeze(2).to_broadcast([128, 8, 64]), ALU.add, [oc, bsT_c], [g_c])
                tt("dve", brt[:, 512:1024], g_[:], pool32[:, UG0 + c, :], ALU.mult, [g_c, p32_c[UG0 + c]], [br_c[bb][1]])

            for c in range(4):
                mc = 4 * t + c
                bb = state["brb"]; state["brb"] ^= 1
                brt = branch[bb]
                ck("c_start")
                ck("m_sgu")
                ttype = {0: 0, 1: 1, 30: 3, 31: 4}.get(mc, 2)
                if mc in (0, 1, 2, 30, 31):
                    btf = bt[:].rearrange("p h j n -> p (h j n)")
                    dma("pool", btf, bt_d[s, ttype], [], [bt_c])
                    act(btf, btf, AF.Exp, [bt_c], [bt_c])
                Opair, Oc = PS[3], [ps_c[6], ps_c[7]]
                Ov = Opair[:, :].rearrange("p (h d) -> p h d", d=128)

                def S_stage(i):
                    u = state["spair"]; state["spair"] = u ^ 1
                    base = u * 1536
                    cells = [ps_c[3 * u], ps_c[3 * u + 1], ps_c[3 * u + 2]]
                    for j in range(5):
                        e = mc + j
                        blk, ci = e // 4, e % 4
                        slot = blk % 3
                        for hh in range(2):
                            hp = hh * 64
                            off = hh * 640 + j * 128
                            mm(PSB[:, base + off: base + off + 128], kT[slot][hp:hp + 64, i, ci * 128:(ci + 1) * 128],
                               pool32[hp:hp + 64, QT0 + i, c * 128:(c + 1) * 128], True, True,
                               [kT_c[slot], p32_c[QT0 + i]], [cells[off // 512]])
                    return PSB[:, base: base + 1280], cells

                def E_stage(i, Sp, Sc):
                    pb = state["ptb"]; state["ptb"] = pb ^ 1
                    for hh in range(2):
                        hsl = slice(hh * 640, (hh + 1) * 640)
                        act(PT[pb][:, hsl], Sp[:, hsl], AF.Exp, Sc, [PT_c[pb][hh]])
                        tt("dve", PT[pb][:, hsl], PT[pb][:, hsl], bt[:, 2 * i + hh, :, :].rearrange("p j n -> p (j n)"), ALU.mult,
                           [PT_c[pb][hh], bt_c], [PT_c[pb][hh]])
                    return pb

                def PV_stage(i, pb):
                    for hh in range(2):
                        h = 2 * i + hh
                        for j in range(5):
                            e = mc + j
                            blk, ci = e // 4, e % 4
                            slot = blk % 3
                            mm(Ov[:, h, 0:65], PT[pb][:, hh * 640 + j * 128: hh * 640 + (j + 1) * 128], Vr[slot][:, ci, h, :], j == 0, j == 4,
                               [PT_c[pb][hh], V_c[slot]], [Oc[h // 4]])

                Sq = [S_stage(0), S_stage(1)]
                for i in range(4):
                    pb = E_stage(i, *Sq.pop(0))
                    PV_stage(i, pb)
                    if i + 2 < 4:
                        Sq.append(S_stage(i + 2))
                S.add("dve", lambda q, Ov=Ov: q.reciprocal(out=rec[:], in_=Ov[:, :, 64:65]), Oc, [rec_c])
                tt("dve", brt[:, 0:512].rearrange("p (h d) -> p h d", d=64), Ov[:, :, 0:64], rec[:].to_broadcast([128, 8, 64]),
                   ALU.mult, Oc + [rec_c], [br_c[bb][0]])
                sgu_back(c, brt, bb)
                ck("m_att")
                pr, prc = next_pair()
                for k in range(8):
                    tr(pr[:, k * 128:(k + 1) * 128], brt[:, k * 128:(k + 1) * 128], ident[:], [br_c[bb][k // 4], ident_c], [prc[k // 4]])
                ck("m_trp")
                for k in range(8):
                    evac_copy(pool32[:, BRT0 + k, c * 128:(c + 1) * 128], pr[:, k * 128:(k + 1) * 128], [prc[k // 4]], [p32_c[BRT0 + k]])
                ck("m_c%d" % c)
            kvbufs = load_x4(s, (t + 2) * 512) if t + 2 <= 8 else None
            ck("m_tr")
            for half in range(2):
                au, su, auc = load_pair(w_au_v[:, :, half * 512:(half + 1) * 512], w_su_v[:, :, half * 512:(half + 1) * 512], 4, 512)
                suc = auc
                ga, gac = load_slab(w_in_v[:, :, 2560 + half * 512: 2560 + (half + 1) * 512], 8, 512)
                gb, gbc = load_slab(w_in_v[:, :, 3584 + half * 512: 3584 + (half + 1) * 512], 8, 512)
                for m in range(4):
                    oA, oAc = next_bank()
                    for k in range(4):
                        mm(oA, au[:, k, m * 128:(m + 1) * 128], pool32[:, BRT0 + k, :], k == 0, k == 3, [auc, p32_c[BRT0 + k]], [oAc])
                    oB, oBc = next_bank()
                    for k in range(4):
                        mm(oB, su[:, k, m * 128:(m + 1) * 128], pool32[:, BRT0 + 4 + k, :], k == 0, k == 3, [suc, p32_c[BRT0 + 4 + k]], [oBc])
                    oG, oGc = next_bank()
                    for k in range(8):
                        mm(oG, ga[:, k, m * 128:(m + 1) * 128], hT[:, k, :], k == 0, k == 7, [gac, hT_c[k]], [oGc])
                    oH, oHc = next_bank()
                    for k in range(8):
                        mm(oH, gb[:, k, m * 128:(m + 1) * 128], hT[:, k, :], k == 0, k == 7, [gbc, hT_c[k]], [oHc])
                    tail_step()
                    act(sga[:], oG, AF.Sigmoid, [oGc], [sga_c])
                    act(sgb[:], oH, AF.Sigmoid, [oHc], [sgb_c])
                    tt("dve", sga[:], oA, sga[:], ALU.mult, [oAc, sga_c], [sga_c])
                    tt("dve", sgb[:], oB, sgb[:], ALU.mult, [oBc, sgb_c], [sgb_c])
                    tt("dve", pool32[:, MT0 + half * 4 + m, :], sga[:], sgb[:], ALU.add, [sga_c, sgb_c], [p32_c[MT0 + half * 4 + m]])
                    tail_step()
            ck("m_merge")
            flush_all()
            if t + 2 <= 8:
                kv_front(s, t + 2, kvbufs)
            ck("m_merge")
            wo = [load_slab(w_o_v[:, :, half * 512:(half + 1) * 512], 8, 512) for half in range(2)]
            xr = [next_xs() for _ in range(4)]
            ln1_slots = []
            for c in range(4):
                tok = 256 + t * 512 + c * 128
                dma("sp", xs[xr[c]][:], xs_d[s, tok:tok + 128, :], [], [xs_c[xr[c]]])
            for c in range(4):
                xb = xr[c]
                for half in range(2):
                    o, oc = next_bank()
                    for k in range(8):
                        mm(o, pool32[:, MT0 + k, c * 128:(c + 1) * 128], wo[half][0][:, k, :], k == 0, k == 7, [p32_c[MT0 + k], wo[half][1]], [oc])
                    hs = slice(half * 512, (half + 1) * 512)
                    tt("dve", x1[:, c, hs], o, g1bc[:, hs], ALU.mult, [oc, g1bc_c], [x1_c[c]])
                tt("dve", x1[:, c, :], x1[:, c, :], xs[xb][:], ALU.add, [xs_c[xb], x1_c[c]], [x1_c[c]])
                lnb_stats(c, x1[:, c, :], [x1_c[c]], LN_EPS / (ALPHA * ALPHA))
            lnb_rsqrt()
            for c in range(4):
                lnb_norm(c, x1[:, c, :], x1[:, c, :], [x1_c[c]])
            if t + 2 <= 8:
                kv_back(s, t + 2)
            ck("m_h2")
            transpose_affine([x1[:, c, :] for c in range(4)], [x1_c[c] for c in range(4)], 4, 5, act_only=True)
            for c in range(4):
                tt("dve", x1[:, c, :], x1[:, c, :], ln1g_bc[:], ALU.mult, [x1_c[c], ln1g_bc_c], [x1_c[c]])
                tt("dve", x1[:, c, :], x1[:, c, :], b1bc[:], ALU.add, [x1_c[c], b1bc_c], [x1_c[c]])
            ck("m_wo")
            for sl in range(8):
                slab, sc = load_slab(w_ff1_v[:, :, sl * 512:(sl + 1) * 512], 8, 512)
                for m in range(4):
                    mi = sl * 4 + m
                    o, oc = next_bank()
                    for k in range(8):
                        mm(o, slab[:, k, m * 128:(m + 1) * 128], hT[:, k, :], k == 0, k == 7, [sc, hT_c[k]], [oc])
                    act(rr[:], o, AF.Relu, [oc, b1col_c], [rr_c], bias=b1col[:, mi:mi + 1])
                    tt("dve", pool32[:, mi, :], rr[:], rr[:], ALU.mult, [rr_c], [p32_c[mi]])
            ck("m_ff1")
            for sl in range(8):
                slab, sc = load_slab(w_ff2_v[:, 4 * sl:4 * sl + 4, :], 4, 1024)
                for c in range(4):
                    for half in range(2):
                        b = c * 2 + half
                        o = PS[b // 2][:, (b % 2) * 512:(b % 2) * 512 + 512]
                        for kk in range(4):
                            mm(o, pool32[:, 4 * sl + kk, c * 128:(c + 1) * 128], slab[:, kk, half * 512:(half + 1) * 512],
                               sl == 0 and kk == 0, sl == 7 and kk == 3, [p32_c[4 * sl + kk], sc], [ps_c[b]])
            state["bank"] = 0
            state["pair"] = 0
            ck("m_ff2")
            tmps = [(gv, gv_c), (sga, sga_c), (sgb, sgb_c), (rr, rr_c)]
            bank = lambda b: PS[b // 2][:, (b % 2) * 512:(b % 2) * 512 + 512]
            for b in range(4):
                hs = slice((b % 2) * 512, (b % 2 + 1) * 512)
                tt("dve", tmps[b][0][:], bank(b), g2bc[:, hs], ALU.mult, [ps_c[b], g2bc_c], [tmps[b][1]])
            for b in range(4, 8):
                i = b - 4
                act(branch[i // 2][:, (i % 2) * 512:(i % 2 + 1) * 512], bank(b), AF.Copy, [ps_c[b]], [br_c[i // 2][i % 2]])
            for b in range(4):
                c, hs = b // 2, slice((b % 2) * 512, (b % 2 + 1) * 512)
                tt("dve", x1[:, c, hs], x1[:, c, hs], tmps[b][0][:], ALU.add, [x1_c[c], tmps[b][1]], [x1_c[c]])
            for b in range(4, 8):
                i = b - 4
                c, hs = b // 2, slice((b % 2) * 512, (b % 2 + 1) * 512)
                src = branch[i // 2][:, (i % 2) * 512:(i % 2 + 1) * 512]
                tt("dve", src, src, g2bc[:, hs], ALU.mult, [br_c[i // 2][i % 2], g2bc_c], [br_c[i // 2][i % 2]])
                tt("dve", x1[:, c, hs], x1[:, c, hs], src, ALU.add, [x1_c[c], br_c[i // 2][i % 2]], [x1_c[c]])

            def ln2_tails(s=s, t=t):
                for c in range(4):
                    lnb_stats(c, x1[:, c, :], [x1_c[c]], LN_EPS / (ALPHA * ALPHA))
                    yield
                lnb_rsqrt()
                yield
                for c in range(4):
                    lnb_norm(c, x1[:, c, :], x1[:, c, :], [x1_c[c]])
                    yield
                    tt("dve", x1[:, c, :], x1[:, c, :], ln2g_bc[:], ALU.mult, [x1_c[c], ln2g_bc_c], [x1_c[c]])
                    tt("dve", x1[:, c, :], x1[:, c, :], ln2b_bc[:], ALU.add, [x1_c[c], ln2b_bc_c], [x1_c[c]])
                    tok = t * 512 + c * 128
                    out_ops.append(dma("sp", y_d[s, tok:tok + 128, :], x1[:, c, :], [x1_c[c]], []))
                    yield

            pending.append(ln2_tails())

        pending = []

        def tail_step():
            while pending:
                try:
                    next(pending[0])
                    return
                except StopIteration:
                    pending.pop(0)

        def flush_all():
            while pending:
                tail_step()

        try:
          ck("setup")
          for s in range(nseg):
            segment_setup(s)
            ck("segsetup")
            kv_block(s, 0)
            ck("kv0")
            kv_block(s, 1)
            for t in range(ntiles):
                main_tile(s, t)
          flush_all()
        except _Stop:
            pass
        if _os.environ.get("KTAGS"):
            import json as _json
            _json.dump({e: [o.tag for o in S.ops[e] if not o.is_dma] for e in S.ops}, open(_os.environ["KTAGS"], "w"))
        S.emit(nc, final_waits=out_ops)
    return nc


def _ext_rows(seq_rows, row0):
    er = np.full((36, 2), -1, np.int64)
    src_chunk = np.zeros(36, np.int64)
    c0 = row0 // 2
    nchunk = seq_rows // 2
    for mc in range(32):
        er[2 + mc] = (row0 + 2 * mc, row0 + 2 * mc + 1)
        src_chunk[2 + mc] = c0 + mc
    if row0 > 0:
        for e in range(2):
            g = c0 - 2 + e
            er[e] = (2 * g, 2 * g + 1)
            src_chunk[e] = g
    else:
        er[1] = (6, 7)
        src_chunk[1] = 3
        src_chunk[0] = 0
    if row0 + 64 < seq_rows:
        for e in range(2):
            g = c0 + 32 + e
            er[34 + e] = (2 * g, 2 * g + 1)
            src_chunk[34 + e] = g
    else:
        er[34] = (seq_rows - 8, seq_rows - 7)
        src_chunk[34] = nchunk - 4
        src_chunk[35] = nchunk - 1
    return er, src_chunk


def _build_btab(rpb, seq_rows, row0):
    er, _ = _ext_rows(seq_rows, row0)
    out = np.full((5, 128, 8, 5, 128), NEG, np.float32)
    q = np.arange(128)
    qro, qc = q // 64, q % 64
    cs = np.clip(qc - 8, 0, 48)
    kcol = np.arange(64)
    colvalid = (kcol[None, :] >= cs[:, None]) & (kcol[None, :] < cs[:, None] + 16)
    dc = np.clip(kcol[None, :] - qc[:, None] + 15, 0, 30)
    for ti, mc in enumerate((0, 1, 2, 30, 31)):
        r = row0 + 2 * mc + qro
        rs = np.clip(r - 4, 0, seq_rows - 8)
        covered = np.zeros((128, seq_rows), bool)
        for j in range(5):
            e = mc + j
            for half in range(2):
                krow = int(er[e, half])
                if krow < 0:
                    continue
                rowvalid = (krow >= rs) & (krow < rs + 8) & (~covered[:, krow])
                covered[:, krow] |= rowvalid
                valid = rowvalid[:, None] & colvalid
                dr = np.clip(krow - r + 7, 0, 14)
                vals = rpb[:, dr[:, None], dc]
                vals = np.transpose(vals, (2, 0, 1))
                out[ti, half * 64:(half + 1) * 64, :, j, :] = np.where(valid.T[:, None, :], vals, np.float32(NEG))
    return out.reshape(5, 128, 5120)


def _ext_tokens(xseq, seq_rows, row0):
    _, src = _ext_rows(seq_rows, row0)
    xc = xseq.reshape(seq_rows // 2, 128, D)
    return xc[src].reshape(EXT_TOK, D)


_NC_CACHE = {}


def kernel(x_prompt, x_sample, c_prompt, c_sample, w_ada, b_ada, w_in, rpb, sgu_ln_g, sgu_ln_b,
           w_s, b_s, w_attn_up, w_sgu_up, w_o, ln1_g, ln1_b, w_ff1, b_ff1, w_ff2, b_ff2, ln2_g, ln2_b):
    f = lambda a: np.ascontiguousarray(np.asarray(a, dtype=np.float32))
    x_prompt, x_sample, c_prompt, c_sample = f(x_prompt), f(x_sample), f(c_prompt), f(c_sample)
    rpb0 = f(rpb)[0]
    shared = {
        "w_ada": f(w_ada)[0], "b_ada": f(b_ada)[0].reshape(1, -1), "w_in": f(w_in)[0],
        "sgu_ln_g": f(sgu_ln_g)[0].reshape(1, -1), "sgu_ln_b": f(sgu_ln_b)[0].reshape(1, -1),
        "w_s": f(w_s)[0], "b_s": f(b_s)[0], "w_attn_up": f(w_attn_up)[0], "w_sgu_up": f(w_sgu_up)[0],
        "w_o": f(w_o)[0], "ln1_g": f(ln1_g)[0].reshape(1, -1), "ln1_b": f(ln1_b)[0].reshape(1, -1),
        "w_ff1": f(w_ff1)[0], "b_ff1": f(b_ff1)[0].reshape(32, 128), "w_ff2": f(w_ff2)[0],
        "b_ff2": f(b_ff2)[0].reshape(1, -1), "ln2_g": f(ln2_g)[0].reshape(1, -1), "ln2_b": f(ln2_b)[0].reshape(1, -1),
    }
    bt_sample = _build_btab(rpb0, 64, 0)
    bt_prompt = [_build_btab(rpb0, 256, 64 * qi) for qi in range(4)]
    in_maps = []
    for i in range(NCORES):
        pi, qi = i // 4, i % 4
        xs = np.stack([_ext_tokens(x_sample[i], 64, 0), _ext_tokens(x_prompt[pi], 256, 64 * qi)])
        cc = np.stack([c_sample[i], c_prompt[pi]]).reshape(2, 8, 128)
        m = dict(shared)
        m["xs"] = np.ascontiguousarray(xs)
        m["cc"] = np.ascontiguousarray(cc)
        m["btab"] = np.ascontiguousarray(np.stack([bt_sample, bt_prompt[qi]]))
        in_maps.append(m)
    if "nc" not in _NC_CACHE:
        _NC_CACHE["nc"] = build_program()
    res = run_bass_kernel_spmd(_NC_CACHE["nc"], in_maps, core_ids=list(range(NCORES)))
    y_prompt = np.empty((2, 16384, D), np.float32)
    y_sample = np.empty((8, 4096, D), np.float32)
    for i in range(NCORES):
        y = np.asarray(res.results[i]["y"], dtype=np.float32)
        y_sample[i] = y[0]
        y_prompt[i // 4, (i % 4) * 4096:(i % 4 + 1) * 4096] = y[1]
    return (y_prompt, y_sample)
```

```python
import numpy as np
from contextlib import ExitStack
import concourse.bass as bass
import concourse.mybir as mybir
from concourse.bass_utils import run_bass_kernel_spmd

F32 = mybir.dt.float32
BF16 = mybir.dt.bfloat16
AF = mybir.ActivationFunctionType
ALU = mybir.AluOpType

D = 1024
ALPHA = 2.0 ** 0.25
LN_EPS = 1e-5
NEG = -30000.0
NCORES = 8
SEG_TOK = 4096
EXT_TOK = 4608
NSLOT = 6
NDMASEM = 8
COMPUTE = ("pe", "act", "dve", "pool")


class Op:
    __slots__ = ("eng", "fn", "deps", "is_dma", "signal", "sigidx", "dsem", "dval", "prev_on_sem", "tag")

    def __init__(self, eng, fn, is_dma):
        self.eng = eng
        self.fn = fn
        self.deps = []
        self.is_dma = is_dma
        self.signal = False
        self.sigidx = 0
        self.dsem = None
        self.dval = 0
        self.prev_on_sem = None


class Cell:
    __slots__ = ("w", "r", "excl")

    def __init__(self, excl=False):
        self.w = None
        self.r = []
        self.excl = excl


class Sched:
    def __init__(self):
        self.ops = {e: [] for e in ("pe", "act", "dve", "pool", "sp")}
        self.dma_count = {e: 0 for e in ("act", "pool", "sp")}
        self.dma_last = {}

    def add(self, eng, fn, reads=(), writes=(), dma=False):
        op = Op(eng, fn, dma)
        op.tag = getattr(self, "stage", "")
        deps = []
        if any(c.excl for c in reads):
            writes = list(writes) + [c for c in reads if c.excl]
            reads = [c for c in reads if not c.excl]
        for c in reads:
            if c.w is not None:
                deps.append(c.w)
        for c in writes:
            if c.w is not None:
                deps.append(c.w)
            deps.extend(c.r)
        seen = set()
        out = []
        for d in deps:
            if id(d) in seen:
                continue
            seen.add(id(d))
            if (not d.is_dma) and (not dma) and d.eng == "pe" and eng == "pe":
                continue
            out.append(d)
        op.deps = out
        for c in reads:
            if not dma:
                c.r = [o for o in c.r if o.is_dma or o.eng != eng]
            c.r.append(op)
        for c in writes:
            c.w = op
            c.r = []
        self.ops[eng].append(op)
        if dma:
            k = self.dma_count[eng]
            self.dma_count[eng] = k + 1
            key = (eng, k % NDMASEM)
            op.prev_on_sem = self.dma_last.get(key)
            op.dsem = key
            op.dval = 16 * (k // NDMASEM + 1)
            self.dma_last[key] = op
            op.signal = True
        for d in out:
            d.signal = True
        return op

    def emit(self, nc, final_waits=()):
        with ExitStack() as es:
            csem = {e: es.enter_context(nc.semaphore(f"c_{e}")) for e in COMPUTE}
            dsem = {}
            for e in ("act", "pool", "sp"):
                for j in range(NDMASEM):
                    if self.dma_count[e] > j:
                        dsem[(e, j)] = es.enter_context(nc.semaphore(f"d_{e}{j}"))
            for e in COMPUTE:
                n = 0
                for op in self.ops[e]:
                    if not op.is_dma and op.signal:
                        n += 1
                        op.sigidx = n
            block = es.enter_context(nc.Block())

            def run(ename):
                def body(eng):
                    known = {}

                    def wait_for(d):
                        if d.is_dma:
                            key, val, sem = d.dsem, d.dval, dsem[d.dsem]
                        else:
                            key, val, sem = d.eng, d.sigidx, csem[d.eng]
                        if known.get(key, 0) >= val:
                            return
                        eng.wait_ge(sem, val)
                        known[key] = val

                    for op in self.ops[ename]:
                        for d in op.deps:
                            wait_for(d)
                        if op.is_dma and op.prev_on_sem is not None:
                            wait_for(op.prev_on_sem)
                        ins = op.fn(eng)
                        if op.is_dma:
                            ins.then_inc(dsem[op.dsem], 16)
                        elif op.signal:
                            ins.then_inc(csem[ename], 1)
                    if ename == "sp":
                        for d in final_waits:
                            wait_for(d)

                return body

            block.tensor(run("pe"))
            block.scalar(run("act"))
            block.vector(run("dve"))
            block.gpsimd(run("pool"))
            block.sync(run("sp"))


def build_program(nseg=2, ntiles=8):
    nc = bass.Bass("TRN2", target_bir_lowering=False)
    dt_in = lambda name, shape: nc.dram_tensor(name, shape, F32, kind="ExternalInput").ap()
    xs_d = dt_in("xs", [2, EXT_TOK, D])
    cc_d = dt_in("cc", [2, 8, 128])
    bt_d = dt_in("btab", [2, 5, 128, 5120])
    w_ada_d = dt_in("w_ada", [D, 6 * D])
    b_ada_d = dt_in("b_ada", [1, 6 * D])
    w_in_d = dt_in("w_in", [D, 4608])
    sgu_g_d = dt_in("sgu_ln_g", [1, 512])
    sgu_b_d = dt_in("sgu_ln_b", [1, 512])
    w_s_d = dt_in("w_s", [8, 128, 128])
    b_s_d = dt_in("b_s", [8, 128])
    w_au_d = dt_in("w_attn_up", [512, D])
    w_su_d = dt_in("w_sgu_up", [512, D])
    w_o_d = dt_in("w_o", [D, D])
    ln1_g_d = dt_in("ln1_g", [1, D])
    ln1_b_d = dt_in("ln1_b", [1, D])
    w_ff1_d = dt_in("w_ff1", [D, 4096])
    b_ff1_d = dt_in("b_ff1", [32, 128])
    w_ff2_d = dt_in("w_ff2", [4096, D])
    b_ff2_d = dt_in("b_ff2", [1, D])
    ln2_g_d = dt_in("ln2_g", [1, D])
    ln2_b_d = dt_in("ln2_b", [1, D])
    y_d = nc.dram_tensor("y", [2, SEG_TOK, D], F32, kind="ExternalOutput").ap()

    kp = lambda ap: ap.rearrange("(k p) n -> p k n", p=128)
    w_ada_v, w_in_v, w_au_v, w_su_v = kp(w_ada_d), kp(w_in_d), kp(w_au_d), kp(w_su_d)
    w_o_v, w_ff1_v, w_ff2_v = kp(w_o_d), kp(w_ff1_d), kp(w_ff2_d)

    S = Sched()
    C = Cell
    out_ops = []
    with ExitStack() as es:
        sb = lambda name, shape, dt: es.enter_context(nc.sbuf_tensor(name, shape, dt))
        wring = [sb(f"wring{i}", [128, 4096], BF16) for i in range(NSLOT)]
        wring_c = [C() for _ in range(NSLOT)]
        kT = [sb(f"kT{i}", [128, 4, 512], BF16) for i in range(3)]
        kT_c = [C() for _ in range(3)]
        Vr = [sb(f"V{i}", [128, 4, 8, 65], BF16) for i in range(3)]
        V_c = [C() for _ in range(3)]
        xs = [sb(f"xs{i}", [128, D], F32) for i in range(4)]
        xs_c = [C() for _ in range(4)]
        hT = sb("hT", [128, 8, 512], BF16)
        hT_c = [C() for _ in range(8)]
        pool32 = sb("pool32", [128, 32, 512], BF16)
        p32_c = [C() for _ in range(32)]
        gv = sb("gv", [128, 512], F32); gv_c = C()
        branch = [sb(f"branch{i}", [128, D], F32) for i in range(2)]
        br_c = [[C(), C()] for _ in range(2)]
        sga = sb("sga", [128, 512], F32); sga_c = C()
        sgb = sb("sgb", [128, 512], F32); sgb_c = C()
        x1 = sb("x1", [128, 4, D], F32); x1_c = [C() for _ in range(4)]
        rr = sb("rr", [128, 512], F32); rr_c = C()
        bt = sb("bt", [128, 8, 5, 128], BF16); bt_c = C()
        g1bc = sb("g1bc", [128, D], F32); g1bc_c = C()
        b1bc = sb("b1bc", [128, D], F32); b1bc_c = C()
        g2bc = sb("g2bc", [128, D], F32); g2bc_c = C()
        ln1g_bc = sb("ln1g_bc", [128, D], F32); ln1g_bc_c = C()
        ln2g_bc = sb("ln2g_bc", [128, D], F32); ln2g_bc_c = C()
        ln2b_bc = sb("ln2b_bc", [128, D], F32); ln2b_bc_c = C()
        PT = [sb(f"PT{i}", [128, 1280], BF16) for i in range(2)]
        PT_c = [[C(), C()] for _ in range(2)]
        WsT = sb("WsT", [128, 8, 128], BF16); WsT_c = C()
        ident = sb("ident", [128, 128], F32); ident_c = C()
        sgug_bc = sb("sgug_bc", [128, 512], F32); sgug_c = C()
        sgub_bc = sb("sgub_bc", [128, 512], F32); sgub_c = C()
        bsT = sb("bsT", [128, 8], F32); bsT_c = C()
        b1col = sb("b1col", [128, 32], F32); b1col_c = C()
        cols = sb("cols", [128, 6, 8], F32); cols_c = C()
        lncol = sb("lncol", [128, 2, 8], F32); lncol_c = C()
        rows8 = sb("rows8", [32, 128], F32); rows8_c = C()
        siluc = sb("siluc", [128, 8], F32); siluc_c = C()
        silubc = PT[0][:, 0:1024].rearrange("p (k n) -> p k n", n=128)
        stt = sb("stt", [128, 4, 2, 6], F32); stt_c = [C() for _ in range(4)]
        mv = sb("mv", [128, 4, 2], F32); mv_c = [C() for _ in range(4)]
        rstd = sb("rstd", [128, 4], F32); rstd_c = [C() for _ in range(4)]
        nmr = sb("nmr", [128, 4], F32); nmr_c = [C() for _ in range(4)]
        rec = sb("rec", [128, 8, 1], F32); rec_c = C()
        PSB = es.enter_context(nc.psum_tensor("psb", [128, 4096], F32))
        PS = [PSB[:, i * 1024:(i + 1) * 1024] for i in range(4)]
        ps_c = [C(True) for _ in range(8)]
        state = {"st": 0, "spair": 0, "bank": 0, "pair": 0, "slot": 0, "xsb": 0, "brb": 0, "ptb": 0, "alt": 0}

        def next_bank():
            b = state["bank"]
            state["bank"] = (b + 1) % 8
            return PS[b // 2][:, (b % 2) * 512:(b % 2) * 512 + 512], ps_c[b]

        def next_pair():
            p = state["pair"]
            state["pair"] = (p + 1) % 4
            return PS[p], [ps_c[2 * p], ps_c[2 * p + 1]]

        def mm(out, lhsT, rhs, start, stop, reads, writes):
            S.add("pe", lambda q: q.matmul(out, lhsT=lhsT, rhs=rhs, start=start, stop=stop), reads, writes)

        def tr(out, in_, idn, reads, writes):
            S.add("pe", lambda q: q.transpose(out=out, in_=in_, identity=idn), reads, writes)

        def act(out, in_, func, reads, writes, scale=1.0, bias=0.0):
            S.add("act", lambda q: q.activation(out=out, in_=in_, func=func, bias=bias, scale=scale), reads, writes)

        def tt(eng, out, in0, in1, op, reads, writes):
            S.add(eng, lambda q: q.tensor_tensor(out=out, in0=in0, in1=in1, op=op), reads, writes)

        def ts(eng, out, in0, s1, s2, op0, op1, reads, writes):
            S.add(eng, lambda q: q.tensor_scalar(out=out, in0=in0, scalar1=s1, scalar2=s2, op0=op0, op1=op1), reads, writes)

        def stt_op(eng, out, in0, scalar, in1, op0, op1, reads, writes):
            S.add(eng, lambda q: q.scalar_tensor_tensor(out=out, in0=in0, scalar=scalar, in1=in1, op0=op0, op1=op1), reads, writes)

        def cp(eng, out, in_, reads, writes):
            S.add(eng, lambda q: q.tensor_copy(out=out, in_=in_), reads, writes)

        def dma(eng, out, in_, reads, writes):
            return S.add(eng, lambda q: q.dma_start(out=out, in_=in_), reads, writes, dma=True)

        def load_slab(src_ap, nk, ncol):
            i = state["slot"]
            state["slot"] = (i + 1) % NSLOT
            view = wring[i][:, 0:nk * ncol].rearrange("p (k n) -> p k n", k=nk)
            dma("pool", view, src_ap, [], [wring_c[i]])
            return view, wring_c[i]

        def load_pair(srcA, srcB, nk, ncol):
            i = state["slot"]
            state["slot"] = (i + 1) % NSLOT
            half = nk * ncol
            vA = wring[i][:, 0:half].rearrange("p (k n) -> p k n", k=nk)
            vB = wring[i][:, half:2 * half].rearrange("p (k n) -> p k n", k=nk)
            dma("pool", vA, srcA, [], [wring_c[i]])
            dma("pool", vB, srcB, [], [wring_c[i]])
            return vA, vB, wring_c[i]

        def alt_eng():
            import os
            f = os.environ.get("KEVAC", "")
            if f:
                return f
            state["alt"] ^= 1
            return "act" if state["alt"] else "dve"

        def evac_affine(out, in_, scol, bcol, reads, writes):
            import os
            if os.environ.get("KAFF", "dve") == "dve" and alt_eng() == "dve":
                ts("dve", out, in_, scol, bcol, ALU.mult, ALU.add, reads, writes)
            else:
                act(out, in_, AF.Identity, reads, writes, scale=scol, bias=bcol)

        def evac_copy(out, in_, reads, writes, scale=None):
            if alt_eng() == "act":
                if scale is None:
                    act(out, in_, AF.Copy, reads, writes)
                else:
                    act(out, in_, AF.Copy, reads, writes, scale=scale)
            else:
                if scale is None:
                    cp("dve", out, in_, reads, writes)
                else:
                    S.add("dve", lambda q: q.tensor_scalar_mul(out=out, in0=in_, scalar1=scale), reads, writes)

        def next_xs():
            i = state["xsb"]
            state["xsb"] = (i + 1) % 4
            return i

        def ln_phase_a(src, src_cells, eps, nhalf=2):
            i = state["st"]
            state["st"] = (i + 1) % 4
            for h in range(nhalf):
                S.add("dve", lambda q, h=h: q.bn_stats(out=stt[:, i, h, :], in_=src[:, h * 512:(h + 1) * 512]), src_cells, [stt_c[i]])
            S.add("dve", lambda q: q.bn_aggr(out=mv[:, i, :], in_=stt[:, i, 0:nhalf, :].rearrange("p a b -> p (a b)")), [stt_c[i]], [mv_c[i]])
            S.add("dve", lambda q: q.tensor_scalar_add(out=rstd[:, i:i + 1], in0=mv[:, i, 1:2], scalar1=eps), [mv_c[i]], [rstd_c[i]])
            act(rstd[:, i:i + 1], rstd[:, i:i + 1], AF.Sqrt, [rstd_c[i]], [rstd_c[i]])
            return i

        def ln_phase_b(i, dst, src, cells):
            S.add("dve", lambda q: q.reciprocal(out=rstd[:, i:i + 1], in_=rstd[:, i:i + 1]), [rstd_c[i]], [rstd_c[i]])
            stt_op("dve", nmr[:, i:i + 1], mv[:, i, 0:1], -1.0, rstd[:, i:i + 1], ALU.mult, ALU.mult, [mv_c[i], rstd_c[i]], [nmr_c[i]])
            act(dst, src, AF.Identity, cells + [rstd_c[i], nmr_c[i]], cells, scale=rstd[:, i:i + 1], bias=nmr[:, i:i + 1])

        def lnb_stats(c, src, src_cells, eps, nhalf=2):
            for h in range(nhalf):
                S.add("dve", lambda q, h=h: q.bn_stats(out=stt[:, c, h, :], in_=src[:, h * 512:(h + 1) * 512]), src_cells, [stt_c[c]])
            S.add("dve", lambda q: q.bn_aggr(out=mv[:, c, :], in_=stt[:, c, 0:nhalf, :].rearrange("p a b -> p (a b)")), [stt_c[c]], [mv_c[c]])
            S.add("dve", lambda q: q.tensor_scalar_add(out=rstd[:, c:c + 1], in0=mv[:, c, 1:2], scalar1=eps), [mv_c[c]], [rstd_c[c]])

        def lnb_rsqrt():
            act(rstd[:, 0:4], rstd[:, 0:4], AF.Sqrt, rstd_c, rstd_c)
            S.add("dve", lambda q: q.reciprocal(out=rstd[:, 0:4], in_=rstd[:, 0:4]), rstd_c, rstd_c)
            stt_op("dve", nmr[:, 0:4], mv[:, :, 0], -1.0, rstd[:, 0:4], ALU.mult, ALU.mult, mv_c + rstd_c, nmr_c)

        def lnb_norm(c, dst, src, cells):
            act(dst, src, AF.Identity, cells + [rstd_c[c], nmr_c[c]], cells, scale=rstd[:, c:c + 1], bias=nmr[:, c:c + 1])

        def layer_norm_stats(src, src_cells, eps, nhalf=2):
            i = state["st"]
            state["st"] = (i + 1) % 4
            for h in range(nhalf):
                S.add("dve", lambda q, h=h: q.bn_stats(out=stt[:, i, h, :], in_=src[:, h * 512:(h + 1) * 512]), src_cells, [stt_c[i]])
            S.add("dve", lambda q: q.bn_aggr(out=mv[:, i, :], in_=stt[:, i, 0:nhalf, :].rearrange("p a b -> p (a b)")), [stt_c[i]], [mv_c[i]])
            S.add("dve", lambda q: q.tensor_scalar_add(out=rstd[:, i:i + 1], in0=mv[:, i, 1:2], scalar1=eps), [mv_c[i]], [rstd_c[i]])
            act(rstd[:, i:i + 1], rstd[:, i:i + 1], AF.Sqrt, [rstd_c[i]], [rstd_c[i]])
            S.add("dve", lambda q: q.reciprocal(out=rstd[:, i:i + 1], in_=rstd[:, i:i + 1]), [rstd_c[i]], [rstd_c[i]])
            stt_op("dve", nmr[:, i:i + 1], mv[:, i, 0:1], -1.0, rstd[:, i:i + 1], ALU.mult, ALU.mult, [mv_c[i], rstd_c[i]], [nmr_c[i]])
            return i

        S.add("pool", lambda q: q.memset(ident[:], 0.0), [], [ident_c])
        S.add("pool", lambda q: q.affine_select(out=ident[:], in_=ident[:], compare_op=ALU.not_equal, fill=1.0, base=0,
                                                pattern=[[-1, 128]], channel_multiplier=1), [ident_c], [ident_c])
        for i in range(3):
            S.add("pool", lambda q, i=i: q.memset(Vr[i][:], 1.0), [], [V_c[i]])
        dma("sp", ln1g_bc[:], ln1_g_d.partition_broadcast(128), [], [ln1g_bc_c])
        dma("sp", ln2g_bc[:], ln2_g_d.partition_broadcast(128), [], [ln2g_bc_c])
        dma("sp", ln2b_bc[:], ln2_b_d.partition_broadcast(128), [], [ln2b_bc_c])
        dma("sp", sgug_bc[:], sgu_g_d.partition_broadcast(128), [], [sgug_c])
        dma("sp", sgub_bc[:], sgu_b_d.partition_broadcast(128), [], [sgub_c])
        dma("sp", rows8[:, :], b_ff1_d, [], [rows8_c])
        o, oc = next_bank()
        tr(o[:, 0:32], rows8[0:32, :], ident[0:32, 0:32], [rows8_c, ident_c], [oc])
        cp("dve", b1col[:], o[:, 0:32], [oc], [b1col_c])
        dma("sp", rows8[0:8, :], b_s_d, [b1col_c], [rows8_c])
        o, oc = next_bank()
        tr(o[:, 0:8], rows8[0:8, :], ident[0:8, 0:8], [rows8_c, ident_c], [oc])
        cp("dve", bsT[:], o[:, 0:8], [oc], [bsT_c])
        dma("sp", rows8[0:8, :], ln1_g_d.rearrange("o (k p) -> (o k) p", p=128), [bsT_c], [rows8_c])
        dma("sp", rows8[8:16, :], ln1_b_d.rearrange("o (k p) -> (o k) p", p=128), [bsT_c], [rows8_c])
        o, oc = next_bank()
        tr(o[:, 0:16], rows8[0:16, :], ident[0:16, 0:16], [rows8_c, ident_c], [oc])
        cp("dve", lncol[:].rearrange("p a b -> p (a b)"), o[:, 0:16], [oc], [lncol_c])
        for g in range(8):
            dma("sp", xs[0][:, g * 128:(g + 1) * 128], w_s_d[g], [], [xs_c[0]])
        pr, prc = next_pair()
        for g in range(8):
            tr(pr[:, g * 128:(g + 1) * 128], xs[0][:, g * 128:(g + 1) * 128], ident[:], [xs_c[0], ident_c], [prc[g // 4]])
        cp("dve", WsT[:].rearrange("p g n -> p (g n)"), pr[:, :], prc, [WsT_c])

        def segment_setup(s):
            dma("sp", rows8[0:8, :], cc_d[s], [lncol_c, cols_c], [rows8_c])
            o, oc = next_bank()
            tr(o[:, 0:8], rows8[0:8, :], ident[0:8, 0:8], [rows8_c, ident_c], [oc])
            act(siluc[:], o[:, 0:8], AF.Silu, [oc], [siluc_c])
            cp("dve", silubc, siluc[:].unsqueeze(2).to_broadcast([128, 8, 128]), [siluc_c], PT_c[0])
            colmap = {0: 0, 1: 1, 3: 2, 4: 3}
            for n in range(12):
                comp, half = n // 2, n % 2
                slab, sc = load_slab(w_ada_v[:, :, n * 512:(n + 1) * 512], 8, 512)
                o, oc = next_bank()
                for k in range(8):
                    mm(o, silubc[:, k, :], slab[:, k, :], k == 0, k == 7, PT_c[0] + [sc], [oc])
                dma("sp", gv[:], b_ada_d[:, n * 512:(n + 1) * 512].partition_broadcast(128), [], [gv_c])
                if comp == 2 or comp == 5:
                    dst, dc = (g1bc, g1bc_c) if comp == 2 else (g2bc, g2bc_c)
                    tt("dve", dst[:, half * 512:(half + 1) * 512], o, gv[:], ALU.add, [oc, gv_c], [dc])
                    S.add("dve", lambda q, dst=dst, half=half: q.tensor_scalar_mul(out=dst[:, half * 512:(half + 1) * 512],
                                                                                 in0=dst[:, half * 512:(half + 1) * 512], scalar1=1.0 / ALPHA), [dc], [dc])
                else:
                    tt("dve", sga[:], o, gv[:], ALU.add, [oc, gv_c], [sga_c])
                    o2, oc2 = next_bank()
                    for j in range(4):
                        tr(o2[:, j * 128:(j + 1) * 128], sga[:, j * 128:(j + 1) * 128], ident[:], [sga_c, ident_c], [oc2])
                    ci = colmap[comp]
                    src = o2.rearrange("p (j n) -> p j n", n=128)[:, :, 0:1]
                    dstc = cols[:, ci, half * 4:(half + 1) * 4].unsqueeze(2)
                    if comp in (1, 4):
                        S.add("dve", lambda q, dstc=dstc, src=src: q.tensor_scalar_add(out=dstc, in0=src, scalar1=1.0), [oc2], [cols_c])
                    else:
                        cp("dve", dstc, src, [oc2], [cols_c])
            tt("dve", cols[:, 4, :], lncol[:, 0, :], cols[:, 3, :], ALU.mult, [lncol_c, cols_c], [cols_c])
            tt("dve", cols[:, 5, :], lncol[:, 1, :], cols[:, 3, :], ALU.mult, [lncol_c, cols_c], [cols_c])
            tt("dve", cols[:, 5, :], cols[:, 5, :], cols[:, 2, :], ALU.add, [cols_c], [cols_c])
            dma("sp", xs[0][:], b_ff2_d.partition_broadcast(128), [], [xs_c[0]])
            dma("sp", b1bc[:], ln1_b_d.partition_broadcast(128), [], [b1bc_c])
            tt("dve", xs[0][:], xs[0][:], g2bc[:], ALU.mult, [xs_c[0], g2bc_c], [xs_c[0]])
            tt("dve", b1bc[:], b1bc[:], xs[0][:], ALU.add, [xs_c[0], b1bc_c], [b1bc_c])

        import os as _os
        _stop = _os.environ.get("KSTOP", "")

        class _Stop(Exception):
            pass

        def ck(name):
            S.stage = name
            if name == _stop:
                raise _Stop()

        def transpose_affine(srcs, src_cells, scol_i, bcol_i, act_only=False):
            for half in range(2):
                banks = [next_bank() for _ in range(4)]
                for kk in range(4):
                    k = half * 4 + kk
                    for c in range(4):
                        tr(banks[kk][0][:, c * 128:(c + 1) * 128], srcs[c][:, k * 128:(k + 1) * 128], ident[:], [src_cells[c], ident_c], [banks[kk][1]])
                for kk in range(4):
                    k = half * 4 + kk
                    if act_only:
                        act(hT[:, k, :], banks[kk][0], AF.Identity, [banks[kk][1], cols_c], [hT_c[k]],
                            scale=cols[:, scol_i, k:k + 1], bias=cols[:, bcol_i, k:k + 1])
                    else:
                        evac_affine(hT[:, k, :], banks[kk][0], cols[:, scol_i, k:k + 1], cols[:, bcol_i, k:k + 1], [banks[kk][1], cols_c], [hT_c[k]])

        def load_x4(s, tok0):
            bufs = [next_xs() for _ in range(4)]
            for c in range(4):
                dma("sp", xs[bufs[c]][:], xs_d[s, tok0 + c * 128: tok0 + (c + 1) * 128, :], [], [xs_c[bufs[c]]])
            return bufs

        def make_hT(s, tok0, nchunks, bufs=None, act_only=False):
            if bufs is None:
                bufs = load_x4(s, tok0)
            transpose_affine([xs[b] for b in bufs], [xs_c[b] for b in bufs], 1, 0, act_only)

        def kv_front(s, b, bufs=None):
            ck("kv_start")
            make_hT(s, b * 512, 4, bufs, act_only=(bufs is not None))

        def kv_back(s, b):
            slot = b % 3
            ck("kv_h")
            slab, sc = load_slab(w_in_v[:, :, 512:1024], 8, 512)
            for m in range(4):
                o, oc = next_bank()
                for k in range(8):
                    mm(o, slab[:, k, m * 128:(m + 1) * 128], hT[:, k, :], k == 0, k == 7, [sc, hT_c[k]], [oc])
                act(kT[slot][:, m, :], o, AF.Copy, [oc], [kT_c[slot]])
            ck("kv_k")
            slab, sc = load_slab(w_in_v[:, :, 1024:1536], 8, 512)
            for c in range(4):
                o, oc = next_bank()
                for k in range(8):
                    mm(o, hT[:, k, c * 128:(c + 1) * 128], slab[:, k, :], k == 0, k == 7, [sc, hT_c[k]], [oc])
                act(Vr[slot][:, c, :, 0:64], o.rearrange("p (h d) -> p h d", d=64), AF.Copy, [oc], [V_c[slot]])

        def kv_block(s, b):
            kv_front(s, b)
            kv_back(s, b)

        QT0, UG0, VN0, VN1, BRT0, MT0 = 16, 20, 24, 25, 0, 8

        def main_tile(s, t):
            ck("m_start")
            make_hT(s, 256 + t * 512, 4, act_only=True)
            ck("m_h")
            vslab, vsc = load_slab(w_in_v[:, :, 2048:2560], 8, 512)
            gvb = [(gv, gv_c), (sga, sga_c), (sgb, sgb_c), (rr, rr_c)]
            for c in range(4):
                g_, g_c = gvb[c]
                o, oc = next_bank()
                for k in range(8):
                    mm(o, hT[:, k, c * 128:(c + 1) * 128], vslab[:, k, :], k == 0, k == 7, [vsc, hT_c[k]], [oc])
                act(g_[:], o, AF.Gelu_apprx_tanh, [oc], [g_c])
                lnb_stats(c, g_, [g_c], LN_EPS, nhalf=1)
            lnb_rsqrt()
            for c in range(4):
                g_, g_c = gvb[c]
                lnb_norm(c, g_[:], g_[:], [g_c])
                tt("dve", g_[:], g_[:], sgug_bc[:], ALU.mult, [g_c, sgug_c], [g_c])
                tt("dve", pool32[:, VN0 + c, :], g_[:], sgub_bc[:], ALU.add, [g_c, sgub_c], [p32_c[VN0 + c]])

            slab, sc = load_slab(w_in_v[:, :, 0:512], 8, 512)
            for m in range(4):
                o, oc = next_bank()
                for k in range(8):
                    mm(o, slab[:, k, m * 128:(m + 1) * 128], hT[:, k, :], k == 0, k == 7, [sc, hT_c[k]], [oc])
                evac_copy(pool32[:, QT0 + m, :], o, [oc], [p32_c[QT0 + m]], scale=0.125)
            ck("m_q")
            slab, sc = load_slab(w_in_v[:, :, 1536:2048], 8, 512)
            for c in range(4):
                o, oc = next_bank()
                for k in range(8):
                    mm(o, hT[:, k, c * 128:(c + 1) * 128], slab[:, k, :], k == 0, k == 7, [sc, hT_c[k]], [oc])
                act(pool32[:, UG0 + c, :], o, AF.Gelu_apprx_tanh, [oc], [p32_c[UG0 + c]])
            ck("m_u")
            def sgu_back(c, brt, bb):
                g_, g_c = gvb[c]
                vn_i = VN0 + c
                o, oc = next_bank()
                for g in range(8):
                    mm(o[:, g * 64:(g + 1) * 64], WsT[:, g, :], pool32[:, vn_i, g * 64:(g + 1) * 64], True, True, [WsT_c, p32_c[vn_i]], [oc])
                tt("dve", g_[:].rearrange("p (g d) -> p g d", d=64), o.rearrange("p (g d) -> p g d", d=64),
                   bsT[:].unsqueeze(2).to_broadcast([128, 8, 64]), ALU.add, [oc, bsT_c], [g_c])
                tt("dve", brt[:, 512:1024], g_[:], pool32[:, UG0 + c, :], ALU.mult, [g_c, p32_c[UG0 + c]], [br_c[bb][1]])

            for c in range(4):
                mc = 4 * t + c
                bb = state["brb"]; state["brb"] ^= 1
                brt = branch[bb]
                ck("c_start")
                ck("m_sgu")
                ttype = {0: 0, 1: 1, 30: 3, 31: 4}.get(mc, 2)
                if mc in (0, 1, 2, 30, 31):
                    btf = bt[:].rearrange("p h j n -> p (h j n)")
                    dma("pool", btf, bt_d[s, ttype], [], [bt_c])
                    act(btf, btf, AF.Exp, [bt_c], [bt_c])
                Opair, Oc = PS[3], [ps_c[6], ps_c[7]]
                Ov = Opair[:, :].rearrange("p (h d) -> p h d", d=128)

                def S_stage(i):
                    u = state["spair"]; state["spair"] = u ^ 1
                    base = u * 1536
                    cells = [ps_c[3 * u], ps_c[3 * u + 1], ps_c[3 * u + 2]]
                    for j in range(5):
                        e = mc + j
                        blk, ci = e // 4, e % 4
                        slot = blk % 3
                        for hh in range(2):
                            hp = hh * 64
                            off = hh * 640 + j * 128
                            mm(PSB[:, base + off: base + off + 128], kT[slot][hp:hp + 64, i, ci * 128:(ci + 1) * 128],
                               pool32[hp:hp + 64, QT0 + i, c * 128:(c + 1) * 128], True, True,
                               [kT_c[slot], p32_c[QT0 + i]], [cells[off // 512]])
                    return PSB[:, base: base + 1280], cells

                def E_stage(i, Sp, Sc):
                    pb = state["ptb"]; state["ptb"] = pb ^ 1
                    for hh in range(2):
                        hsl = slice(hh * 640, (hh + 1) * 640)
                        act(PT[pb][:, hsl], Sp[:, hsl], AF.Exp, Sc, [PT_c[pb][hh]])
                        tt("dve", PT[pb][:, hsl], PT[pb][:, hsl], bt[:, 2 * i + hh, :, :].rearrange("p j n -> p (j n)"), ALU.mult,
                           [PT_c[pb][hh], bt_c], [PT_c[pb][hh]])
                    return pb

                def PV_stage(i, pb):
                    for hh in range(2):
                        h = 2 * i + hh
                        for j in range(5):
                            e = mc + j
                            blk, ci = e // 4, e % 4
                            slot = blk % 3
                            mm(Ov[:, h, 0:65], PT[pb][:, hh * 640 + j * 128: hh * 640 + (j + 1) * 128], Vr[slot][:, ci, h, :], j == 0, j == 4,
                               [PT_c[pb][hh], V_c[slot]], [Oc[h // 4]])

                Sq = [S_stage(0), S_stage(1)]
                for i in range(4):
                    pb = E_stage(i, *Sq.pop(0))
                    PV_stage(i, pb)
                    if i + 2 < 4:
                        Sq.append(S_stage(i + 2))
                S.add("dve", lambda q, Ov=Ov: q.reciprocal(out=rec[:], in_=Ov[:, :, 64:65]), Oc, [rec_c])
                tt("dve", brt[:, 0:512].rearrange("p (h d) -> p h d", d=64), Ov[:, :, 0:64], rec[:].to_broadcast([128, 8, 64]),
                   ALU.mult, Oc + [rec_c], [br_c[bb][0]])
                sgu_back(c, brt, bb)
                ck("m_att")
                pr, prc = next_pair()
                for k in range(8):
                    tr(pr[:, k * 128:(k + 1) * 128], brt[:, k * 128:(k + 1) * 128], ident[:], [br_c[bb][k // 4], ident_c], [prc[k // 4]])
                ck("m_trp")
                for k in range(8):
                    evac_copy(pool32[:, BRT0 + k, c * 128:(c + 1) * 128], pr[:, k * 128:(k + 1) * 128], [prc[k // 4]], [p32_c[BRT0 + k]])
                ck("m_c%d" % c)
            kvbufs = load_x4(s, (t + 2) * 512) if t + 2 <= 8 else None
            ck("m_tr")
            for half in range(2):
                au, su, auc = load_pair(w_au_v[:, :, half * 512:(half + 1) * 512], w_su_v[:, :, half * 512:(half + 1) * 512], 4, 512)
                suc = auc
                ga, gac = load_slab(w_in_v[:, :, 2560 + half * 512: 2560 + (half + 1) * 512], 8, 512)
                gb, gbc = load_slab(w_in_v[:, :, 3584 + half * 512: 3584 + (half + 1) * 512], 8, 512)
                for m in range(4):
                    oA, oAc = next_bank()
                    for k in range(4):
                        mm(oA, au[:, k, m * 128:(m + 1) * 128], pool32[:, BRT0 + k, :], k == 0, k == 3, [auc, p32_c[BRT0 + k]], [oAc])
                    oB, oBc = next_bank()
                    for k in range(4):
                        mm(oB, su[:, k, m * 128:(m + 1) * 128], pool32[:, BRT0 + 4 + k, :], k == 0, k == 3, [suc, p32_c[BRT0 + 4 + k]], [oBc])
                    oG, oGc = next_bank()
                    for k in range(8):
                        mm(oG, ga[:, k, m * 128:(m + 1) * 128], hT[:, k, :], k == 0, k == 7, [gac, hT_c[k]], [oGc])
                    oH, oHc = next_bank()
                    for k in range(8):
                        mm(oH, gb[:, k, m * 128:(m + 1) * 128], hT[:, k, :], k == 0, k == 7, [gbc, hT_c[k]], [oHc])
                    tail_step()
                    act(sga[:], oG, AF.Sigmoid, [oGc], [sga_c])
                    act(sgb[:], oH, AF.Sigmoid, [oHc], [sgb_c])
                    tt("dve", sga[:], oA, sga[:], ALU.mult, [oAc, sga_c], [sga_c])
                    tt("dve", sgb[:], oB, sgb[:], ALU.mult, [oBc, sgb_c], [sgb_c])
                    tt("dve", pool32[:, MT0 + half * 4 + m, :], sga[:], sgb[:], ALU.add, [sga_c, sgb_c], [p32_c[MT0 + half * 4 + m]])
                    tail_step()
            ck("m_merge")
            flush_all()
            if t + 2 <= 8:
                kv_front(s, t + 2, kvbufs)
            ck("m_merge")
            wo = [load_slab(w_o_v[:, :, half * 512:(half + 1) * 512], 8, 512) for half in range(2)]
            xr = [next_xs() for _ in range(4)]
            ln1_slots = []
            for c in range(4):
                tok = 256 + t * 512 + c * 128
                dma("sp", xs[xr[c]][:], xs_d[s, tok:tok + 128, :], [], [xs_c[xr[c]]])
            for c in range(4):
                xb = xr[c]
                for half in range(2):
                    o, oc = next_bank()
                    for k in range(8):
                        mm(o, pool32[:, MT0 + k, c * 128:(c + 1) * 128], wo[half][0][:, k, :], k == 0, k == 7, [p32_c[MT0 + k], wo[half][1]], [oc])
                    hs = slice(half * 512, (half + 1) * 512)
                    tt("dve", x1[:, c, hs], o, g1bc[:, hs], ALU.mult, [oc, g1bc_c], [x1_c[c]])
                tt("dve", x1[:, c, :], x1[:, c, :], xs[xb][:], ALU.add, [xs_c[xb], x1_c[c]], [x1_c[c]])
                lnb_stats(c, x1[:, c, :], [x1_c[c]], LN_EPS / (ALPHA * ALPHA))
            lnb_rsqrt()
            for c in range(4):
                lnb_norm(c, x1[:, c, :], x1[:, c, :], [x1_c[c]])
            if t + 2 <= 8:
                kv_back(s, t + 2)
            ck("m_h2")
            transpose_affine([x1[:, c, :] for c in range(4)], [x1_c[c] for c in range(4)], 4, 5)
            for c in range(4):
                tt("dve", x1[:, c, :], x1[:, c, :], ln1g_bc[:], ALU.mult, [x1_c[c], ln1g_bc_c], [x1_c[c]])
                tt("dve", x1[:, c, :], x1[:, c, :], b1bc[:], ALU.add, [x1_c[c], b1bc_c], [x1_c[c]])
            ck("m_wo")
            for sl in range(8):
                slab, sc = load_slab(w_ff1_v[:, :, sl * 512:(sl + 1) * 512], 8, 512)
                for m in range(4):
                    mi = sl * 4 + m
                    o, oc = next_bank()
                    for k in range(8):
                        mm(o, slab[:, k, m * 128:(m + 1) * 128], hT[:, k, :], k == 0, k == 7, [sc, hT_c[k]], [oc])
                    act(rr[:], o, AF.Relu, [oc, b1col_c], [rr_c], bias=b1col[:, mi:mi + 1])
                    tt("dve", pool32[:, mi, :], rr[:], rr[:], ALU.mult, [rr_c], [p32_c[mi]])
            ck("m_ff1")
            for sl in range(8):
                slab, sc = load_slab(w_ff2_v[:, 4 * sl:4 * sl + 4, :], 4, 1024)
                for c in range(4):
                    for half in range(2):
                        b = c * 2 + half
                        o = PS[b // 2][:, (b % 2) * 512:(b % 2) * 512 + 512]
                        for kk in range(4):
                            mm(o, pool32[:, 4 * sl + kk, c * 128:(c + 1) * 128], slab[:, kk, half * 512:(half + 1) * 512],
                               sl == 0 and kk == 0, sl == 7 and kk == 3, [p32_c[4 * sl + kk], sc], [ps_c[b]])
            state["bank"] = 0
            state["pair"] = 0
            ck("m_ff2")
            tmps = [(gv, gv_c), (sga, sga_c), (sgb, sgb_c), (rr, rr_c)]
            bank = lambda b: PS[b // 2][:, (b % 2) * 512:(b % 2) * 512 + 512]
            for b in range(4):
                hs = slice((b % 2) * 512, (b % 2 + 1) * 512)
                tt("dve", tmps[b][0][:], bank(b), g2bc[:, hs], ALU.mult, [ps_c[b], g2bc_c], [tmps[b][1]])
            for b in range(4, 8):
                i = b - 4
                act(branch[i // 2][:, (i % 2) * 512:(i % 2 + 1) * 512], bank(b), AF.Copy, [ps_c[b]], [br_c[i // 2][i % 2]])
            for b in range(4):
                c, hs = b // 2, slice((b % 2) * 512, (b % 2 + 1) * 512)
                tt("dve", x1[:, c, hs], x1[:, c, hs], tmps[b][0][:], ALU.add, [x1_c[c], tmps[b][1]], [x1_c[c]])
            for b in range(4, 8):
                i = b - 4
                c, hs = b // 2, slice((b % 2) * 512, (b % 2 + 1) * 512)
                src = branch[i // 2][:, (i % 2) * 512:(i % 2 + 1) * 512]
                tt("dve", src, src, g2bc[:, hs], ALU.mult, [br_c[i // 2][i % 2], g2bc_c], [br_c[i // 2][i % 2]])
                tt("dve", x1[:, c, hs], x1[:, c, hs], src, ALU.add, [x1_c[c], br_c[i // 2][i % 2]], [x1_c[c]])

            def ln2_tails(s=s, t=t):
                for c in range(4):
                    lnb_stats(c, x1[:, c, :], [x1_c[c]], LN_EPS / (ALPHA * ALPHA))
                    yield
                lnb_rsqrt()
                yield
                for c in range(4):
                    lnb_norm(c, x1[:, c, :], x1[:, c, :], [x1_c[c]])
                    yield
                    tt("dve", x1[:, c, :], x1[:, c, :], ln2g_bc[:], ALU.mult, [x1_c[c], ln2g_bc_c], [x1_c[c]])
                    tt("dve", x1[:, c, :], x1[:, c, :], ln2b_bc[:], ALU.add, [x1_c[c], ln2b_bc_c], [x1_c[c]])
                    tok = t * 512 + c * 128
                    out_ops.append(dma("sp", y_d[s, tok:tok + 128, :], x1[:, c, :], [x1_c[c]], []))
                    yield

            pending.append(ln2_tails())

        pending = []

        def tail_step():
            while pending:
                try:
                    next(pending[0])
                    return
                except StopIteration:
                    pending.pop(0)

        def flush_all():
            while pending:
                tail_step()

        try:
          ck("setup")
          for s in range(nseg):
            segment_setup(s)
            ck("segsetup")
            kv_block(s, 0)
            ck("kv0")
            kv_block(s, 1)
            for t in range(ntiles):
                main_tile(s, t)
          flush_all()
        except _Stop:
            pass
        if _os.environ.get("KTAGS"):
            import json as _json
            _json.dump({e: [o.tag for o in S.ops[e] if not o.is_dma] for e in S.ops}, open(_os.environ["KTAGS"], "w"))
        S.emit(nc, final_waits=out_ops)
    return nc


def _ext_rows(seq_rows, row0):
    er = np.full((36, 2), -1, np.int64)
    src_chunk = np.zeros(36, np.int64)
    c0 = row0 // 2
    nchunk = seq_rows // 2
    for mc in range(32):
        er[2 + mc] = (row0 + 2 * mc, row0 + 2 * mc + 1)
        src_chunk[2 + mc] = c0 + mc
    if row0 > 0:
        for e in range(2):
            g = c0 - 2 + e
            er[e] = (2 * g, 2 * g + 1)
            src_chunk[e] = g
    else:
        er[1] = (6, 7)
        src_chunk[1] = 3
        src_chunk[0] = 0
    if row0 + 64 < seq_rows:
        for e in range(2):
            g = c0 + 32 + e
            er[34 + e] = (2 * g, 2 * g + 1)
            src_chunk[34 + e] = g
    else:
        er[34] = (seq_rows - 8, seq_rows - 7)
        src_chunk[34] = nchunk - 4
        src_chunk[35] = nchunk - 1
    return er, src_chunk


def _build_btab(rpb, seq_rows, row0):
    er, _ = _ext_rows(seq_rows, row0)
    out = np.full((5, 128, 8, 5, 128), NEG, np.float32)
    q = np.arange(128)
    qro, qc = q // 64, q % 64
    cs = np.clip(qc - 8, 0, 48)
    kcol = np.arange(64)
    colvalid = (kcol[None, :] >= cs[:, None]) & (kcol[None, :] < cs[:, None] + 16)
    dc = np.clip(kcol[None, :] - qc[:, None] + 15, 0, 30)
    for ti, mc in enumerate((0, 1, 2, 30, 31)):
        r = row0 + 2 * mc + qro
        rs = np.clip(r - 4, 0, seq_rows - 8)
        covered = np.zeros((128, seq_rows), bool)
        for j in range(5):
            e = mc + j
            for half in range(2):
                krow = int(er[e, half])
                if krow < 0:
                    continue
                rowvalid = (krow >= rs) & (krow < rs + 8) & (~covered[:, krow])
                covered[:, krow] |= rowvalid
                valid = rowvalid[:, None] & colvalid
                dr = np.clip(krow - r + 7, 0, 14)
                vals = rpb[:, dr[:, None], dc]
                vals = np.transpose(vals, (2, 0, 1))
                out[ti, half * 64:(half + 1) * 64, :, j, :] = np.where(valid.T[:, None, :], vals, np.float32(NEG))
    return out.reshape(5, 128, 5120)


def _ext_tokens(xseq, seq_rows, row0):
    _, src = _ext_rows(seq_rows, row0)
    xc = xseq.reshape(seq_rows // 2, 128, D)
    return xc[src].reshape(EXT_TOK, D)


_NC_CACHE = {}


def kernel(x_prompt, x_sample, c_prompt, c_sample, w_ada, b_ada, w_in, rpb, sgu_ln_g, sgu_ln_b,
           w_s, b_s, w_attn_up, w_sgu_up, w_o, ln1_g, ln1_b, w_ff1, b_ff1, w_ff2, b_ff2, ln2_g, ln2_b):
    f = lambda a: np.ascontiguousarray(np.asarray(a, dtype=np.float32))
    x_prompt, x_sample, c_prompt, c_sample = f(x_prompt), f(x_sample), f(c_prompt), f(c_sample)
    rpb0 = f(rpb)[0]
    shared = {
        "w_ada": f(w_ada)[0], "b_ada": f(b_ada)[0].reshape(1, -1), "w_in": f(w_in)[0],
        "sgu_ln_g": f(sgu_ln_g)[0].reshape(1, -1), "sgu_ln_b": f(sgu_ln_b)[0].reshape(1, -1),
        "w_s": f(w_s)[0], "b_s": f(b_s)[0], "w_attn_up": f(w_attn_up)[0], "w_sgu_up": f(w_sgu_up)[0],
        "w_o": f(w_o)[0], "ln1_g": f(ln1_g)[0].reshape(1, -1), "ln1_b": f(ln1_b)[0].reshape(1, -1),
        "w_ff1": f(w_ff1)[0], "b_ff1": f(b_ff1)[0].reshape(32, 128), "w_ff2": f(w_ff2)[0],
        "b_ff2": f(b_ff2)[0].reshape(1, -1), "ln2_g": f(ln2_g)[0].reshape(1, -1), "ln2_b": f(ln2_b)[0].reshape(1, -1),
    }
    bt_sample = _build_btab(rpb0, 64, 0)
    bt_prompt = [_build_btab(rpb0, 256, 64 * qi) for qi in range(4)]
    in_maps = []
    for i in range(NCORES):
        pi, qi = i // 4, i % 4
        xs = np.stack([_ext_tokens(x_sample[i], 64, 0), _ext_tokens(x_prompt[pi], 256, 64 * qi)])
        cc = np.stack([c_sample[i], c_prompt[pi]]).reshape(2, 8, 128)
        m = dict(shared)
        m["xs"] = np.ascontiguousarray(xs)
        m["cc"] = np.ascontiguousarray(cc)
        m["btab"] = np.ascontiguousarray(np.stack([bt_sample, bt_prompt[qi]]))
        in_maps.append(m)
    if "nc" not in _NC_CACHE:
        _NC_CACHE["nc"] = build_program()
    res = run_bass_kernel_spmd(_NC_CACHE["nc"], in_maps, core_ids=list(range(NCORES)))
    y_prompt = np.empty((2, 16384, D), np.float32)
    y_sample = np.empty((8, 4096, D), np.float32)
    for i in range(NCORES):
        y = np.asarray(res.results[i]["y"], dtype=np.float32)
        y_sample[i] = y[0]
        y_prompt[i // 4, (i % 4) * 4096:(i % 4 + 1) * 4096] = y[1]
    return (y_prompt, y_sample)
```
